# Optimizing a Trainium2 kernel written in Bass

```python
import math
import jax, jax.numpy as jnp
from jax import lax
import numpy as np

D_MODEL = 1024
BATCH = 8
SEQ = 2048
DEPTH = 2

GRID_W = 64
CTX_LEN = 256
N_EVEN = (DEPTH + 1) // 2
N_ODD = DEPTH // 2
NORM_EPS = 1e-6
ROPE_BASE = 10000.0

RET_HEADS = 8
RET_HD = 64
RET_W = RET_HEADS * RET_HD
RET_CHUNK = 128
RWKV_HEADS = 8
RWKV_HD = 64
RWKV_W = RWKV_HEADS * RWKV_HD
DECAY_LORA = 64
A_LORA = 64
RWKV_GN_EPS = 64e-5
HY_W = 512
HY_ORDER = 2
HY_BANDS = 16
HY_EMB = 1 + 2 * HY_BANDS
HY_FFN = 64
HY_SHORT = 3
HY_TARGET = 1e-2
HY_FAST_PCT = 0.3
HY_SLOW_PCT = 1.5
NA_HEADS = 8
NA_HD = 64
NA_W = NA_HEADS * NA_HD
NA_WIN_R = 8
NA_WIN_C = 16

E_RQ = 0
E_RK = RET_W
E_RV = 2 * RET_W
E_RZ = 3 * RET_W
E_WZ = 4 * RET_W
E_SHIFT = E_WZ + RWKV_W
SHIFT_W = 3 * RWKV_W + 2 * DECAY_LORA + 2 * A_LORA
EVEN_IN = E_SHIFT + SHIFT_W
S_K = RWKV_W
S_V = 2 * RWKV_W
S_WD = 3 * RWKV_W
S_AD = S_WD + 2 * DECAY_LORA
O_HZ = 3 * HY_W
O_NQ = 4 * HY_W
O_NK = O_NQ + NA_W
O_NV = O_NK + NA_W
O_NZ = O_NV + NA_W
ODD_IN = O_NZ + NA_W

kernel_name = 'hybrid_retention_rwkv7_hyena_natten_diffusion'

F32 = jnp.float32


def rmsnorm(x, w):
    xf = x.astype(F32)
    y = xf * lax.rsqrt(jnp.mean(xf * xf, axis=-1, keepdims=True) + NORM_EPS)
    return (y * w.astype(F32)).astype(x.dtype)


def split_heads(t, n_heads):
    b, l, _ = t.shape
    return t.reshape(b, l, n_heads, -1).transpose(0, 2, 1, 3)


def merge_heads(t):
    b, h, l, d = t.shape
    return t.transpose(0, 2, 1, 3).reshape(b, l, h * d)


def axial_rope(length, dim):
    t = jnp.arange(length)
    row = (t // GRID_W).astype(F32)
    col = (t % GRID_W).astype(F32)
    nf = dim // 4
    inv = ROPE_BASE ** (-jnp.arange(nf, dtype=F32) / nf)
    ang = jnp.concatenate([row[:, None] * inv, col[:, None] * inv], axis=-1)
    return jnp.cos(ang), jnp.sin(ang)


def apply_rope(x, cos, sin):
    half = x.shape[-1] // 2
    x1, x2 = x[..., :half], x[..., half:]
    return jnp.concatenate([x1 * cos - x2 * sin, x1 * sin + x2 * cos], axis=-1)


def head_rms(y):
    return y * lax.rsqrt(jnp.mean(y * y, axis=-1, keepdims=True) + NORM_EPS)


def retention_scan(q, k, v, log_g, s0):
    b, h, l, dk = q.shape
    dv = v.shape[-1]
    n = l // RET_CHUNK
    qc = q.reshape(b, h, n, RET_CHUNK, dk)
    kc = k.reshape(b, h, n, RET_CHUNK, dk)
    vc = v.reshape(b, h, n, RET_CHUNK, dv)
    pos = jnp.arange(RET_CHUNK, dtype=F32)
    diff = pos[:, None] - pos[None, :]
    lower = diff >= 0
    dmat = jnp.where(lower, jnp.exp(jnp.where(lower, diff, 0.0)[None] * log_g[:, None, None]), 0.0)
    scores = jnp.einsum('bhncd,bhnsd->bhncs', qc, kc) * dmat[None, :, None]
    y_intra = jnp.einsum('bhncs,bhnse->bhnce', scores, vc)
    k_decay = jnp.exp((RET_CHUNK - 1 - pos)[None] * log_g[:, None])
    kv = jnp.einsum('bhnsd,bhnse->nbhde', kc * k_decay[None, :, None, :, None], vc)
    g_chunk = jnp.exp(RET_CHUNK * log_g)[None, :, None, None]

    def step(s, kv_n):
        return g_chunk * s + kv_n, s

    s_fin, s_prev = lax.scan(step, s0, kv)
    q_decay = jnp.exp((pos + 1.0)[None] * log_g[:, None])
    y_cross = jnp.einsum('bhncd,nbhde->bhnce', qc * q_decay[None, :, None, :, None], s_prev)
    return (y_intra + y_cross).reshape(b, h, l, dv), s_fin


def token_shift_mix(u, mu):
    prev = jnp.pad(u[:, :-1], ((0, 0), (1, 0), (0, 0)))
    nxt = jnp.pad(u[:, 1:], ((0, 0), (0, 1), (0, 0)))
    return u + mu * (0.5 * (prev + nxt) - u)


def rwkv_prep(s, kk_w, ka, w0, w2, a0, a2):
    s = s.astype(F32)
    b, l, _ = s.shape
    hd = lambda t: t.reshape(b, l, RWKV_HEADS, RWKV_HD)
    r = s[..., :S_K]
    k = s[..., S_K:S_V]
    v = s[..., S_V:S_WD]
    wd = s[..., S_WD:S_AD].reshape(b, l, 2, DECAY_LORA)
    ad = s[..., S_AD:].reshape(b, l, 2, A_LORA)
    kk = hd(k * kk_w)
    kk = kk / jnp.maximum(jnp.sqrt(jnp.sum(kk * kk, axis=-1, keepdims=True)), 1e-12)
    dirs = []
    for d in range(2):
        w_log = -jax.nn.softplus(-(w0[d] + jnp.tanh(wd[:, :, d]) @ w2[d])) - 0.5
        a = jax.nn.sigmoid(a0[d] + ad[:, :, d] @ a2[d])
        dirs.append((hd(jnp.exp(-jnp.exp(w_log))), hd(k * (1.0 + (a - 1.0) * ka)), hd(a)))
    return hd(r), hd(v), kk, dirs


def rwkv7_scan(r, w, k, v, kk, a, s0):
    def step(s, inp):
        r_t, w_t, k_t, v_t, kk_t, a_t = inp
        sa = jnp.einsum('bhvk,bhk->bhv', s, -kk_t)
        s = s * w_t[:, :, None, :] + sa[..., None] * (kk_t * a_t)[:, :, None, :] + v_t[..., None] * k_t[:, :, None, :]
        return s, jnp.einsum('bhvk,bhk->bhv', s, r_t)

    xs = tuple(jnp.swapaxes(t.astype(F32), 0, 1) for t in (r, w, k, v, kk, a))
    s_fin, ys = lax.scan(step, s0, xs)
    return jnp.swapaxes(ys, 0, 1), s_fin


def even_layer(h_l, h_c, w_in, w_out, ret_decay, rw_mu, rw_w0, rw_w2, rw_a0, rw_a2, rw_kk, rw_ka, rw_rk,
               rw_ln_w, rw_ln_b, cos, sin, with_ctx):
    b = h_l.shape[0]
    u_l = h_l @ w_in
    u_c = h_c @ w_in

    def ret_qkv(u, rotate):
        q = split_heads(u[..., E_RQ:E_RK], RET_HEADS).astype(F32)
        k = split_heads(u[..., E_RK:E_RV], RET_HEADS).astype(F32) * RET_HD ** -0.5
        v = split_heads(u[..., E_RV:E_RZ], RET_HEADS).astype(F32)
        if rotate:
            q, k = apply_rope(q, cos, sin), apply_rope(k, cos, sin)
        return q, k, v

    log_g = -jnp.exp(ret_decay.astype(F32))
    lat = ret_qkv(u_l, True)
    con = ret_qkv(u_c, False)
    s0 = jnp.zeros((b, RET_HEADS, RET_HD, RET_HD), F32)
    flip_h = lambda t: t[:, :, ::-1]
    ret_c_f, st_f = retention_scan(*con, log_g[0], s0)
    ret_l_f, _ = retention_scan(*lat, log_g[0], st_f)
    ret_c_b, st_b = retention_scan(*[flip_h(t) for t in con], log_g[1], s0)
    ret_l_b, _ = retention_scan(*[flip_h(t) for t in lat], log_g[1], st_b)
    ret_l = ret_l_f + flip_h(ret_l_b)
    ret_c = ret_c_f + flip_h(ret_c_b)

    prep_l = rwkv_prep(token_shift_mix(u_l[..., E_SHIFT:], rw_mu), rw_kk, rw_ka, rw_w0, rw_w2, rw_a0, rw_a2)
    prep_c = rwkv_prep(token_shift_mix(u_c[..., E_SHIFT:], rw_mu), rw_kk, rw_ka, rw_w0, rw_w2, rw_a0, rw_a2)

    def dir_inputs(p, d):
        r, v, kk, dirs = p
        decay, k_d, a = dirs[d]
        return (r, decay, k_d, v, kk, a)

    flip_t = lambda t: t[:, ::-1]
    sw = jnp.zeros((b, RWKV_HEADS, RWKV_HD, RWKV_HD), F32)
    rw_c_f, sf = rwkv7_scan(*dir_inputs(prep_c, 0), sw)
    rw_l_f, _ = rwkv7_scan(*dir_inputs(prep_l, 0), sf)
    rw_c_b, sb = rwkv7_scan(*[flip_t(t) for t in dir_inputs(prep_c, 1)], sw)
    rw_l_b, _ = rwkv7_scan(*[flip_t(t) for t in dir_inputs(prep_l, 1)], sb)
    rw_l = rw_l_f + flip_t(rw_l_b)
    rw_c = rw_c_f + flip_t(rw_c_b)

    def combine(u, ret_y, rw_y, prep):
        r, v, _, dirs = prep
        bl, l = u.shape[0], u.shape[1]
        ret_o = merge_heads(head_rms(ret_y)) * jax.nn.silu(u[..., E_RZ:E_WZ])
        mean = jnp.mean(rw_y, axis=-1, keepdims=True)
        var = jnp.mean(jnp.square(rw_y - mean), axis=-1, keepdims=True)
        gn = ((rw_y - mean) * lax.rsqrt(var + RWKV_GN_EPS)).reshape(bl, l, RWKV_W) * rw_ln_w + rw_ln_b
        bonus = sum(jnp.sum(r * dirs[d][1] * rw_rk, axis=-1, keepdims=True) * v for d in range(2))
        rw_o = (gn + bonus.reshape(bl, l, RWKV_W)) * jax.nn.silu(u[..., E_WZ:E_SHIFT])
        return jnp.concatenate([ret_o, rw_o], axis=-1) @ w_out

    y_l = combine(u_l, ret_l, rw_l, prep_l)
    y_c = combine(u_c, ret_c, rw_c, prep_c) if with_ctx else None
    return y_l, y_c


def short_conv(u, w, bias):
    l = u.shape[1]
    p = jnp.pad(u, ((0, 0), (1, 1), (0, 0)))
    return p[:, :l] * w[0] + p[:, 1:l + 1] * w[1] + p[:, 2:] * w[2] + bias


def hyena_filters(length, w1, b1, f1, w2, b2, f2, w3):
    t = jnp.linspace(0.0, 1.0, length)[:, None]
    bands = jnp.linspace(1e-4, HY_BANDS - 1, HY_BANDS)
    ang = (2.0 * math.pi / length) * jnp.arange(length, dtype=F32)[:, None] * bands[None]
    z = jnp.concatenate([t, jnp.cos(ang), -jnp.sin(ang)], axis=-1)
    hid = jnp.sin(f1 * (z @ w1 + b1))
    hid = jnp.sin(f2 * (hid @ w2 + b2))
    h = (hid @ w3).reshape(length, HY_ORDER, 2, HY_W)
    deltas = jnp.abs(jnp.linspace(math.log(HY_TARGET) / HY_SLOW_PCT, math.log(HY_TARGET) / HY_FAST_PCT, HY_W))
    return (h * jnp.exp(-t * deltas)[:, None, None, :]).astype(F32)


def two_sided_fftconv(u, h_f, h_b):
    l = u.shape[1]
    kern = jnp.concatenate([h_f, jnp.zeros_like(h_f[:1]), h_b[1:][::-1]], axis=0)
    uf = jnp.fft.rfft(u, n=2 * l, axis=1)
    kf = jnp.fft.rfft(kern, axis=0)
    return jnp.fft.irfft(uf * kf[None], n=2 * l, axis=1)[:, :l]


def neighbourhood_attention(q, k, v, k_ctx, v_ctx, rpb):
    b, h, l, d = q.shape
    rows = l // GRID_W
    wr = min(NA_WIN_R, rows)
    r = jnp.arange(rows)
    c = jnp.arange(GRID_W)
    row_idx = jnp.clip(r - NA_WIN_R // 2, 0, rows - wr)[:, None] + jnp.arange(wr)[None]
    col_idx = jnp.clip(c - NA_WIN_C // 2, 0, GRID_W - NA_WIN_C)[:, None] + jnp.arange(NA_WIN_C)[None]
    qi = c[:, None]
    qg = q.reshape(b, h, rows, GRID_W, d)
    k_rows = k.reshape(b, h, rows, GRID_W, d)[:, :, row_idx]
    v_rows = v.reshape(b, h, rows, GRID_W, d)[:, :, row_idx]
    s_rows = jnp.einsum('bhrqd,bhrwkd->bhrwqk', qg, k_rows)
    s_win = s_rows[..., qi, col_idx]
    bias = rpb[:, (row_idx - r[:, None] + NA_WIN_R - 1)[:, :, None, None],
               (col_idx - c[:, None] + NA_WIN_C - 1)[None, None]]
    s_win = (s_win + bias[None]).transpose(0, 1, 2, 4, 3, 5).reshape(b, h, rows, GRID_W, wr * NA_WIN_C)
    s_ctx = jnp.einsum('bhrqd,bhcd->bhrqc', qg, k_ctx)
    p = jax.nn.softmax(jnp.concatenate([s_win, s_ctx], axis=-1).astype(F32), axis=-1)
    p_win = p[..., :wr * NA_WIN_C].reshape(b, h, rows, GRID_W, wr, NA_WIN_C).transpose(0, 1, 2, 4, 3, 5)
    p_ctx = p[..., wr * NA_WIN_C:]
    p_rows = jnp.zeros(s_rows.shape, p.dtype).at[..., qi, col_idx].set(p_win)
    out = jnp.einsum('bhrwqk,bhrwkd->bhrqd', p_rows, v_rows) + jnp.einsum('bhrqc,bhcd->bhrqd', p_ctx, v_ctx)
    return out.reshape(b, h, l, d)


def dense_attention(q, k, v):
    p = jax.nn.softmax(jnp.einsum('bhqd,bhkd->bhqk', q, k).astype(F32), axis=-1)
    return jnp.einsum('bhqk,bhkd->bhqd', p, v)


def odd_layer(h_l, h_c, w_in, w_out, conv_w, conv_b, f_w1, f_b1, f_f1, f_w2, f_b2, f_f2, f_w3, hy_bias, rpb,
              with_ctx):
    u_l = h_l @ w_in
    u_c = h_c @ w_in if with_ctx else None
    kv_c = u_c[..., O_NK:O_NZ] if with_ctx else h_c @ w_in[:, O_NK:O_NZ]
    k_c = split_heads(kv_c[..., :NA_W], NA_HEADS)
    v_c = split_heads(kv_c[..., NA_W:], NA_HEADS)

    def hyena(u):
        p = short_conv(u[..., :O_HZ], conv_w, conv_b).astype(F32)
        v, x1, x2 = p[..., :HY_W], p[..., HY_W:2 * HY_W], p[..., 2 * HY_W:]
        filt = hyena_filters(u.shape[1], f_w1, f_b1, f_f1, f_w2, f_b2, f_f2, f_w3)
        z = v
        for o, g in enumerate((x1, x2)):
            z = g * (two_sided_fftconv(z, filt[:, o, 0], filt[:, o, 1]) + z * hy_bias[o])
        return z * jax.nn.silu(u[..., O_HZ:O_NQ])

    q_l = split_heads(u_l[..., O_NQ:O_NK], NA_HEADS) * NA_HD ** -0.5
    k_l = split_heads(u_l[..., O_NK:O_NV], NA_HEADS)
    v_l = split_heads(u_l[..., O_NV:O_NZ], NA_HEADS)
    na_l = merge_heads(neighbourhood_attention(q_l, k_l, v_l, k_c, v_c, rpb)) * jax.nn.silu(u_l[..., O_NZ:])
    y_l = jnp.concatenate([hyena(u_l), na_l], axis=-1) @ w_out
    y_c = None
    if with_ctx:
        q_c = split_heads(u_c[..., O_NQ:O_NK], NA_HEADS) * NA_HD ** -0.5
        na_c = merge_heads(dense_attention(q_c, k_c, v_c)) * jax.nn.silu(u_c[..., O_NZ:])
        y_c = jnp.concatenate([hyena(u_c), na_c], axis=-1) @ w_out
    return y_l, y_c


def setup_inputs(seed: int = 0) -> dict:
    key = jax.random.key(seed)
    ks = iter(jax.random.split(key, 40))
    nrm = lambda shape, s: jax.random.normal(next(ks), shape, F32) * s
    ret_base = jnp.asarray(np.log(-np.log(1.0 - 2.0 ** (-5.0 - np.arange(RET_HEADS)))), F32)
    return {
        'x': nrm((BATCH, SEQ, D_MODEL), 1.0),
        'c': nrm((BATCH, D_MODEL), 1.0),
        'ctx': nrm((BATCH, CTX_LEN, D_MODEL), 1.0),
        'c_ctx': nrm((D_MODEL,), 1.0),
        'ada_w': nrm((DEPTH, D_MODEL, 3 * D_MODEL), 0.5 * D_MODEL ** -0.5),
        'ada_b': nrm((DEPTH, 3 * D_MODEL), 0.02),
        'norm_w': 1.0 + nrm((DEPTH, D_MODEL), 0.02),
        'final_norm_w': 1.0 + nrm((D_MODEL,), 0.02),
        'even_w_in': nrm((N_EVEN, D_MODEL, EVEN_IN), D_MODEL ** -0.5),
        'even_w_out': nrm((N_EVEN, RET_W + RWKV_W, D_MODEL), (RET_W + RWKV_W) ** -0.5),
        'ret_decay': ret_base + nrm((N_EVEN, 2, RET_HEADS), 0.05),
        'rw_mu': jax.random.uniform(next(ks), (N_EVEN, SHIFT_W), F32),
        'rw_w0': jnp.linspace(-6.0, 1.0, RWKV_W) + nrm((N_EVEN, 2, RWKV_W), 0.1),
        'rw_w2': nrm((N_EVEN, 2, DECAY_LORA, RWKV_W), 0.1),
        'rw_a0': nrm((N_EVEN, 2, RWKV_W), 0.1),
        'rw_a2': nrm((N_EVEN, 2, A_LORA, RWKV_W), 0.1),
        'rw_kk': 0.85 + nrm((N_EVEN, RWKV_W), 0.02),
        'rw_ka': 1.0 + nrm((N_EVEN, RWKV_W), 0.02),
        'rw_rk': nrm((N_EVEN, RWKV_HEADS, RWKV_HD), 0.1),
        'rw_ln_w': 1.0 + nrm((N_EVEN, RWKV_W), 0.02),
        'rw_ln_b': nrm((N_EVEN, RWKV_W), 0.02),
        'odd_w_in': nrm((N_ODD, D_MODEL, ODD_IN), D_MODEL ** -0.5),
        'odd_w_out': nrm((N_ODD, HY_W + NA_W, D_MODEL), (HY_W + NA_W) ** -0.5),
        'hy_conv_w': nrm((N_ODD, HY_SHORT, 3 * HY_W), 0.5),
        'hy_conv_b': nrm((N_ODD, 3 * HY_W), 0.02),
        'hy_w1': nrm((N_ODD, HY_EMB, HY_FFN), HY_EMB ** -0.5),
        'hy_b1': nrm((N_ODD, HY_FFN), 0.1),
        'hy_f1': 1.0 + nrm((N_ODD, HY_FFN), 0.05),
        'hy_w2': nrm((N_ODD, HY_FFN, HY_FFN), HY_FFN ** -0.5),
        'hy_b2': nrm((N_ODD, HY_FFN), 0.1),
        'hy_f2': 1.0 + nrm((N_ODD, HY_FFN), 0.05),
        'hy_w3': nrm((N_ODD, HY_FFN, HY_ORDER * 2 * HY_W), 0.01),
        'hy_bias': nrm((N_ODD, HY_ORDER, HY_W), 0.5),
        'na_rpb': nrm((N_ODD, NA_HEADS, 2 * NA_WIN_R - 1, 2 * NA_WIN_C - 1), 0.1),
    }


def reference(x, c, ctx, c_ctx, ada_w, ada_b, norm_w, final_norm_w, even_w_in, even_w_out, ret_decay, rw_mu,
              rw_w0, rw_w2, rw_a0, rw_a2, rw_kk, rw_ka, rw_rk, rw_ln_w, rw_ln_b, odd_w_in, odd_w_out, hy_conv_w,
              hy_conv_b, hy_w1, hy_b1, hy_f1, hy_w2, hy_b2, hy_f2, hy_w3, hy_bias, na_rpb):
    cos, sin = axial_rope(x.shape[1], RET_HD)
    sc = jax.nn.silu(c)
    scc = jax.nn.silu(c_ctx)
    h = x
    hc = ctx
    for i in range(DEPTH):
        with_ctx = i < DEPTH - 1
        shift, scale, gate = jnp.split(sc @ ada_w[i] + ada_b[i], 3, axis=-1)
        shift_c, scale_c, gate_c = jnp.split(scc @ ada_w[i] + ada_b[i], 3, axis=-1)
        n_l = rmsnorm(h, norm_w[i]) * (1.0 + scale[:, None]) + shift[:, None]
        n_c = rmsnorm(hc, norm_w[i]) * (1.0 + scale_c) + shift_c
        j = i // 2
        if i % 2 == 0:
            y_l, y_c = even_layer(n_l, n_c, even_w_in[j], even_w_out[j], ret_decay[j], rw_mu[j], rw_w0[j], rw_w2[j],
                                  rw_a0[j], rw_a2[j], rw_kk[j], rw_ka[j], rw_rk[j], rw_ln_w[j], rw_ln_b[j],
                                  cos, sin, with_ctx)
        else:
            y_l, y_c = odd_layer(n_l, n_c, odd_w_in[j], odd_w_out[j], hy_conv_w[j], hy_conv_b[j], hy_w1[j],
                                 hy_b1[j], hy_f1[j], hy_w2[j], hy_b2[j], hy_f2[j], hy_w3[j], hy_bias[j], na_rpb[j],
                                 with_ctx)
        h = h + (gate[:, None] * y_l).astype(h.dtype)
        if with_ctx:
            hc = hc + (gate_c * y_c).astype(hc.dtype)
    return rmsnorm(h, final_norm_w)
```

```python
import math
import os
from contextlib import ExitStack

import numpy as np
import ml_dtypes
import concourse.bass as bass
import concourse.mybir as mybir
from concourse.bass_utils import run_bass_kernel_spmd

F32 = mybir.dt.float32
BF16 = mybir.dt.bfloat16
AF = mybir.ActivationFunctionType
ALU = mybir.AluOpType
AX = mybir.AxisListType

SAME_ENGINE_SYNC = True

D = 1024
L = 2048
CL = 256
NTL = 16
NTT = 18
LOFF = 1
COFF = 2050
NTC = 2308
EPS = 1e-6
E_RQ, E_RK, E_RV, E_RZ, E_WZ, E_SHIFT = 0, 512, 1024, 1536, 2048, 2560
NEG = -30000.0


class Buf:
    def __init__(self, ap, name=""):
        self.ap = ap
        self.name = name
        self.w = {}
        self.r = {}

    def __getitem__(self, idx):
        return self.ap[idx]


class View:
    def __init__(self, parent, ap):
        self.parent = parent
        self.ap = ap

    w = property(lambda s: s.parent.w, lambda s, v: setattr(s.parent, "w", v))
    r = property(lambda s: s.parent.r, lambda s, v: setattr(s.parent, "r", v))

    def __getitem__(self, idx):
        return self.ap[idx]


class Eng:
    def __init__(self, fw, name, handle, is_pe=False):
        self.fw = fw
        self.name = name
        self.h = handle
        self.sem = fw.new_sem("c_" + name)
        self.count = 0
        self.seen = {}
        self.is_pe = is_pe
        self.dsems = []
        self.dcnt = []
        self.dnext = 0

    def wait_tokens(self, toks):
        for key, (sem, val) in toks.items():
            if self.seen.get(key, 0) >= val:
                continue
            if sem is self.sem and (self.is_pe or not SAME_ENGINE_SYNC):
                continue
            self.h.wait_ge(sem, val)
            self.seen[key] = val
            self.fw.nwaits += 1
            self.nw = getattr(self, 'nw', 0) + 1


def _merge(d, key, sem, val):
    if key not in d or d[key][1] < val:
        d[key] = (sem, val)


class FW:
    def __init__(self, nc, es, n_dma_sems=8):
        self.nc = nc
        self.es = es
        self.nwaits = 0
        self.ninstr = 0
        self.pe = Eng(self, "pe", nc.tensor, is_pe=True)
        self.act = Eng(self, "act", nc.scalar)
        self.dve = Eng(self, "dve", nc.vector)
        self.pool = Eng(self, "pool", nc.gpsimd)
        self.sp = Eng(self, "sp", nc.sync)
        for q in (self.sp, self.pool):
            n = n_dma_sems if q is not self.sp else 2 * n_dma_sems
            for i in range(n):
                q.dsems.append(self.new_sem(f"d_{q.name}{i}"))
                q.dcnt.append(0)
        self._uid = 0

    def new_sem(self, name):
        return self.es.enter_context(self.nc.semaphore(name))

    def uid(self, p):
        self._uid += 1
        return f"{p}{self._uid}"

    def sbuf(self, shape, dtype, name=None, es=None):
        t = (es or self.es).enter_context(self.nc.sbuf_tensor("s_" + (name or self.uid("sb")), list(shape), dtype))
        return Buf(t[:] if False else t, name)

    def psum(self, shape, dtype, name=None):
        t = self.es.enter_context(self.nc.psum_tensor(name or self.uid("ps"), list(shape), dtype))
        return Buf(t, name)

    def _collect(self, reads, writes):
        toks = {}
        for b in reads:
            for k, (s, v) in b.w.items():
                _merge(toks, k, s, v)
        for b in writes:
            for k, (s, v) in b.w.items():
                _merge(toks, k, s, v)
            for k, (s, v) in b.r.items():
                _merge(toks, k, s, v)
        return toks

    def op(self, eng, fn, reads=(), writes=(), skip_self=False):
        toks = self._collect(reads, writes)
        if skip_self:
            toks = {k: v for k, v in toks.items() if v[0] is not eng.sem}
        eng.wait_tokens(toks)
        ins = fn()
        eng.count += 1
        ins.then_inc(eng.sem, 1)
        self.ninstr += 1
        key = id(eng.sem)
        for b in reads:
            _merge(b.r, key, eng.sem, eng.count)
        for b in writes:
            b.w = {key: (eng.sem, eng.count)}
            b.r = {}
        return ins

    def dma(self, q, out_ap, in_ap, reads=(), writes=(), **kw):
        toks = self._collect(reads, writes)
        i = q.dnext
        q.dnext = (q.dnext + 1) % len(q.dsems)
        sem = q.dsems[i]
        key = id(sem)
        if q.dcnt[i] > 0:
            _merge(toks, key, sem, q.dcnt[i])
        q.wait_tokens(toks)
        q.dcnt[i] += 16
        val = q.dcnt[i]
        q.h.dma_start(out=out_ap, in_=in_ap, **kw).then_inc(sem, 16)
        self.ninstr += 1
        for b in reads:
            _merge(b.r, key, sem, val)
        for b in writes:
            b.w = {key: (sem, val)}
            b.r = {}

    def barrier(self):
        toks = {}
        for e in (self.pe, self.act, self.dve, self.pool):
            if e.count:
                toks[id(e.sem)] = (e.sem, e.count)
        for q in (self.sp, self.pool):
            for s, c in zip(q.dsems, q.dcnt):
                if c:
                    toks[id(s)] = (s, c)
        for e in (self.pe, self.act, self.dve, self.pool, self.sp):
            for key, (sem, val) in toks.items():
                if sem is e.sem or e.seen.get(key, 0) >= val:
                    continue
                e.h.wait_ge(sem, val)
                e.seen[key] = val
                self.nwaits += 1
                e.nw = getattr(e, 'nw', 0) + 1

    def finish(self, bufs):
        toks = {}
        for b in bufs:
            for k, (s, v) in b.w.items():
                _merge(toks, k, s, v)
        self.sp.wait_tokens(toks)


def tokcol(tile):
    return LOFF + 128 * tile if tile < NTL else COFF + 128 * (tile - NTL)


class KB:
    def __init__(self, nc, es, dbg):
        self.nc = nc
        self.es = es
        self.fw = FW(nc, es)
        self.dbg = dbg or {}
        self.din = {}
        self.taps = {}
        self.ps = [self.fw.psum([128, 512], F32, name=f"psb{i}") for i in range(8)][: int(os.environ.get("NPS", "8"))]
        self.psn = 0
        self.held = set()

    def inp(self, name, shape, dtype=F32):
        ap = self.nc.dram_tensor(name, list(shape), dtype, kind="ExternalInput").ap()
        self.din[name] = Buf(ap, name)
        return self.din[name]

    def scratch(self, name, shape, dtype=F32):
        return Buf(self.nc.dram_tensor(name, list(shape), dtype, kind="Internal").ap(), name)

    def out(self, name, shape, dtype=F32):
        return Buf(self.nc.dram_tensor(name, list(shape), dtype, kind="ExternalOutput").ap(), name)

    def tap(self, name, buf, ap, shape, dtype=F32):
        if name not in self.dbg:
            return
        o = self.out("tap_" + name, shape, dtype)
        self.fw.dma(self.fw.sp, o.ap, ap, reads=[buf], writes=[o])
        self.taps[name] = o

    def psum(self, hold=False):
        while True:
            p = self.ps[self.psn]
            self.psn = (self.psn + 1) % len(self.ps)
            if id(p) not in self.held:
                break
        if hold:
            self.held.add(id(p))
        return p

    def release(self, p):
        self.held.discard(id(p))

    def mm(self, ob, out, lb, lhsT, rb, rhs, start=True, stop=True):
        nc = self.nc
        return self.fw.op(self.fw.pe, lambda: nc.tensor.matmul(out, lhsT=lhsT, rhs=rhs, start=start, stop=stop),
                          reads=[lb, rb], writes=[ob])

    def tr(self, ob, out, ib, in_, idb, ident):
        nc = self.nc
        return self.fw.op(self.fw.pe, lambda: nc.tensor.transpose(out, in_, ident), reads=[ib, idb], writes=[ob])

    def act(self, ob, out, ibs, in_, func, scale=None, bias=None, accum=None, extra_w=()):
        nc = self.nc
        kw = {}
        if scale is not None:
            kw["scale"] = scale
        if bias is not None:
            kw["bias"] = bias
        if accum is not None:
            kw["accum_out"] = accum
        return self.fw.op(self.fw.act, lambda: nc.scalar.activation(out=out, in_=in_, func=func, **kw),
                          reads=list(ibs), writes=[ob] + list(extra_w))

    def tt(self, ob, out, ibs, in0, in1, op, eng=None):
        eng = eng or self.fw.dve
        return self.fw.op(eng, lambda: eng.h.tensor_tensor(out=out, in0=in0, in1=in1, op=op), reads=list(ibs), writes=[ob])

    def ts(self, ob, out, ibs, in0, s1, s2=None, op0=ALU.mult, op1=None, eng=None):
        eng = eng or self.fw.dve
        if op1 is None:
            return self.fw.op(eng, lambda: eng.h.tensor_scalar(out=out, in0=in0, scalar1=s1, scalar2=None, op0=op0),
                              reads=list(ibs), writes=[ob])
        return self.fw.op(eng, lambda: eng.h.tensor_scalar(out=out, in0=in0, scalar1=s1, scalar2=s2, op0=op0, op1=op1),
                          reads=list(ibs), writes=[ob])

    def stt(self, ob, out, ibs, in0, scalar, in1, op0, op1):
        nc = self.nc
        return self.fw.op(self.fw.dve, lambda: nc.vector.scalar_tensor_tensor(out=out, in0=in0, scalar=scalar, in1=in1, op0=op0, op1=op1),
                          reads=list(ibs), writes=[ob])

    def copy(self, ob, out, ibs, in_, eng=None):
        eng = eng or self.fw.dve
        if eng is self.fw.act:
            return self.act(ob, out, ibs, in_, AF.Identity)
        return self.fw.op(eng, lambda: eng.h.tensor_copy(out=out, in_=in_), reads=list(ibs), writes=[ob])

    def memset(self, ob, out, val, eng=None):
        eng = eng or self.fw.pool
        return self.fw.op(eng, lambda: eng.h.memset(out, val), writes=[ob])

    def recip(self, ob, out, ibs, in_):
        nc = self.nc
        return self.fw.op(self.fw.dve, lambda: nc.vector.reciprocal(out=out, in_=in_), reads=list(ibs), writes=[ob])

    def load(self, ob, out, ib, in_, q=None, **kw):
        self.fw.dma(q or self.fw.sp, out, in_, reads=[ib], writes=[ob], **kw)

    def loadc(self, ob, out, ib, in_, **kw):
        self.fw.dma(self.fw.pool, out, in_, reads=[ib], writes=[ob], **kw)


def rr(ap, p=128):
    return ap.rearrange("(k p) c -> p k c", p=p)


def build(dbg=None, stop=None):
    nc = bass.Bass("TRN2", target_bir_lowering=False)
    es = ExitStack()
    kb = KB(nc, es, dbg)
    fw = kb.fw
    I = kb.inp
    x_d = I("x", [L, D]); ctx_d = I("ctx", [CL, D]); cc_d = I("ccT", [128, 8, 2])
    adaw_d = I("ada_w", [2, D, 3 * D]); adab_d = I("ada_b", [2, 3 * D]); normw_d = I("norm_w", [2, D]); fnw_d = I("fnw", [1, D])
    win0_d = I("w_in0", [D, 4352]); wrot0_d = I("w_rot0", [D, 1024]); wout0_d = I("w_out0", [D, D])
    ident_d = I("ident", [128, 128])
    csq_d = I("cs_fm", [64, L]); snq_d = I("sn_fm", [64, L]); cstm_d = I("cs_tm", [L, 64]); sntm_d = I("sn_tm", [L, 64])
    retdec_d = I("ret_decay", [16]); colAB_d = I("colAB", [128, 2]); iota12_d = I("iota12", [64, 256])
    dmask_d = I("dmask", [128, 512])
    muT_d = I("muT", [64, 28]); mu128_d = I("mu128", [128, 2]); p512_d = I("P512", [64, 9, 8]); lw_d = I("LW", [128, 2, 512])
    scanmask_d = I("scanmask", [64, NTC]); masks4_d = I("masks4", [128, 512])
    win1_d = I("w_in1", [D, 4096]); wout1_d = I("w_out1", [D, D]); rpb_d = I("na_rpb", [8, 15, 31]); wmask_d = I("wmask", [64, 64])
    Gt_d = I("Gt", [17, 128, 2 * 17 * 128], BF16); zT_d = I("hy_zT", [33, L]); trow_d = I("hy_trow", [1, L]); ndel_d = I("hy_ndelta", [1, 512])
    wtc_d = I("hy_wt", [128, 17]); hw1_d = I("hy_w1", [33, 64]); hw2_d = I("hy_w2", [64, 64]); hw3_d = I("hy_w3", [64, 2048]); hvec_d = I("hy_vec", [64, 4])
    cw_d = I("hy_conv", [4, 1536]); hbias_d = I("hy_bias", [2, 512])
    y_d = kb.out("y", [L, D])
    h1_d = kb.scratch("h1", [L + CL, D])

    ident_bf = fw.sbuf([128, 128], BF16, "ident_bf"); kb.loadc(ident_bf, ident_bf[:], ident_d, ident_d.ap)
    ident_f = fw.sbuf([128, 128], F32, "ident_f"); kb.load(ident_f, ident_f[:], ident_d, ident_d.ap)
    ones_f = fw.sbuf([128, 128], F32, "ones_f"); kb.memset(ones_f, ones_f[:], 1.0)
    ccT = fw.sbuf([128, 8, 2], F32, "ccT"); kb.load(ccT, ccT[:], cc_d, cc_d.ap)
    scT = fw.sbuf([128, 8, 2], F32, "scT"); kb.act(scT, scT[:], [ccT], ccT[:], AF.Silu)
    nT = fw.sbuf([128, 8, NTC], BF16, "nT")
    kb.memset(nT, nT[:], 0.0)
    oT = kb.scratch("oT_d", [128, 8, L + CL], BF16)

    def layer_mod(li):
        modcol = fw.sbuf([128, 4, 8], F32, f"modcol{li}")
        gateb = fw.sbuf([128, 2, D], F32, f"gateb{li}")
        with ExitStack() as les:
            rows = [fw.sbuf([1, 3 * D], F32, f"modrow{li}_{r}", es=les) for r in range(2)]
            adab = fw.sbuf([1, 3 * D], F32, f"adab{li}", es=les); kb.load(adab, adab[:], adab_d, adab_d.ap[li:li + 1, :])
            nw = fw.sbuf([1, D], F32, f"nw{li}", es=les); kb.load(nw, nw[:], normw_d, normw_d.ap[li:li + 1, :])
            wbufs = [fw.sbuf([128, 8, 512], F32, f"adaw{li}_{i}", es=les) for i in range(2)]
            for cg in range(6):
                wb = wbufs[cg % 2]
                kb.load(wb, wb[:], adaw_d, rr(adaw_d.ap[li, :, cg * 512:(cg + 1) * 512]))
                for r in range(2):
                    ps = kb.psum()
                    for kc in range(8):
                        kb.mm(ps, ps[0:1, :], scT, scT[:, kc, r:r + 1], wb, wb[:, kc, :], start=(kc == 0), stop=(kc == 7))
                    kb.tt(rows[r], rows[r][0:1, cg * 512:(cg + 1) * 512], [ps, adab], ps[0:1, :], adab[0:1, cg * 512:(cg + 1) * 512], ALU.add)
            for r in range(2):
                grow = fw.sbuf([1, D], F32, f"grow{li}_{r}", es=les)
                kb.stt(grow, grow[:], [rows[r], nw], rows[r][0:1, D:2 * D], 1.0, nw[:], ALU.add, ALU.mult)
                ps = kb.psum()
                for kc in range(8):
                    kb.mm(ps, ps[:, kc:kc + 1], grow, grow[0:1, kc * 128:(kc + 1) * 128], ones_f, ones_f[0:1, 0:1])
                    kb.mm(ps, ps[:, 8 + kc:9 + kc], rows[r], rows[r][0:1, kc * 128:(kc + 1) * 128], ones_f, ones_f[0:1, 0:1])
                kb.copy(modcol, modcol[:, 2 * r:2 * r + 2, :], [ps], ps[:, 0:16].rearrange("p (a k) -> p a k", a=2))
                for cg in range(2):
                    ps = kb.psum()
                    kb.mm(ps, ps[:], ones_f, ones_f[0:1, :], rows[r], rows[r][0:1, 2 * D + cg * 512:2 * D + (cg + 1) * 512])
                    kb.copy(gateb, gateb[:, r, cg * 512:(cg + 1) * 512], [ps], ps[:], eng=fw.act)
            fw.barrier()
        return modcol, gateb

    def layer_norm(li, src_tile, modcol):
        with ExitStack() as les:
            hb = [fw.sbuf([128, D], F32, f"nh{li}_{i}", es=les) for i in range(2)]
            xs = [fw.sbuf([128, D], BF16, f"nxs{li}_{i}", es=les) for i in range(2)]
            junk = fw.sbuf([128, D], F32, f"njunk{li}", es=les)
            st = [fw.sbuf([128, 4], F32, f"nst{li}_{i}", es=les) for i in range(2)]
            for ti in range(NTT):
                h = hb[ti % 2]; xsb = xs[ti % 2]; s = st[ti % 2]
                sb, sap = src_tile(ti)
                kb.load(h, h[:], sb, sap)
                kb.act(junk, junk[:], [h], h[:], AF.Square, accum=s[:, 0:1], extra_w=[s])
                kb.act(s, s[:, 1:2], [s], s[:, 0:1], AF.Sqrt, scale=1.0 / D, bias=EPS)
                kb.recip(s, s[:, 2:3], [s], s[:, 1:2])
                kb.act(xsb, xsb[:], [h, s], h[:], AF.Identity, scale=s[:, 2:3])
                ps = kb.psum()
                psb = ps.ap[:].bitcast(BF16)
                for kc in range(8):
                    kb.tr(ps, psb[:, kc * 128:(kc + 1) * 128], xsb, xsb[:, kc * 128:(kc + 1) * 128], ident_bf, ident_bf[:])
                a = 0 if ti < NTL else 2
                c0 = tokcol(ti)
                for kc in range(8):
                    if kc % 2 == 0:
                        kb.ts(nT, nT[:, kc, c0:c0 + 128], [ps, modcol], psb[:, kc * 128:(kc + 1) * 128],
                              modcol[:, a, kc:kc + 1], modcol[:, a + 1, kc:kc + 1], ALU.mult, ALU.add)
                    else:
                        kb.act(nT, nT[:, kc, c0:c0 + 128], [ps, modcol], psb[:, kc * 128:(kc + 1) * 128], AF.Identity,
                               scale=modcol[:, a, kc:kc + 1], bias=modcol[:, a + 1, kc:kc + 1])
            fw.barrier()

    def src0(ti):
        if ti < NTL:
            return x_d, x_d.ap[ti * 128:(ti + 1) * 128, :]
        return ctx_d, ctx_d.ap[(ti - NTL) * 128:(ti - NTL + 1) * 128, :]


    def hyena():
        PI = math.pi
        CW = 256
        with ExitStack() as les:
            sb = lambda shape, dt, name: fw.sbuf(shape, dt, "y_" + name, es=les)
            zT = sb([33, L], F32, "zT"); kb.load(zT, zT[:], zT_d, zT_d.ap)
            w1 = sb([33, 64], F32, "w1"); kb.load(w1, w1[:], hw1_d, hw1_d.ap)
            w2 = sb([64, 64], F32, "w2"); kb.load(w2, w2[:], hw2_d, hw2_d.ap)
            w3 = sb([64, 2048], F32, "w3"); kb.load(w3, w3[:], hw3_d, hw3_d.ap)
            hv = sb([64, 4], F32, "hv"); kb.load(hv, hv[:], hvec_d, hvec_d.ap)
            fb = sb([64, 2], F32, "fb")
            kb.tt(fb, fb[:, 0:1], [hv], hv[:, 0:1], hv[:, 1:2], ALU.mult); kb.tt(fb, fb[:, 1:2], [hv], hv[:, 2:3], hv[:, 3:4], ALU.mult)
            hid1 = sb([64, L], F32, "hid1"); hid2 = sb([64, L], F32, "hid2")
            ya = sb([64, 512], F32, "ya"); yb_ = sb([64, 512], F32, "yb")

            def sin_layer(dst, wmat, K_, srcb, fcol, fbcol):
                for g in range(4):
                    ps = kb.psum()
                    kb.mm(ps, ps[0:64, :], wmat, wmat[0:K_, :], srcb, srcb[0:K_, 512 * g:512 * (g + 1)])
                    kb.ts(ya, ya[:], [ps, hv, fb], ps[0:64, :], fcol, fbcol, ALU.mult, ALU.add)
                    for _ in range(2):
                        kb.ts(yb_, yb_[:], [ya], ya[:], -PI, 2 * PI, ALU.is_lt, ALU.mult)
                        kb.tt(yb_, yb_[:], [ya, yb_], ya[:], yb_[:], ALU.add)
                        kb.ts(ya, ya[:], [yb_], yb_[:], PI, -2 * PI, ALU.is_gt, ALU.mult)
                        kb.tt(ya, ya[:], [ya, yb_], ya[:], yb_[:], ALU.add)
                    kb.act(dst, dst[:, 512 * g:512 * (g + 1)], [ya], ya[:], AF.Sin)

            sin_layer(hid1, w1, 33, zT, hv[:, 1:2], fb[:, 0:1])
            sin_layer(hid2, w2, 64, hid1, hv[:, 3:4], fb[:, 1:2])
            kb.tap("y_hid2", hid2, hid2[:], [64, L])
            trow = sb([1, L], F32, "trow"); kb.load(trow, trow[:], trow_d, trow_d.ap)
            ndel = sb([1, 512], F32, "ndel"); kb.load(ndel, ndel[:], ndel_d, ndel_d.ap)
            wtc = sb([128, 17], F32, "wtc"); kb.load(wtc, wtc[:], wtc_d, wtc_d.ap)
            wtn = sb([128, 17], F32, "wtn"); kb.ts(wtn, wtn[:], [wtc], wtc[:], -1.0)
            hsum = sb([128, 16, CW], BF16, "hsum"); hdif = sb([128, 16, CW], BF16, "hdif")
            zz = [sb([128, 16, CW], BF16, f"zz{i}") for i in range(2)]
            Zre = sb([128, 17, CW], BF16, "Zre"); Zim = sb([128, 17, CW], BF16, "Zim")
            Gcs = [sb([128, 2, 17, 128], BF16, f"Gcs{i}") for i in range(2)]
            cwb = sb([128, 4, CW], F32, "cwb"); hbb = sb([128, 2, CW], F32, "hbb")
            wcv = [sb([128, 8, CW], BF16, f"wcv{i}") for i in range(4)]
            dec = sb([128, CW], F32, "dec"); hf = sb([128, CW], F32, "hf"); hb_ = sb([128, CW], F32, "hb")
            t_a = sb([128, CW], F32, "t_a"); t_b = sb([128, CW], F32, "t_b"); t_c = sb([128, CW], F32, "t_c")
            Kc_s = sb([128, CW], F32, "Kc_s"); Ks_s = sb([128, CW], F32, "Ks_s")
            ob = [sb([128, CW], BF16, f"ob{i}") for i in range(2)]
            ostg = [sb([128, 2, 128], BF16, f"ostg{i}") for i in range(2)]
            gn = 0

            def load_G(col_tile):
                nonlocal gn
                g_ = Gcs[gn % 2]
                gn += 1
                kb.load(g_, g_[:].rearrange("p t r c -> p (t r c)"), Gt_d, Gt_d.ap[col_tile])
                return View(g_, g_.ap[:, 0]), View(g_, g_.ap[:, 1])

            def conv_proj(wt, nt, widx, dst):
                c0 = tokcol(nt)
                for s_ in range(3):
                    ps = kb.psum()
                    for kc in range(8):
                        kb.mm(ps, ps[:, 0:CW], nT, nT[:, kc, c0 - 1 + s_:c0 - 1 + s_ + 128], wt, wt[:, kc, :], start=(kc == 0), stop=(kc == 7))
                    if s_ == 0:
                        kb.tt(dst, dst[:], [ps, cwb], ps[:, 0:CW], cwb[:, 0, :], ALU.mult)
                    else:
                        kb.tt(t_c, t_c[:], [ps, cwb], ps[:, 0:CW], cwb[:, s_, :], ALU.mult)
                        kb.tt(dst, dst[:], [dst, t_c], dst[:], t_c[:], ALU.add, eng=fw.pool)
                kb.tt(dst, dst[:], [dst, cwb], dst[:], cwb[:, 3, :], ALU.add, eng=fw.pool)

            for cg in range(512 // CW):
                ch0 = cg * CW
                for i_, base in enumerate((0, 512, 1024, 1536)):
                    kb.loadc(wcv[i_], wcv[i_][:], win1_d, rr(win1_d.ap[:, base + ch0:base + ch0 + CW]))
                for k_ in range(4):
                    pass
                kb.load(cwb, cwb[:], cw_d, bass.AP(cw_d.ap.tensor, ch0, [[0, 128], [1536, 4], [1, CW]]))
                for nt in range(NTL):
                    conv_proj(wcv[0], nt, 0, t_a)
                    kb.copy(zz[0], zz[0][:, nt, :], [t_a], t_a[:], eng=fw.act)
                for o in range(2):
                    zin, zout = zz[o % 2], zz[(o + 1) % 2]
                    kb.load(cwb, cwb[:], cw_d, bass.AP(cw_d.ap.tensor, 512 * (o + 1) + ch0, [[0, 128], [1536, 4], [1, CW]]))
                    kb.load(hbb, hbb[:], hbias_d, bass.AP(hbias_d.ap.tensor, ch0, [[0, 128], [512, 2], [1, CW]]))
                    for nt in range(NTL):
                        psd = kb.psum()
                        kb.mm(psd, psd[:, 0:CW], trow, trow[0:1, 128 * nt:128 * (nt + 1)], ndel, ndel[0:1, ch0:ch0 + CW])
                        kb.act(dec, dec[:], [psd], psd[:, 0:CW], AF.Exp)
                        psf = kb.psum(); psb_ = kb.psum()
                        kb.mm(psf, psf[:, 0:CW], hid2, hid2[:, 128 * nt:128 * (nt + 1)], w3, w3[:, (2 * o) * 512 + ch0:(2 * o) * 512 + ch0 + CW])
                        kb.mm(psb_, psb_[:, 0:CW], hid2, hid2[:, 128 * nt:128 * (nt + 1)], w3, w3[:, (2 * o + 1) * 512 + ch0:(2 * o + 1) * 512 + ch0 + CW])
                        kb.copy(hf, hf[:], [psf], psf[:, 0:CW], eng=fw.act)
                        kb.copy(hb_, hb_[:], [psb_], psb_[:, 0:CW])
                        if nt == 0:
                            kb.memset(hb_, hb_[0:1, :], 0.0)
                        kb.tt(t_a, t_a[:], [hf, hb_], hf[:], hb_[:], ALU.add, eng=fw.pool)
                        kb.tt(hsum, hsum[:, nt, :], [t_a, dec], t_a[:], dec[:], ALU.mult)
                        kb.tt(t_b, t_b[:], [hf, hb_], hb_[:], hf[:], ALU.subtract, eng=fw.pool)
                        kb.tt(hdif, hdif[:, nt, :], [t_b, dec], t_b[:], dec[:], ALU.mult)
                    for ft in range(17):
                        gc, gs = load_G(ft)
                        pUc = kb.psum(hold=True); pUs = kb.psum(hold=True); pKc = kb.psum(hold=True); pKs = kb.psum(hold=True)
                        for kc in range(16):
                            st, sp_ = (kc == 0), (kc == 15)
                            kb.mm(pUc, pUc[:, 0:CW], gc, gc[:, kc, :], zin, zin[:, kc, :], start=st, stop=sp_)
                            kb.mm(pKc, pKc[:, 0:CW], gc, gc[:, kc, :], hsum, hsum[:, kc, :], start=st, stop=sp_)
                            kb.mm(pUs, pUs[:, 0:CW], gs, gs[:, kc, :], zin, zin[:, kc, :], start=st, stop=sp_)
                            kb.mm(pKs, pKs[:, 0:CW], gs, gs[:, kc, :], hdif, hdif[:, kc, :], start=st, stop=sp_)
                        for p_ in (pUc, pUs, pKc, pKs):
                            kb.release(p_)
                        kb.copy(Kc_s, Kc_s[:], [pKc], pKc[:, 0:CW], eng=fw.act)
                        kb.copy(Ks_s, Ks_s[:], [pKs], pKs[:, 0:CW], eng=fw.act)
                        kb.tt(t_a, t_a[:], [pUc, Kc_s], pUc[:, 0:CW], Kc_s[:], ALU.mult)
                        kb.tt(t_b, t_b[:], [pUs, Ks_s], pUs[:, 0:CW], Ks_s[:], ALU.mult)
                        kb.tt(t_a, t_a[:], [t_a, t_b], t_a[:], t_b[:], ALU.add, eng=fw.pool)
                        kb.ts(Zre, Zre[:, ft, :], [t_a, wtc], t_a[:], wtc[:, ft:ft + 1], eng=fw.pool)
                        kb.tt(t_b, t_b[:], [pUs, Kc_s], pUs[:, 0:CW], Kc_s[:], ALU.mult)
                        kb.tt(t_c, t_c[:], [pUc, Ks_s], pUc[:, 0:CW], Ks_s[:], ALU.mult)
                        kb.tt(t_b, t_b[:], [t_b, t_c], t_b[:], t_c[:], ALU.subtract, eng=fw.pool)
                        kb.ts(Zim, Zim[:, ft, :], [t_b, wtc], t_b[:], wtc[:, ft:ft + 1], eng=fw.pool)
                    for nt in range(NTL):
                        gc, gs = load_G(nt)
                        py = kb.psum(hold=True)
                        for fc in range(17):
                            kb.mm(py, py[:, 0:CW], gc, gc[:, fc, :], Zre, Zre[:, fc, :], start=(fc == 0), stop=False)
                            kb.mm(py, py[:, 0:CW], gs, gs[:, fc, :], Zim, Zim[:, fc, :], start=False, stop=(fc == 16))
                        kb.release(py)
                        kb.tt(t_a, t_a[:], [zin, hbb], zin[:, nt, :], hbb[:, o, :], ALU.mult)
                        kb.tt(t_a, t_a[:], [t_a, py], t_a[:], py[:, 0:CW], ALU.add)
                        conv_proj(wcv[1 + o], nt, 1 + o, t_b)
                        if o == 0:
                            kb.tt(zout, zout[:, nt, :], [t_a, t_b], t_a[:], t_b[:], ALU.mult)
                        else:
                            kb.tt(t_a, t_a[:], [t_a, t_b], t_a[:], t_b[:], ALU.mult)
                            c0 = tokcol(nt)
                            ps = kb.psum()
                            for kc in range(8):
                                kb.mm(ps, ps[:, 0:CW], nT, nT[:, kc, c0:c0 + 128], wcv[3], wcv[3][:, kc, :], start=(kc == 0), stop=(kc == 7))
                            kb.act(t_b, t_b[:], [ps], ps[:, 0:CW], AF.Silu)
                            o_ = ob[nt % 2]
                            kb.tt(o_, o_[:], [t_a, t_b], t_a[:], t_b[:], ALU.mult)
                            pst = kb.psum(); pstb = pst.ap[:].bitcast(BF16)
                            for j in range(CW // 128):
                                kb.tr(pst, pstb[:, 128 * j:128 * (j + 1)], o_, o_[:, 128 * j:128 * (j + 1)], ident_bf, ident_bf[:])
                            stg = ostg[nt % 2]
                            kb.copy(stg, stg[:].rearrange("p a b -> p (a b)"), [pst], pstb[:, 0:CW], eng=fw.act)
                            kb.load(oT, oT.ap[:, cg * (CW // 128):(cg + 1) * (CW // 128), 128 * nt:128 * (nt + 1)], stg, stg[:])
            fw.barrier()

    SKIP_L0 = os.environ.get('SKIP_L0') == '1'
    if SKIP_L0:
        h1in_d = I('h1in', [L + CL, D])
    if not SKIP_L0:
        modcol0, gateb0 = layer_mod(0)
    if not SKIP_L0:
        kb.tap("modcol0", modcol0, modcol0[:], [128, 4, 8])
        kb.tap("gateb0", gateb0, gateb0[:], [128, 2, D])
    if not SKIP_L0:
        layer_norm(0, src0, modcol0)
    kb.tap("nT0", nT, nT[:], [128, 8, NTC], BF16)
    if stop == "norm0":
        return finish(kb, es, y_d)

    def proj_fm(ps, M, w, wcols, c0, n):
        for kc in range(8):
            kb.mm(ps, ps[0:M, 0:n], w, w[:, kc, wcols], nT, nT[:, kc, c0:c0 + n], start=(kc == 0), stop=(kc == 7))

    def proj_tm(ps, c0, w, wcols, N):
        for kc in range(8):
            kb.mm(ps, ps[:, 0:N], nT, nT[:, kc, c0:c0 + 128], w, w[:, kc, wcols], start=(kc == 0), stop=(kc == 7))

    def qcol(tile):
        return tile * 128

    def retention():
        with ExitStack() as les:
            sb = lambda shape, dt, name: fw.sbuf(shape, dt, "r_" + name, es=les)
            cs_fm = sb([64, L], F32, "cs_fm"); kb.load(cs_fm, cs_fm[:], csq_d, csq_d.ap)
            sn_fm = sb([64, L], F32, "sn_fm"); kb.load(sn_fm, sn_fm[:], snq_d, snq_d.ap)
            cs_tm = sb([128, 16, 64], F32, "cs_tm"); kb.load(cs_tm, cs_tm[:], cstm_d, rr(cstm_d.ap))
            sn_tm = sb([128, 16, 64], F32, "sn_tm"); kb.load(sn_tm, sn_tm[:], sntm_d, rr(sntm_d.ap))
            colAB = sb([128, 2], F32, "colAB"); kb.load(colAB, colAB[:], colAB_d, colAB_d.ap)
            iota12 = sb([64, 256], F32, "iota12"); kb.load(iota12, iota12[:], iota12_d, iota12_d.ap)
            dmask = sb([128, 512], F32, "dmask"); kb.load(dmask, dmask[:], dmask_d, dmask_d.ap)
            lg0 = sb([128, 16], F32, "lg0"); kb.load(lg0, lg0[:], retdec_d, retdec_d.ap.partition_broadcast(128))
            LGc = sb([128, 16], F32, "LGc")
            kb.act(lg0, lg0[:], [lg0], lg0[:], AF.Exp)
            kb.ts(LGc, LGc[:], [lg0], lg0[:], -1.0)
            wnames = ["wq", "wqr", "wk", "wkr", "wv", "wz"]
            W = [{n: sb([128, 8, 64], BF16, f"{n}{i}") for n in wnames} for i in range(2)]
            qT = sb([64, L + CL], BF16, "qT"); kT = sb([64, L + CL], BF16, "kT")
            qdf = sb([64, L + CL], BF16, "qdf"); qdb = sb([64, L + CL], BF16, "qdb")
            Kdf = sb([128, NTT, 64], BF16, "Kdf"); Kdb = sb([128, NTT, 64], BF16, "Kdb"); Vtm = sb([128, NTT, 64], BF16, "Vtm")
            Saf = sb([64, NTT, 64], BF16, "Saf"); Sab = sb([64, NTT, 64], BF16, "Sab")
            Sf = sb([64, 64], F32, "Sf"); Sb_ = sb([64, 64], F32, "Sb")
            dec = sb([128, 4], F32, "dec"); qdec = sb([64, 256], F32, "qdec"); WtT = sb([128, 128], F32, "WtT"); wtmp = sb([128, 256], F32, "wtmp")
            t1 = [sb([64, 512], F32, f"t1_{i}") for i in range(2)]; t2 = [sb([64, 512], F32, f"t2_{i}") for i in range(2)]
            ktmp = [sb([128, 64], F32, f"ktmp{i}") for i in range(2)]; ktmp2 = [sb([128, 64], F32, f"ktmpb{i}") for i in range(2)]
            PT = [sb([128, 128], BF16, f"PT{i}") for i in range(2)]
            ostg = [sb([64, 512], BF16, f"ostg{i}") for i in range(2)]
            sq = sb([64, 512], F32, "sq"); sd = sb([64, 512], F32, "sd"); sz = sb([64, 512], F32, "sz"); yv = sb([64, 512], F32, "yv")
            for h in range(8):
                w = W[h % 2]
                for n, (srcd, c0) in zip(wnames, [(win0_d, E_RQ + 64 * h), (wrot0_d, 64 * h), (win0_d, E_RK + 64 * h),
                                                  (wrot0_d, 512 + 64 * h), (win0_d, E_RV + 64 * h), (win0_d, E_RZ + 64 * h)]):
                    kb.loadc(w[n], w[n][:], srcd, rr(srcd.ap[:, c0:c0 + 64]))
                lf = LGc[:, h:h + 1]; lb = LGc[:, 8 + h:9 + h]
                kb.act(dec, dec[:, 0:1], [colAB, LGc], colAB[:, 0:1], AF.Exp, scale=lf)
                kb.act(dec, dec[:, 1:2], [colAB, LGc], colAB[:, 1:2], AF.Exp, scale=lb)
                kb.act(dec, dec[0:64, 2:3], [LGc], LGc[0:64, h:h + 1], AF.Exp, scale=128.0)
                kb.act(dec, dec[0:64, 3:4], [LGc], LGc[0:64, 8 + h:9 + h], AF.Exp, scale=128.0)
                kb.act(qdec, qdec[:, 0:128], [iota12, LGc], iota12[:, 0:128], AF.Exp, scale=LGc[0:64, h:h + 1])
                kb.act(qdec, qdec[:, 128:256], [iota12, LGc], iota12[:, 128:256], AF.Exp, scale=LGc[0:64, 8 + h:9 + h])
                kb.act(wtmp, wtmp[:, 0:128], [dmask, LGc], dmask[:, 0:128], AF.Exp, scale=lf)
                kb.act(wtmp, wtmp[:, 128:256], [dmask, LGc], dmask[:, 128:256], AF.Exp, scale=lb)
                kb.tt(wtmp, wtmp[:], [wtmp, dmask], wtmp[:], dmask[:, 256:512], ALU.mult)
                kb.tt(WtT, WtT[:], [wtmp], wtmp[:, 0:128], wtmp[:, 128:256], ALU.add)
                for which, dst, wa, wb_, sc in (("q", qT, "wq", "wqr", 1.0), ("k", kT, "wk", "wkr", 0.125)):
                    for g in range(4):
                        psA = kb.psum(); proj_fm(psA, 64, w[wa], slice(0, 64), LOFF + 512 * g, 512)
                        psB = kb.psum(); proj_fm(psB, 64, w[wb_], slice(0, 64), LOFF + 512 * g, 512)
                        a = t1[g % 2]; b_ = t2[g % 2]
                        kb.stt(a, a[:], [psA, cs_fm], psA[0:64, :], sc, cs_fm[:, 512 * g:512 * (g + 1)], ALU.mult, ALU.mult)
                        kb.stt(b_, b_[:], [psB, sn_fm], psB[0:64, :], sc, sn_fm[:, 512 * g:512 * (g + 1)], ALU.mult, ALU.mult)
                        kb.tt(dst, dst[:, 512 * g:512 * (g + 1)], [a, b_], a[:], b_[:], ALU.add, eng=fw.pool)
                    psA = kb.psum(); proj_fm(psA, 64, w[wa], slice(0, 64), COFF, 256)
                    kb.act(dst, dst[:, L:L + CL], [psA], psA[0:64, 0:256], AF.Identity, scale=sc)
                kb.tt(qdf, qdf[:].rearrange("p (t c) -> p t c", c=128), [qT, qdec], qT[:].rearrange("p (t c) -> p t c", c=128),
                      qdec[:, 0:128].unsqueeze(1).broadcast_to([64, NTT, 128]), ALU.mult)
                kb.tt(qdb, qdb[:].rearrange("p (t c) -> p t c", c=128), [qT, qdec], qT[:].rearrange("p (t c) -> p t c", c=128),
                      qdec[:, 128:256].unsqueeze(1).broadcast_to([64, NTT, 128]), ALU.mult, eng=fw.pool)
                for ti in range(NTT):
                    c0 = tokcol(ti)
                    psA = kb.psum(); proj_tm(psA, c0, w["wk"], slice(0, 64), 64)
                    ka = ktmp[ti % 2]
                    if ti < NTL:
                        psB = kb.psum(); proj_tm(psB, c0, w["wkr"], slice(0, 64), 64)
                        kb_ = ktmp2[ti % 2]
                        kb.stt(ka, ka[:], [psA, cs_tm], psA[:, 0:64], 0.125, cs_tm[:, ti, :], ALU.mult, ALU.mult)
                        kb.stt(kb_, kb_[:], [psB, sn_tm], psB[:, 0:64], 0.125, sn_tm[:, ti, :], ALU.mult, ALU.mult)
                        kb.tt(ka, ka[:], [ka, kb_], ka[:], kb_[:], ALU.add, eng=fw.pool)
                    else:
                        kb.act(ka, ka[:], [psA], psA[:, 0:64], AF.Identity, scale=0.125)
                    kb.ts(Kdf, Kdf[:, ti, :], [ka, dec], ka[:], dec[:, 0:1], eng=fw.pool)
                    kb.ts(Kdb, Kdb[:, ti, :], [ka, dec], ka[:], dec[:, 1:2], eng=fw.pool)
                    psV = kb.psum(); proj_tm(psV, c0, w["wv"], slice(0, 64), 64)
                    kb.copy(Vtm, Vtm[:, ti, :], [psV], psV[:, 0:64], eng=fw.act)
                for S, Sall, Kd, gcol, order in ((Sf, Saf, Kdf, 2, [16, 17] + list(range(16))), (Sb_, Sab, Kdb, 3, [17, 16] + list(range(15, -1, -1)))):
                    kb.memset(S, S[:], 0.0)
                    for i, ti in enumerate(order):
                        kb.copy(Sall, Sall[:, ti, :], [S], S[:], eng=fw.pool)
                        if i == len(order) - 1:
                            break
                        ps = kb.psum()
                        kb.mm(ps, ps[0:64, 0:64], Kd, Kd[:, ti, :], Vtm, Vtm[:, ti, :])
                        kb.stt(S, S[:], [S, dec, ps], S[:], dec[0:64, gcol:gcol + 1], ps[0:64, 0:64], ALU.mult, ALU.add)
                groups = [list(range(4 * g, 4 * g + 4)) for g in range(4)] + [[16, 17]]
                for gi, tiles in enumerate(groups):
                    psY = kb.psum()
                    for j, ti in enumerate(tiles):
                        tc = qcol(ti)
                        psS = kb.psum()
                        kb.mm(psS, psS[:, 0:128], kT, kT[:, tc:tc + 128], qT, qT[:, tc:tc + 128])
                        pt = PT[j % 2]
                        kb.tt(pt, pt[:], [psS, WtT], psS[:, 0:128], WtT[:], ALU.mult)
                        yo = psY[0:64, 128 * j:128 * (j + 1)]
                        kb.mm(psY, yo, Vtm, Vtm[:, ti, :], pt, pt[:], start=True, stop=False)
                        kb.mm(psY, yo, Saf, Saf[:, ti, :], qdf, qdf[:, tc:tc + 128], start=False, stop=False)
                        kb.mm(psY, yo, Sab, Sab[:, ti, :], qdb, qdb[:, tc:tc + 128], start=False, stop=True)
                    n = 128 * len(tiles)
                    tc0 = qcol(tiles[0])
                    nc0 = tokcol(tiles[0])
                    kb.act(sq, sq[:, 0:n], [psY], psY[0:64, 0:n], AF.Square)
                    psN = kb.psum()
                    kb.mm(psN, psN[0:64, 0:n], ones_f, ones_f[0:64, 0:64], sq, sq[:, 0:n])
                    kb.act(sd, sd[:, 0:n], [psN], psN[0:64, 0:n], AF.Sqrt, scale=1.0 / 64, bias=EPS)
                    kb.recip(sd, sd[:, 0:n], [sd], sd[:, 0:n])
                    psZ = kb.psum(); proj_fm(psZ, 64, w["wz"], slice(0, 64), nc0, n)
                    kb.act(sz, sz[:, 0:n], [psZ], psZ[0:64, 0:n], AF.Silu)
                    kb.tt(yv, yv[:, 0:n], [psY, sd], psY[0:64, 0:n], sd[:, 0:n], ALU.mult)
                    po = (h % 2) * 64
                    stg = ostg[gi % 2]
                    kb.tt(stg, stg[:, 0:n], [yv, sz], yv[:, 0:n], sz[:, 0:n], ALU.mult, eng=fw.pool)
                    kb.load(oT, oT.ap[po:po + 64, h // 2, tc0:tc0 + n], stg, stg[:, 0:n])
                if h == 0:
                    kb.tap("r_wq", w["wq"], w["wq"][:], [128, 8, 64], BF16); kb.tap("r_nT", nT, nT[:], [128, 8, NTC], BF16)
                    kb.tap("r_LGc", LGc, LGc[:], [128, 16]); kb.tap("r_dec", dec, dec[:], [128, 4]); kb.tap("r_qdec", qdec, qdec[:], [64, 256])
                    kb.tap("r_WtT", WtT, WtT[:], [128, 128]); kb.tap("r_qT", qT, qT[:], [64, L + CL], BF16); kb.tap("r_kT", kT, kT[:], [64, L + CL], BF16)
                    kb.tap("r_Kdf", Kdf, Kdf[:], [128, NTT, 64], BF16); kb.tap("r_Vtm", Vtm, Vtm[:], [128, NTT, 64], BF16)
                    kb.tap("r_Saf", Saf, Saf[:], [64, NTT, 64], BF16); kb.tap("r_Sab", Sab, Sab[:], [64, NTT, 64], BF16)
                    kb.tap("r_sd", sd, sd[:], [64, 512]); kb.tap("r_sz", sz, sz[:], [64, 512]); kb.tap("r_yv", yv, yv[:], [64, 512])
            fw.barrier()


    def rwkv():
        with ExitStack() as les:
            sb = lambda shape, dt, name: fw.sbuf(shape, dt, "w_" + name, es=les)
            NC_ = NTC
            muT = sb([64, 28], F32, "muT"); kb.load(muT, muT[:], muT_d, muT_d.ap)
            mu128 = sb([128, 2], F32, "mu128"); kb.load(mu128, mu128[:], mu128_d, mu128_d.ap)
            om = sb([64, 28], F32, "om"); hm = sb([64, 28], F32, "hm")
            kb.ts(om, om[:], [muT], muT[:], -1.0, 1.0, ALU.mult, ALU.add); kb.ts(hm, hm[:], [muT], muT[:], 0.5)
            om128 = sb([128, 2], F32, "om128"); hm128 = sb([128, 2], F32, "hm128")
            kb.ts(om128, om128[:], [mu128], mu128[:], -1.0, 1.0, ALU.mult, ALU.add); kb.ts(hm128, hm128[:], [mu128], mu128[:], 0.5)
            P5 = sb([64, 9, 8], F32, "P5"); kb.load(P5, P5[:], p512_d, p512_d.ap)
            omka = sb([64, 8], F32, "omka"); kb.ts(omka, omka[:], [P5], P5[:, 1, :], -1.0, 1.0, ALU.mult, ALU.add)
            LW = sb([128, 2, 512], BF16, "LW"); kb.loadc(LW, LW[:], lw_d, lw_d.ap)
            smask = sb([64, NC_], BF16, "smask"); kb.loadc(smask, smask[:], scanmask_d, scanmask_d.ap)
            m4 = sb([128, 512], F32, "m4"); kb.load(m4, m4[:], masks4_d, masks4_d.ap)
            m4n = sb([128, 512], F32, "m4n"); kb.ts(m4n, m4n[:], [m4], m4[:], -1.0)
            ones64 = sb([64, 64], F32, "ones64"); kb.memset(ones64, ones64[:], 1.0 / 64)
            Upad = sb([128, NC_], F32, "Upad"); kb.memset(Upad, Upad[:], 0.0)
            tmpw = sb([128, NC_], F32, "tmpw"); kb.memset(tmpw, tmpw[:], 0.0)
            wdT = sb([128, NC_], BF16, "wdT"); adT = sb([128, NC_], BF16, "adT")
            kb.memset(wdT, wdT[:], 0.0); kb.memset(adT, adT[:], 0.0)
            wsh = [sb([128, 8, 128], BF16, f"wsh{i}") for i in range(1)] * 2
            groups = [(LOFF + 512 * g, 512) for g in range(4)] + [(COFF, 256)]

            def shift_proj(w, wcols, M, omc, hmc, out, func=None):
                for a_, b_ in ((0, 1), (2049, 2050), (2306, 2308)):
                    kb.memset(Upad, Upad[0:M, a_:b_], 0.0)
                for c0, n in groups:
                    ps = kb.psum(); proj_fm(ps, M, w, wcols, c0, n)
                    kb.copy(Upad, Upad[0:M, c0:c0 + n], [ps], ps[0:M, 0:n], eng=fw.act)
                kb.tt(tmpw, tmpw[0:M, 0:NC_ - 2], [Upad], Upad[0:M, 0:NC_ - 2], Upad[0:M, 2:NC_], ALU.add, eng=fw.pool)
                kb.ts(Upad, Upad[0:M, 1:NC_ - 1], [Upad, om, om128], Upad[0:M, 1:NC_ - 1], omc)
                if func is None:
                    kb.stt(out, out[0:M, 1:NC_ - 1], [tmpw, Upad, hm, hm128], tmpw[0:M, 0:NC_ - 2], hmc, Upad[0:M, 1:NC_ - 1], ALU.mult, ALU.add)
                else:
                    kb.stt(tmpw, tmpw[0:M, 0:NC_ - 2], [tmpw, Upad, hm, hm128], tmpw[0:M, 0:NC_ - 2], hmc, Upad[0:M, 1:NC_ - 1], ALU.mult, ALU.add)
                    kb.act(out, out[0:M, 1:NC_ - 1], [tmpw], tmpw[0:M, 0:NC_ - 2], func)

            for j, (dst, func) in enumerate(((wdT, AF.Tanh), (adT, None))):
                w = wsh[j]
                c0 = E_SHIFT + 1536 + 128 * j
                kb.loadc(w, w[:], win0_d, rr(win0_d.ap[:, c0:c0 + 128]))
                shift_proj(w, slice(0, 128), 128, om128[:, j:j + 1], hm128[:, j:j + 1], dst, func)
            kb.tap("w_wdT", wdT, wdT[:], [128, NC_], BF16); kb.tap("w_adT", adT, adT[:], [128, NC_], BF16)
            if stop == "rwkv_a":
                fw.barrier(); return

            wh = [{n: sb([128, 8, 64], BF16, f"{n}{i}") for n in ("r", "k", "v", "z")} for i in range(1)] * 2
            bfa = lambda name: sb([64, NC_], BF16, name)
            f32a = lambda name: sb([64, NC_], F32, name)
            rS, kS, vS, kkn, aD, kd, bD, rks = [bfa(n) for n in ("rS", "kS", "vS", "kkn", "aD", "kd", "bD", "rks")]
            Bi, Ki, KKd, Rd = [bfa(n) for n in ("Bi", "Ki", "KKd", "Rd")]
            BeT, KeT = Bi, Ki
            lw_, YT = [f32a(n) for n in ("lw", "YT")]
            ex = View(Upad, Upad.ap[0:64, :])
            cum = View(tmpw, tmpw.ap[0:64, :])
            for t_ in (rS, kS, vS, kkn, aD, kd, bD, rks, Bi, Ki, KKd, Rd, lw_, YT):
                kb.memset(t_, t_[:], 0.0)
            tot = sb([64, NTT], F32, "tot"); WC = sb([64, NTT], F32, "WC")
            Ak = sb([128, NTT, 128], BF16, "Ak"); Gb = sb([128, NTT, 128], BF16, "Gb"); Gk = sb([128, NTT, 128], BF16, "Gk")
            Tall = sb([128, NTT, 128], BF16, "Tall")
            Vt = sb([128, NTT, 64], BF16, "Vt"); Be = sb([128, NTT, 64], BF16, "Be"); Ke = sb([128, NTT, 64], BF16, "Ke")
            NDT = F32
            NPh = [[sb([128, 256], NDT, f"NP{h_}{i}") for i in range(2)] for h_ in range(2)]
            NPTh = [[sb([128, 256], NDT, f"NPT{h_}{i}") for i in range(2)] for h_ in range(2)]
            Tnh = [sb([128, 256], NDT, f"Tn{h_}") for h_ in range(2)]
            S0 = sb([64, 64], F32, "S0"); S0b = sb([64, 64], BF16, "S0b"); Stmp = sb([64, 64], F32, "Stmp")
            PTs = [sb([128, 64], BF16, f"PTs{i}") for i in range(2)]; UT = [sb([128, 64], BF16, f"UT{i}") for i in range(2)]
            ostg = [sb([64, 512], BF16, f"ostg{i}") for i in range(2)]
            g1 = sb([64, 512], F32, "g1"); g2 = sb([64, 512], F32, "g2"); g3 = sb([64, 512], F32, "g3")
            lat3 = lambda a: a[:, LOFF:LOFF + L].rearrange("p (t c) -> p t c", c=128)
            ctx3 = lambda a: a[:, COFF:COFF + CL].rearrange("p (t c) -> p t c", c=128)
            for h in range(8):
                w = wh[h % 2]
                for n, c0 in (("r", E_SHIFT + 64 * h), ("k", E_SHIFT + 512 + 64 * h), ("v", E_SHIFT + 1024 + 64 * h), ("z", E_WZ + 64 * h)):
                    kb.loadc(w[n], w[n][:], win0_d, rr(win0_d.ap[:, c0:c0 + 64]))
                pc = lambda v: P5[:, v, h:h + 1]
                shift_proj(w["r"], slice(0, 64), 64, om[:, h:h + 1], hm[:, h:h + 1], rS)
                shift_proj(w["k"], slice(0, 64), 64, om[:, 8 + h:9 + h], hm[:, 8 + h:9 + h], kS)
                shift_proj(w["v"], slice(0, 64), 64, om[:, 16 + h:17 + h], hm[:, 16 + h:17 + h], vS)
                kb.ts(ex, ex[:], [kS, P5], kS[:], pc(0))
                for c0, n in groups:
                    kb.act(g1, g1[:, 0:n], [ex], ex[:, c0:c0 + n], AF.Square)
                    ps = kb.psum(); kb.mm(ps, ps[0:64, 0:n], ones64, ones64[:], g1, g1[:, 0:n])
                    kb.act(g2, g2[:, 0:n], [ps], ps[0:64, 0:n], AF.Sqrt, scale=64.0)
                    kb.ts(g2, g2[:, 0:n], [g2], g2[:, 0:n], 1e-12, None, ALU.max)
                    kb.recip(g2, g2[:, 0:n], [g2], g2[:, 0:n])
                    kb.tt(kkn, kkn[:, c0:c0 + n], [ex, g2], ex[:, c0:c0 + n], g2[:, 0:n], ALU.mult)
                kb.memset(rks, rks[:], 0.0)
                kb.memset(YT, YT[:], 0.0)
                for d in range(2):
                    po = 64 * d
                    for c0, n in groups:
                        ps = kb.psum(); kb.mm(ps, ps[0:64, 0:n], LW, LW[po:po + 64, 1, 64 * h:64 * h + 64], adT, adT[po:po + 64, c0:c0 + n])
                        kb.act(aD, aD[:, c0:c0 + n], [ps, P5], ps[0:64, 0:n], AF.Sigmoid, bias=pc(7 + d))
                        ps = kb.psum(); kb.mm(ps, ps[0:64, 0:n], LW, LW[po:po + 64, 0, 64 * h:64 * h + 64], wdT, wdT[po:po + 64, c0:c0 + n])
                        kb.act(lw_, lw_[:, c0:c0 + n], [ps, P5], ps[0:64, 0:n], AF.Sigmoid, bias=pc(5 + d))
                    kb.ts(lw_, lw_[:], [lw_], lw_[:], -math.exp(-0.5))
                    kb.ts(ex, ex[:], [aD, P5, omka], aD[:], pc(1), omka[:, h:h + 1], ALU.mult, ALU.add)
                    kb.tt(kd, kd[:], [ex, kS], ex[:], kS[:], ALU.mult)
                    kb.tt(bD, bD[:], [kkn, aD], kkn[:], aD[:], ALU.mult, eng=fw.pool)
                    kb.stt(ex, ex[:], [rS, P5, kd], rS[:], pc(2), kd[:], ALU.mult, ALU.mult)
                    kb.tt(rks, rks[:], [rks, ex], rks[:], ex[:], ALU.add, eng=fw.pool)
                    fw.op(fw.dve, lambda: nc.vector.tensor_tensor_scan(out=cum[:], data0=smask[:], data1=lw_[:], initial=0.0, op0=ALU.mult, op1=ALU.add),
                          reads=[smask, lw_], writes=[cum])
                    kb.copy(tot, tot[:, 0:NTL].unsqueeze(2), [cum], lat3(cum)[:, :, 127:128])
                    kb.copy(tot, tot[:, NTL:NTT].unsqueeze(2), [cum], ctx3(cum)[:, :, 127:128])
                    kb.act(WC, WC[:], [tot], tot[:], AF.Exp)
                    if d == 0:
                        kb.act(ex, ex[:], [cum], cum[:], AF.Exp)
                        kb.tt(Rd, Rd[:], [rS, ex], rS[:], ex[:], ALU.mult)
                        kb.tt(ex, ex[:], [cum, lw_], cum[:], lw_[:], ALU.subtract)
                        kb.act(ex, ex[:], [ex], ex[:], AF.Exp)
                        kb.tt(KKd, KKd[:], [kkn, ex], kkn[:], ex[:], ALU.mult)
                    else:
                        for v3, ts_ in ((lat3, slice(0, NTL)), (ctx3, slice(NTL, NTT))):
                            nt_ = ts_.stop - ts_.start
                            kb.tt(cum, v3(cum), [cum, tot], tot[:, ts_].unsqueeze(2).broadcast_to([64, nt_, 128]), v3(cum), ALU.subtract)
                        kb.act(ex, ex[:], [cum], cum[:], AF.Exp)
                        kb.tt(KKd, KKd[:], [kkn, ex], kkn[:], ex[:], ALU.mult)
                        kb.tt(cum, cum[:], [cum, lw_], cum[:], lw_[:], ALU.add)
                        kb.act(ex, ex[:], [cum], cum[:], AF.Exp)
                        kb.tt(Rd, Rd[:], [rS, ex], rS[:], ex[:], ALU.mult)
                    kb.act(ex, ex[:], [cum], cum[:], AF.Exp, scale=-1.0)
                    kb.tt(Bi, Bi[:], [bD, ex], bD[:], ex[:], ALU.mult)
                    kb.tt(Ki, Ki[:], [kd, ex], kd[:], ex[:], ALU.mult, eng=fw.pool)
                    if h == 0 and d == 0:
                        for nm, t_ in (("rS", rS), ("kkn", kkn), ("aD", aD), ("kd", kd), ("KKd", KKd), ("Rd", Rd), ("Bi", Bi), ("Ki", Ki)):
                            kb.tap("w_" + nm, t_, t_[:], [64, NC_], BF16)
                        kb.tap("w_lw", lw_, lw_[:], [64, NC_]); kb.tap("w_cum", cum, cum[:], [64, NC_]); kb.tap("w_WC", WC, WC[:], [64, NTT])
                        if stop == "rwkv_b":
                            fw.barrier(); return
                    ms, mi, msT = (m4[:, 0:128], m4[:, 128:256], m4[:, 256:384]) if d == 0 else (m4[:, 256:384], m4[:, 384:512], m4[:, 0:128])
                    nms, nmsT = (m4n[:, 0:128], m4n[:, 256:384]) if d == 0 else (m4n[:, 256:384], m4n[:, 0:128])
                    X = aD
                    srcs = ([(vS, Vt)] if d == 0 else []) + [(BeT, Be), (KeT, Ke)]
                    ncp = 0
                    for sbuf_, dst_ in srcs:
                        for c_lo, n_ in ((LOFF, L), (COFF, CL)):
                            fw.op(fw.dve, (lambda s_=sbuf_, c_lo=c_lo, n_=n_: nc.vector.transpose(out=X[:, c_lo:c_lo + n_], in_=s_[:, c_lo:c_lo + n_])),
                                  reads=[sbuf_], writes=[X])
                        for c_lo, n_, t0 in ((LOFF, L, 0), (COFF, CL, NTL)):
                            nt_ = n_ // 128
                            for pi in range(2):
                                xv = X[32 * pi:32 * pi + 32, c_lo:c_lo + n_].rearrange("p (t f c) -> p t f c", f=4, c=32)
                                for fj in range(4):
                                    eng_ = fw.pool if ncp % 2 == 0 else fw.act
                                    ncp += 1
                                    kb.copy(dst_, dst_[32 * fj:32 * fj + 32, t0:t0 + nt_, 32 * pi:32 * pi + 32], [X], xv[:, :, fj, :], eng=eng_)
                    def intra_unit(u, hf_):
                        tiles = [2 * u, 2 * u + 1]
                        W_ = 256
                        pA = kb.psum(); pAT = kb.psum(); pK = kb.psum(); pGb = kb.psum(); pGk = kb.psum()
                        for j, ti in enumerate(tiles):
                            c0 = tokcol(ti); cs_ = slice(128 * j, 128 * j + 128)
                            kb.mm(pA, pA[:, cs_], Bi, Bi[:, c0:c0 + 128], KKd, KKd[:, c0:c0 + 128])
                            kb.mm(pAT, pAT[:, cs_], KKd, KKd[:, c0:c0 + 128], Bi, Bi[:, c0:c0 + 128])
                            kb.mm(pK, pK[:, cs_], Ki, Ki[:, c0:c0 + 128], KKd, KKd[:, c0:c0 + 128])
                            kb.mm(pGb, pGb[:, cs_], Bi, Bi[:, c0:c0 + 128], Rd, Rd[:, c0:c0 + 128])
                            kb.mm(pGk, pGk[:, cs_], Ki, Ki[:, c0:c0 + 128], Rd, Rd[:, c0:c0 + 128])
                        v3 = lambda a: a.rearrange("p (t c) -> p t c", c=128)
                        bc = lambda m_: m_.unsqueeze(1).broadcast_to([128, 2, 128])
                        P_, PT_ = NPh[hf_][0], NPTh[hf_][0]
                        kb.tt(P_, v3(P_[:, 0:W_]), [pA, m4n], v3(pA[:, 0:W_]), bc(nms), ALU.mult)
                        kb.tt(PT_, v3(PT_[:, 0:W_]), [pAT, m4n], v3(pAT[:, 0:W_]), bc(nmsT), ALU.mult)
                        kb.tt(Ak, Ak[:, 2 * u:2 * u + 2, :], [pK, m4], v3(pK[:, 0:W_]), bc(ms), ALU.mult)
                        kb.tt(Gb, Gb[:, 2 * u:2 * u + 2, :], [pGb, m4], v3(pGb[:, 0:W_]), bc(mi), ALU.mult)
                        kb.tt(Gk, Gk[:, 2 * u:2 * u + 2, :], [pGk, m4], v3(pGk[:, 0:W_]), bc(mi), ALU.mult)
                        kb.tt(Tnh[hf_], v3(Tnh[hf_][:, 0:W_]), [P_, ident_f], v3(P_[:, 0:W_]), bc(ident_f[:]), ALU.add, eng=fw.pool)

                    def level_unit(hf_, lvl):
                        cur = (lvl - 1) % 2
                        P_, PT_ = NPh[hf_][cur], NPTh[hf_][cur]; Pn, PTn = NPh[hf_][1 - cur], NPTh[hf_][1 - cur]
                        Tn_ = Tnh[hf_]
                        p1 = kb.psum(); p2 = kb.psum(); p3 = kb.psum()
                        for j in range(2):
                            cs_ = slice(128 * j, 128 * j + 128)
                            kb.mm(p2, p2[:, cs_], P_, P_[:, cs_], PT_, PT_[:, cs_])
                            if lvl < 6:
                                kb.mm(p1, p1[:, cs_], PT_, PT_[:, cs_], P_, P_[:, cs_])
                        kb.copy(PTn, PTn[:, 0:256], [p2], p2[:, 0:256], eng=fw.act)
                        if lvl < 6:
                            kb.copy(Pn, Pn[:, 0:256], [p1], p1[:, 0:256])
                        for j in range(2):
                            cs_ = slice(128 * j, 128 * j + 128)
                            kb.mm(p3, p3[:, cs_], PTn, PTn[:, cs_], Tn_, Tn_[:, cs_])
                        kb.tt(Tn_, Tn_[:, 0:256], [Tn_, p3], Tn_[:, 0:256], p3[:, 0:256], ALU.add)

                    for u0 in range(0, NTT // 2, 2):
                        us = [u for u in (u0, u0 + 1) if u < NTT // 2]
                        for hf_, u in enumerate(us):
                            intra_unit(u, hf_)
                        for lvl in range(1, 7):
                            for hf_, u in enumerate(us):
                                level_unit(hf_, lvl)
                        for hf_, u in enumerate(us):
                            kb.copy(Tall, Tall[:, 2 * u:2 * u + 2, :], [Tnh[hf_]], Tnh[hf_][:, 0:256].rearrange("p (t c) -> p t c", c=128), eng=fw.pool)
                    if stop == "rwkv_c":
                        kb.tap("w_Tall", Tall, Tall[:], [128, NTT, 128], BF16); kb.tap("w_Ak", Ak, Ak[:], [128, NTT, 128], BF16)
                        fw.barrier(); return
                    order = ([16, 17] + list(range(16))) if d == 0 else ([17, 16] + list(range(15, -1, -1)))
                    kb.memset(S0, S0[:], 0.0); kb.memset(S0b, S0b[:], 0.0)
                    for i, ti in enumerate(order):
                        c0 = tokcol(ti)
                        pts = PTs[i % 2]; ut = UT[i % 2]
                        ps = kb.psum()
                        kb.mm(ps, ps[:, 0:64], KKd, KKd[:, c0:c0 + 128], S0b, S0b[:], start=True, stop=False)
                        kb.mm(ps, ps[:, 0:64], Ak, Ak[:, ti, :], Vt, Vt[:, ti, :], start=False, stop=True)
                        kb.copy(pts, pts[:], [ps], ps[:, 0:64], eng=fw.act)
                        ps2 = kb.psum()
                        kb.mm(ps2, ps2[:, 0:64], Tall, Tall[:, ti, :], pts, pts[:])
                        kb.ts(ut, ut[:], [ps2], ps2[:, 0:64], -1.0)
                        py = kb.psum()
                        kb.mm(py, py[0:64, 0:128], S0b, S0b[:], Rd, Rd[:, c0:c0 + 128], start=True, stop=False)
                        kb.mm(py, py[0:64, 0:128], ut, ut[:], Gb, Gb[:, ti, :], start=False, stop=False)
                        kb.mm(py, py[0:64, 0:128], Vt, Vt[:, ti, :], Gk, Gk[:, ti, :], start=False, stop=True)
                        kb.tt(YT, YT[:, c0:c0 + 128], [YT, py], YT[:, c0:c0 + 128], py[0:64, 0:128], ALU.add)
                        if i < len(order) - 1:
                            pS = kb.psum()
                            kb.mm(pS, pS[0:64, 0:64], Be, Be[:, ti, :], ut, ut[:], start=True, stop=False)
                            kb.mm(pS, pS[0:64, 0:64], Ke, Ke[:, ti, :], Vt, Vt[:, ti, :], start=False, stop=True)
                            kb.ts(Stmp, Stmp[:], [pS, WC], pS[0:64, 0:64], WC[:, ti:ti + 1])
                            kb.stt(S0, S0[:], [S0, WC, Stmp], S0[:], WC[:, ti:ti + 1], Stmp[:], ALU.mult, ALU.add)
                            kb.copy(S0b, S0b[:], [S0], S0[:], eng=fw.act)
                    if h == 0 and d == 0:
                        kb.tap("w_YTf", YT, YT[:], [64, NC_])
                        if stop == "rwkv_d":
                            fw.barrier(); return
                pp = (h % 2) * 64
                for gci, (c0, n) in enumerate(groups):
                    oc = (c0 - LOFF) if c0 < COFF else (L + c0 - COFF)
                    ps = kb.psum(); kb.mm(ps, ps[0:64, 0:n], ones64, ones64[:], YT, YT[:, c0:c0 + n])
                    kb.tt(g1, g1[:, 0:n], [YT, ps], YT[:, c0:c0 + n], ps[0:64, 0:n], ALU.subtract)
                    kb.act(g2, g2[:, 0:n], [g1], g1[:, 0:n], AF.Square)
                    ps = kb.psum(); kb.mm(ps, ps[0:64, 0:n], ones64, ones64[:], g2, g2[:, 0:n])
                    kb.act(g2, g2[:, 0:n], [ps], ps[0:64, 0:n], AF.Sqrt, bias=64e-5)
                    kb.recip(g2, g2[:, 0:n], [g2], g2[:, 0:n])
                    kb.tt(g1, g1[:, 0:n], [g1, g2], g1[:, 0:n], g2[:, 0:n], ALU.mult)
                    kb.ts(g1, g1[:, 0:n], [g1, P5], g1[:, 0:n], pc(3), pc(4), ALU.mult, ALU.add)
                    kb.copy(g3, g3[:, 0:n], [rks], rks[:, c0:c0 + n], eng=fw.pool)
                    ps = kb.psum(); kb.mm(ps, ps[0:64, 0:n], ones64, ones64[:], g3, g3[:, 0:n])
                    kb.stt(g2, g2[:, 0:n], [ps, vS], ps[0:64, 0:n], 64.0, vS[:, c0:c0 + n], ALU.mult, ALU.mult)
                    kb.tt(g1, g1[:, 0:n], [g1, g2], g1[:, 0:n], g2[:, 0:n], ALU.add, eng=fw.pool)
                    ps = kb.psum(); proj_fm(ps, 64, w["z"], slice(0, 64), c0, n)
                    kb.act(g2, g2[:, 0:n], [ps], ps[0:64, 0:n], AF.Silu)
                    stg = ostg[gci % 2]
                    kb.tt(stg, stg[:, 0:n], [g1, g2], g1[:, 0:n], g2[:, 0:n], ALU.mult, eng=fw.pool)
                    kb.load(oT, oT.ap[pp:pp + 64, 4 + h // 2, oc:oc + n], stg, stg[:, 0:n])
            fw.barrier()

    if not SKIP_L0:
        retention()
    kb.tap("oT0", oT, oT.ap, [128, 8, L + CL], BF16)
    if stop == "ret":
        return finish(kb, es, y_d)
    if os.environ.get("RUN_RWKV", "1") == "1" and not SKIP_L0:
        rwkv()
    kb.tap("oT0b", oT, oT.ap, [128, 8, L + CL], BF16)
    if stop == "rwkv":
        return finish(kb, es, y_d)

    def out_proj(li, wout_d, gateb, src_tile, dst_tile, ntiles, final=False):
        with ExitStack() as les:
            sb = lambda shape, dt, name: fw.sbuf(shape, dt, f"o{li}_" + name, es=les)
            wout = sb([128, 8, D], BF16, "wout"); kb.loadc(wout, wout[:], wout_d, rr(wout_d.ap))
            oTt = [sb([128, 8, 128], BF16, f"oTt{i}") for i in range(2)]
            hb = [sb([128, D], F32, f"h{i}") for i in range(2)]
            hn = [sb([128, D], F32, f"hn{i}") for i in range(2)]
            tmp = sb([128, 512], F32, "tmp")
            if final:
                fnwb = sb([128, D], F32, "fnwb"); kb.load(fnwb, fnwb[:], fnw_d, fnw_d.ap[0].partition_broadcast(128))
                junk = sb([128, D], F32, "junk"); st = [sb([128, 4], F32, f"st{i}") for i in range(2)]
            for ti in range(ntiles):
                ot = oTt[ti % 2]; h = hb[ti % 2]; o = hn[ti % 2]
                nkc = 8 if os.environ.get("RUN_RWKV", "1") == "1" else 4
                kb.load(ot, ot[:, 0:nkc, :], oT, oT.ap[:, 0:nkc, ti * 128:(ti + 1) * 128])
                sbuf_, sap = src_tile(ti)
                kb.load(h, h[:], sbuf_, sap)
                r = 0 if ti < NTL else 1
                for cg in range(2):
                    ps = kb.psum()
                    for kc in range(nkc):
                        kb.mm(ps, ps[:], ot, ot[:, kc, :], wout, wout[:, kc, cg * 512:(cg + 1) * 512], start=(kc == 0), stop=(kc == nkc - 1))
                    kb.tt(tmp, tmp[:], [ps, gateb], ps[:], gateb[:, r, cg * 512:(cg + 1) * 512], ALU.mult)
                    kb.tt(o, o[:, cg * 512:(cg + 1) * 512], [h, tmp], h[:, cg * 512:(cg + 1) * 512], tmp[:], ALU.add, eng=fw.pool)
                if final:
                    s = st[ti % 2]
                    kb.act(junk, junk[:], [o], o[:], AF.Square, accum=s[:, 0:1], extra_w=[s])
                    kb.act(s, s[:, 1:2], [s], s[:, 0:1], AF.Sqrt, scale=1.0 / D, bias=EPS)
                    kb.recip(s, s[:, 2:3], [s], s[:, 1:2])
                    kb.stt(o, o[:], [o, s, fnwb], o[:], s[:, 2:3], fnwb[:], ALU.mult, ALU.mult)
                db, dap = dst_tile(ti)
                kb.load(db, dap, o, o[:])
            fw.barrier()

    def h1_tile(ti):
        if SKIP_L0:
            return h1in_d, h1in_d.ap[ti * 128:(ti + 1) * 128, :]
        return h1_d, h1_d.ap[ti * 128:(ti + 1) * 128, :]

    def y_tile(ti):
        return y_d, y_d.ap[ti * 128:(ti + 1) * 128, :]


    def na_attention():
        with ExitStack() as les:
            sb = lambda shape, dt, name: fw.sbuf(shape, dt, "a_" + name, es=les)
            R_d = kb.scratch("na_R", [120, 64 * 96])
            with ExitStack() as zes:
                z = fw.sbuf([120, 64 * 96], F32, "a_zero", es=zes); kb.memset(z, z[:], 0.0)
                kb.load(R_d, R_d.ap, z, z[:])
                fw.barrier()
            Rt = R_d.ap.tensor
            for h in range(8):
                dst = bass.AP(Rt, h * 15 * 6144, [[6144, 15], [97, 64], [1, 31]])
                srcap = bass.AP(rpb_d.ap.tensor, h * 15 * 31, [[31, 15], [0, 64], [1, 31]])
                kb.load(R_d, dst, rpb_d, srcap)
            BiasT = sb([64, 120, 64], F32, "BiasT")
            for h in range(8):
                srcap = bass.AP(Rt, h * 15 * 6144 + 15, [[96, 64], [6144, 15], [1, 64]])
                kb.load(BiasT, BiasT[:, 15 * h:15 * h + 15, :], R_d, srcap)
            wm = sb([64, 64], F32, "wm"); kb.load(wm, wm[:], wmask_d, wmask_d.ap)
            kb.tt(BiasT, BiasT[:], [BiasT, wm], BiasT[:], wm[:].unsqueeze(1).broadcast_to([64, 120, 64]), ALU.add, eng=fw.pool)
            kb.tap("a_BiasT", BiasT, BiasT[:], [64, 120, 64])
            ones_bf = sb([128, 64], BF16, "ones_bf"); kb.memset(ones_bf, ones_bf[:], 1.0)
            wts = {n: sb([128, 8, 64], BF16, n) for n in ("wq", "wk", "wv", "wz")}
            qT = sb([64, L], BF16, "qT"); kT = sb([64, L + CL], BF16, "kT")
            Ve = sb([128, NTT, 64], BF16, "Ve"); Vo = sb([128, 15, 64], BF16, "Vo")
            szT = sb([64, L], F32, "szT")
            Sb = [sb([64, 768], F32, f"Sb{i}") for i in range(2)]
            Pb = [sb([128, 768], BF16, f"Pb{i}") for i in range(2)]
            PT = [sb([128, 6, 128], BF16, f"PT{i}") for i in range(2)]
            mx = [sb([64, 2], F32, f"mx{i}") for i in range(2)]
            rd = sb([64, 512], F32, "rd"); on = sb([64, 512], F32, "on")
            ostg = [sb([64, 512], BF16, f"ostg{i}") for i in range(2)]
            for h in range(8):
                for n, c0 in (("wq", 2048 + 64 * h), ("wk", 2560 + 64 * h), ("wv", 3072 + 64 * h), ("wz", 3584 + 64 * h)):
                    kb.loadc(wts[n], wts[n][:], win1_d, rr(win1_d.ap[:, c0:c0 + 64]))
                for g in range(4):
                    ps = kb.psum(); proj_fm(ps, 64, wts["wq"], slice(0, 64), LOFF + 512 * g, 512)
                    kb.act(qT, qT[:, 512 * g:512 * (g + 1)], [ps], ps[0:64, :], AF.Identity, scale=0.125)
                    ps = kb.psum(); proj_fm(ps, 64, wts["wk"], slice(0, 64), LOFF + 512 * g, 512)
                    kb.copy(kT, kT[:, 512 * g:512 * (g + 1)], [ps], ps[0:64, :])
                    ps = kb.psum(); proj_fm(ps, 64, wts["wz"], slice(0, 64), LOFF + 512 * g, 512)
                    kb.act(szT, szT[:, 512 * g:512 * (g + 1)], [ps], ps[0:64, :], AF.Silu)
                ps = kb.psum(); proj_fm(ps, 64, wts["wk"], slice(0, 64), COFF, 256)
                kb.copy(kT, kT[:, L:L + CL], [ps], ps[0:64, 0:256])
                for ti in range(NTT):
                    ps = kb.psum(); proj_tm(ps, tokcol(ti), wts["wv"], slice(0, 64), 64)
                    kb.copy(Ve, Ve[:, ti, :], [ps], ps[:, 0:64], eng=fw.act)
                for c in range(15):
                    ps = kb.psum(); proj_tm(ps, LOFF + 64 + 128 * c, wts["wv"], slice(0, 64), 64)
                    kb.copy(Vo, Vo[:, c, :], [ps], ps[:, 0:64], eng=fw.act)
                for g8 in range(4):
                    pnum = kb.psum(hold=True); pden = kb.psum(hold=True)
                    for rp in range(4):
                        i2 = rp % 2
                        pb = Pb[i2]; pt = PT[i2]
                        srs = []
                        for half in range(2):
                            r = 8 * g8 + 2 * rp + half
                            sr = min(max(r - 4, 0), 24)
                            srs.append(sr)
                            sbf = Sb[half]; m_ = mx[half]
                            psA = kb.psum(); psB = kb.psum()
                            kb.mm(psA, psA[0:64, :], qT, qT[:, 64 * r:64 * r + 64], kT, kT[:, 64 * sr:64 * sr + 512])
                            kb.mm(psB, psB[0:64, 0:256], qT, qT[:, 64 * r:64 * r + 64], kT, kT[:, L:L + CL])
                            d0 = sr - r + 7
                            kb.tt(sbf, sbf[:, 0:512], [psA, BiasT], psA[0:64, :], BiasT[:, 15 * h + d0:15 * h + d0 + 8, :].rearrange("p a b -> p (a b)"), ALU.add)
                            kb.copy(sbf, sbf[:, 512:768], [psB], psB[0:64, 0:256], eng=fw.act)
                            fw.op(fw.dve, lambda sbf=sbf, m_=m_: nc.vector.reduce_max(out=m_[:, 0:1], in_=sbf[:], axis=AX.X), reads=[sbf], writes=[m_])
                            kb.ts(m_, m_[:, 1:2], [m_], m_[:, 0:1], -1.0)
                            kb.act(pb, pb[64 * half:64 * half + 64, :], [sbf, m_], sbf[:], AF.Exp, bias=m_[:, 1:2])
                        pst = kb.psum(); pstb = pst.ap[:].bitcast(BF16)
                        for j in range(6):
                            kb.tr(pst, pstb[:, 128 * j:128 * (j + 1)], pb, pb[:, 128 * j:128 * (j + 1)], ident_bf, ident_bf[:])
                        kb.copy(pt, pt[:].rearrange("p a b -> p (a b)"), [pst], pstb[:, 0:768])
                        for half in range(2):
                            sr = srs[half]
                            cs_ = slice(64 * (2 * rp + half), 64 * (2 * rp + half) + 64)
                            hs_ = slice(64 * half, 64 * half + 64)
                            for j in range(6):
                                if j < 4:
                                    vb, vi = (Ve, sr // 2 + j) if sr % 2 == 0 else (Vo, (sr - 1) // 2 + j)
                                else:
                                    vb, vi = Ve, NTL + (j - 4)
                                kb.mm(pnum, pnum[0:64, cs_], vb, vb[:, vi, :], pt, pt[:, j, hs_], start=(j == 0), stop=(j == 5))
                            for j in range(6):
                                kb.mm(pden, pden[0:64, cs_], ones_bf, ones_bf[:], pt, pt[:, j, hs_], start=(j == 0), stop=(j == 5))
                    kb.release(pnum); kb.release(pden)
                    kb.recip(rd, rd[:], [pden], pden[0:64, :])
                    kb.tt(on, on[:], [pnum, rd], pnum[0:64, :], rd[:], ALU.mult)
                    stg = ostg[g8 % 2]
                    kb.tt(stg, stg[:], [on, szT], on[:], szT[:, 512 * g8:512 * (g8 + 1)], ALU.mult, eng=fw.pool)
                    po = (h % 2) * 64
                    kb.load(oT, oT.ap[po:po + 64, 4 + h // 2, 512 * g8:512 * (g8 + 1)], stg, stg[:])
            fw.barrier()

    only0 = os.environ.get('FULL_L0_TEST') != '1'
    only0 = os.environ.get("ONLY_L0") == "1"
    if not SKIP_L0:
        out_proj(0, wout0_d, gateb0, src0, y_tile if only0 else h1_tile, NTL if only0 else NTT, final=only0)
    if only0:
        return finish(kb, es, y_d)
    if not SKIP_L0:
        kb.tap("h1", h1_d, h1_d.ap, [L + CL, D])
    if stop == "l0":
        return finish(kb, es, y_d)
    modcol1, gateb1 = layer_mod(1)
    layer_norm(1, h1_tile, modcol1)
    kb.tap("nT1", nT, nT[:], [128, 8, NTC], BF16)
    if os.environ.get("SKIP_NA") != "1":
        na_attention()
    kb.tap("oT1", oT, oT.ap, [128, 8, L + CL], BF16)
    if stop == "na":
        return finish(kb, es, y_d)
    hyena()
    kb.tap("oT1b", oT, oT.ap, [128, 8, L + CL], BF16)
    if stop == "hy":
        return finish(kb, es, y_d)
    out_proj(1, wout1_d, gateb1, h1_tile, y_tile, NTL, final=True)
    return finish(kb, es, y_d)


def finish(kb, es, y_d):
    outs = [y_d] + list(kb.taps.values())
    kb.fw.finish([o for o in outs if o.w])
    print("ninstr", kb.fw.ninstr, "nwaits", kb.fw.nwaits, {e.name: e.count for e in (kb.fw.pe, kb.fw.act, kb.fw.dve, kb.fw.pool)}, "dma", max(kb.fw.sp.dcnt), max(kb.fw.pool.dcnt), "waits", {e.name: getattr(e, "nw", 0) for e in (kb.fw.pe, kb.fw.act, kb.fw.dve, kb.fw.pool, kb.fw.sp)})
    es.close()
    return kb.nc, kb


def host_consts():
    c = {}
    c["ident"] = np.eye(128, dtype=np.float32)
    t = np.arange(L)
    row = (t // 64).astype(np.float32); col = (t % 64).astype(np.float32)
    inv = (10000.0 ** (-np.arange(16, dtype=np.float32) / 16)).astype(np.float32)
    ang = np.concatenate([row[:, None] * inv, col[:, None] * inv], -1).astype(np.float32)
    cos = np.cos(ang).astype(np.float32); sin = np.sin(ang).astype(np.float32)
    c["cs_fm"] = np.ascontiguousarray(np.concatenate([cos, cos], -1).T)
    c["sn_fm"] = np.ascontiguousarray(np.concatenate([-sin, sin], -1).T)
    c["cs_tm"] = np.ascontiguousarray(np.concatenate([cos, cos], -1))
    c["sn_tm"] = np.ascontiguousarray(np.concatenate([-sin, sin], -1))
    p = np.arange(128, dtype=np.float32)
    c["colAB"] = np.stack([127 - p, p], -1).astype(np.float32)
    tt = np.arange(128, dtype=np.float32)
    c["iota12"] = np.broadcast_to(np.concatenate([tt + 1, 128 - tt])[None], (64, 256)).astype(np.float32).copy()
    s = p[:, None]; t2 = tt[None, :]
    c["dmask"] = np.concatenate([np.maximum(t2 - s, 0), np.maximum(s - t2, 0), (t2 >= s) * 1.0, (s >= t2) * 1.0], -1).astype(np.float32)
    ab = np.arange(2176, dtype=np.int64)
    prod = (ab[:, None] * ab[None, :]) % 4096
    valid = ((ab[:, None] <= 2048) & (ab[None, :] <= 2048))
    ang_ = prod.astype(np.float64) * (2.0 * np.pi / 4096.0)
    gcs = np.stack([np.where(valid, np.cos(ang_), 0.0), np.where(valid, np.sin(ang_), 0.0)], 0).astype(np.float32)
    gt = gcs.reshape(2, 17, 128, 17, 128).transpose(3, 2, 0, 1, 4)
    c["Gt"] = np.ascontiguousarray(gt.reshape(17, 128, 2 * 17 * 128)).astype(ml_dtypes.bfloat16)
    tl = np.linspace(0.0, 1.0, L, dtype=np.float32)[:, None]
    bands = np.linspace(1e-4, 15, 16, dtype=np.float32)
    angz = (np.float32(2.0 * math.pi / L) * np.arange(L, dtype=np.float32)[:, None] * bands[None]).astype(np.float32)
    c["hy_zT"] = np.ascontiguousarray(np.concatenate([tl, np.cos(angz), -np.sin(angz)], -1).T.astype(np.float32))
    c["hy_trow"] = np.ascontiguousarray(tl.T)
    deltas = np.abs(np.linspace(math.log(1e-2) / 1.5, math.log(1e-2) / 0.3, 512, dtype=np.float32))
    c["hy_ndelta"] = (-deltas[None, :]).astype(np.float32)
    fidx = np.arange(17 * 128).reshape(17, 128).T
    c["hy_wt"] = np.where((fidx == 0) | (fidx == 2048), 1.0 / 4096, np.where(fidx < 2048, 2.0 / 4096, 0.0)).astype(np.float32)
    qc = np.arange(64)[:, None]; kc = np.arange(64)[None, :]
    st = np.clip(qc - 8, 0, 48)
    c["wmask"] = np.where((kc >= st) & (kc < st + 16), 0.0, NEG).astype(np.float32)
    sm = np.ones((64, NTC), np.float32)
    for ti in range(NTT):
        sm[:, tokcol(ti)] = 0.0
    c["scanmask"] = sm
    c["masks4"] = np.concatenate([(s < t2) * 1.0, (s <= t2) * 1.0, (s > t2) * 1.0, (s >= t2) * 1.0], -1).astype(np.float32)
    return c


def rot_perm():
    idx = []
    for blk in range(16):
        b = blk * 64
        idx += list(range(b + 32, b + 64)) + list(range(b, b + 32))
    return np.array(idx)


def prep_inputs(inputs, b, consts):
    f = lambda a: np.ascontiguousarray(np.asarray(a, dtype=np.float32))
    m = dict(consts)
    m["x"] = f(inputs["x"][b]); m["ctx"] = f(inputs["ctx"][b])
    cc = np.stack([np.asarray(inputs["c"][b]), np.asarray(inputs["c_ctx"])], -1)
    m["ccT"] = f(cc.reshape(8, 128, 2).transpose(1, 0, 2))
    m["ada_w"] = f(inputs["ada_w"]); m["ada_b"] = f(inputs["ada_b"]); m["norm_w"] = f(inputs["norm_w"])
    m["fnw"] = f(np.asarray(inputs["final_norm_w"])[None])
    w0 = np.asarray(inputs["even_w_in"][0])
    m["w_in0"] = f(w0); m["w_rot0"] = f(w0[:, rot_perm()]); m["w_out0"] = f(inputs["even_w_out"][0])
    mu = np.asarray(inputs["rw_mu"][0])
    m["muT"] = f(mu.reshape(28, 64).T); m["mu128"] = f(np.stack([mu[1536:1664], mu[1664:1792]], -1))
    vecs = [inputs["rw_kk"][0], inputs["rw_ka"][0], np.asarray(inputs["rw_rk"][0]).reshape(-1), inputs["rw_ln_w"][0], inputs["rw_ln_b"][0],
            inputs["rw_w0"][0][0], inputs["rw_w0"][0][1], inputs["rw_a0"][0][0], inputs["rw_a0"][0][1]]
    m["P512"] = f(np.stack([np.asarray(v).reshape(8, 64).T for v in vecs], 1))
    m["LW"] = f(np.stack([np.asarray(inputs["rw_w2"][0]).reshape(128, 512), np.asarray(inputs["rw_a2"][0]).reshape(128, 512)], 1))
    m["w_in1"] = f(inputs["odd_w_in"][0]); m["w_out1"] = f(inputs["odd_w_out"][0]); m["na_rpb"] = f(inputs["na_rpb"][0])
    m["hy_w1"] = f(inputs["hy_w1"][0]); m["hy_w2"] = f(inputs["hy_w2"][0]); m["hy_w3"] = f(inputs["hy_w3"][0])
    m["hy_vec"] = f(np.stack([inputs["hy_b1"][0], inputs["hy_f1"][0], inputs["hy_b2"][0], inputs["hy_f2"][0]], -1))
    m["hy_conv"] = f(np.concatenate([inputs["hy_conv_w"][0], inputs["hy_conv_b"][0][None]], 0)); m["hy_bias"] = f(inputs["hy_bias"][0])
    m["ret_decay"] = f(np.asarray(inputs["ret_decay"][0]).reshape(16))
    return m


_CACHE = {}


def kernel(**inputs):
    if "nc" not in _CACHE:
        _CACHE["nc"] = build()
    nc, kb = _CACHE["nc"]
    consts = host_consts()
    in_maps = []
    for b in range(8):
        m = prep_inputs(inputs, b, consts)
        in_maps.append({k: m[k] for k in kb.din})
    res = run_bass_kernel_spmd(nc, in_maps, core_ids=list(range(8)))
    return np.stack([np.asarray(r["y"]) for r in res.results], 0).astype(np.float32)
```

```python
import math
import os
from contextlib import ExitStack

import numpy as np
import ml_dtypes
import concourse.bass as bass
import concourse.mybir as mybir
from concourse.bass_utils import run_bass_kernel_spmd

F32 = mybir.dt.float32
BF16 = mybir.dt.bfloat16
AF = mybir.ActivationFunctionType
ALU = mybir.AluOpType
AX = mybir.AxisListType

SAME_ENGINE_SYNC = True

D = 1024
L = 2048
CL = 256
NTL = 16
NTT = 18
LOFF = 1
COFF = 2050
NTC = 2308
EPS = 1e-6
E_RQ, E_RK, E_RV, E_RZ, E_WZ, E_SHIFT = 0, 512, 1024, 1536, 2048, 2560
NEG = -30000.0


class Buf:
    def __init__(self, ap, name=""):
        self.ap = ap
        self.name = name
        self.w = {}
        self.r = {}

    def __getitem__(self, idx):
        return self.ap[idx]


class View:
    def __init__(self, parent, ap):
        self.parent = parent
        self.ap = ap

    w = property(lambda s: s.parent.w, lambda s, v: setattr(s.parent, "w", v))
    r = property(lambda s: s.parent.r, lambda s, v: setattr(s.parent, "r", v))

    def __getitem__(self, idx):
        return self.ap[idx]


class Eng:
    def __init__(self, fw, name, handle, is_pe=False):
        self.fw = fw
        self.name = name
        self.h = handle
        self.sem = fw.new_sem("c_" + name)
        self.count = 0
        self.seen = {}
        self.is_pe = is_pe
        self.dsems = []
        self.dcnt = []
        self.dnext = 0

    def wait_tokens(self, toks):
        for key, (sem, val) in toks.items():
            if self.seen.get(key, 0) >= val:
                continue
            if sem is self.sem and (self.is_pe or not SAME_ENGINE_SYNC):
                continue
            self.h.wait_ge(sem, val)
            self.seen[key] = val
            self.fw.nwaits += 1
            self.nw = getattr(self, 'nw', 0) + 1


def _merge(d, key, sem, val):
    if key not in d or d[key][1] < val:
        d[key] = (sem, val)


class FW:
    def __init__(self, nc, es, n_dma_sems=8):
        self.nc = nc
        self.es = es
        self.nwaits = 0
        self.ninstr = 0
        self.pe = Eng(self, "pe", nc.tensor, is_pe=True)
        self.act = Eng(self, "act", nc.scalar)
        self.dve = Eng(self, "dve", nc.vector)
        self.pool = Eng(self, "pool", nc.gpsimd)
        self.sp = Eng(self, "sp", nc.sync)
        for q in (self.sp, self.pool):
            n = n_dma_sems if q is not self.sp else 2 * n_dma_sems
            for i in range(n):
                q.dsems.append(self.new_sem(f"d_{q.name}{i}"))
                q.dcnt.append(0)
        self._uid = 0

    def new_sem(self, name):
        return self.es.enter_context(self.nc.semaphore(name))

    def uid(self, p):
        self._uid += 1
        return f"{p}{self._uid}"

    def sbuf(self, shape, dtype, name=None, es=None):
        t = (es or self.es).enter_context(self.nc.sbuf_tensor("s_" + (name or self.uid("sb")), list(shape), dtype))
        return Buf(t[:] if False else t, name)

    def psum(self, shape, dtype, name=None):
        t = self.es.enter_context(self.nc.psum_tensor(name or self.uid("ps"), list(shape), dtype))
        return Buf(t, name)

    def _collect(self, reads, writes):
        toks = {}
        for b in reads:
            for k, (s, v) in b.w.items():
                _merge(toks, k, s, v)
        for b in writes:
            for k, (s, v) in b.w.items():
                _merge(toks, k, s, v)
            for k, (s, v) in b.r.items():
                _merge(toks, k, s, v)
        return toks

    def op(self, eng, fn, reads=(), writes=(), skip_self=False):
        toks = self._collect(reads, writes)
        if skip_self:
            toks = {k: v for k, v in toks.items() if v[0] is not eng.sem}
        eng.wait_tokens(toks)
        ins = fn()
        eng.count += 1
        ins.then_inc(eng.sem, 1)
        self.ninstr += 1
        key = id(eng.sem)
        for b in reads:
            _merge(b.r, key, eng.sem, eng.count)
        for b in writes:
            b.w = {key: (eng.sem, eng.count)}
            b.r = {}
        return ins

    def dma(self, q, out_ap, in_ap, reads=(), writes=(), **kw):
        toks = self._collect(reads, writes)
        i = q.dnext
        q.dnext = (q.dnext + 1) % len(q.dsems)
        sem = q.dsems[i]
        key = id(sem)
        if q.dcnt[i] > 0:
            _merge(toks, key, sem, q.dcnt[i])
        q.wait_tokens(toks)
        q.dcnt[i] += 16
        val = q.dcnt[i]
        q.h.dma_start(out=out_ap, in_=in_ap, **kw).then_inc(sem, 16)
        self.ninstr += 1
        for b in reads:
            _merge(b.r, key, sem, val)
        for b in writes:
            b.w = {key: (sem, val)}
            b.r = {}

    def barrier(self):
        toks = {}
        for e in (self.pe, self.act, self.dve, self.pool):
            if e.count:
                toks[id(e.sem)] = (e.sem, e.count)
        for q in (self.sp, self.pool):
            for s, c in zip(q.dsems, q.dcnt):
                if c:
                    toks[id(s)] = (s, c)
        for e in (self.pe, self.act, self.dve, self.pool, self.sp):
            for key, (sem, val) in toks.items():
                if sem is e.sem or e.seen.get(key, 0) >= val:
                    continue
                e.h.wait_ge(sem, val)
                e.seen[key] = val
                self.nwaits += 1
                e.nw = getattr(e, 'nw', 0) + 1

    def finish(self, bufs):
        toks = {}
        for b in bufs:
            for k, (s, v) in b.w.items():
                _merge(toks, k, s, v)
        self.sp.wait_tokens(toks)


def tokcol(tile):
    return LOFF + 128 * tile if tile < NTL else COFF + 128 * (tile - NTL)


class KB:
    def __init__(self, nc, es, dbg):
        self.nc = nc
        self.es = es
        self.fw = FW(nc, es)
        self.dbg = dbg or {}
        self.din = {}
        self.taps = {}
        self.ps = [self.fw.psum([128, 512], F32, name=f"psb{i}") for i in range(8)][: int(os.environ.get("NPS", "8"))]
        self.psn = 0
        self.held = set()

    def inp(self, name, shape, dtype=F32):
        ap = self.nc.dram_tensor(name, list(shape), dtype, kind="ExternalInput").ap()
        self.din[name] = Buf(ap, name)
        return self.din[name]

    def scratch(self, name, shape, dtype=F32):
        return Buf(self.nc.dram_tensor(name, list(shape), dtype, kind="Internal").ap(), name)

    def out(self, name, shape, dtype=F32):
        return Buf(self.nc.dram_tensor(name, list(shape), dtype, kind="ExternalOutput").ap(), name)

    def tap(self, name, buf, ap, shape, dtype=F32):
        if name not in self.dbg:
            return
        o = self.out("tap_" + name, shape, dtype)
        self.fw.dma(self.fw.sp, o.ap, ap, reads=[buf], writes=[o])
        self.taps[name] = o

    def psum(self, hold=False):
        while True:
            p = self.ps[self.psn]
            self.psn = (self.psn + 1) % len(self.ps)
            if id(p) not in self.held:
                break
        if hold:
            self.held.add(id(p))
        return p

    def release(self, p):
        self.held.discard(id(p))

    def mm(self, ob, out, lb, lhsT, rb, rhs, start=True, stop=True):
        nc = self.nc
        return self.fw.op(self.fw.pe, lambda: nc.tensor.matmul(out, lhsT=lhsT, rhs=rhs, start=start, stop=stop),
                          reads=[lb, rb], writes=[ob])

    def tr(self, ob, out, ib, in_, idb, ident):
        nc = self.nc
        return self.fw.op(self.fw.pe, lambda: nc.tensor.transpose(out, in_, ident), reads=[ib, idb], writes=[ob])

    def act(self, ob, out, ibs, in_, func, scale=None, bias=None, accum=None, extra_w=()):
        nc = self.nc
        kw = {}
        if scale is not None:
            kw["scale"] = scale
        if bias is not None:
            kw["bias"] = bias
        if accum is not None:
            kw["accum_out"] = accum
        return self.fw.op(self.fw.act, lambda: nc.scalar.activation(out=out, in_=in_, func=func, **kw),
                          reads=list(ibs), writes=[ob] + list(extra_w))

    def tt(self, ob, out, ibs, in0, in1, op, eng=None):
        eng = eng or self.fw.dve
        return self.fw.op(eng, lambda: eng.h.tensor_tensor(out=out, in0=in0, in1=in1, op=op), reads=list(ibs), writes=[ob])

    def ts(self, ob, out, ibs, in0, s1, s2=None, op0=ALU.mult, op1=None, eng=None):
        eng = eng or self.fw.dve
        if op1 is None:
            return self.fw.op(eng, lambda: eng.h.tensor_scalar(out=out, in0=in0, scalar1=s1, scalar2=None, op0=op0),
                              reads=list(ibs), writes=[ob])
        return self.fw.op(eng, lambda: eng.h.tensor_scalar(out=out, in0=in0, scalar1=s1, scalar2=s2, op0=op0, op1=op1),
                          reads=list(ibs), writes=[ob])

    def stt(self, ob, out, ibs, in0, scalar, in1, op0, op1):
        nc = self.nc
        return self.fw.op(self.fw.dve, lambda: nc.vector.scalar_tensor_tensor(out=out, in0=in0, scalar=scalar, in1=in1, op0=op0, op1=op1),
                          reads=list(ibs), writes=[ob])

    def copy(self, ob, out, ibs, in_, eng=None):
        eng = eng or self.fw.dve
        if eng is self.fw.act:
            return self.act(ob, out, ibs, in_, AF.Identity)
        return self.fw.op(eng, lambda: eng.h.tensor_copy(out=out, in_=in_), reads=list(ibs), writes=[ob])

    def memset(self, ob, out, val, eng=None):
        eng = eng or self.fw.pool
        return self.fw.op(eng, lambda: eng.h.memset(out, val), writes=[ob])

    def recip(self, ob, out, ibs, in_):
        nc = self.nc
        return self.fw.op(self.fw.dve, lambda: nc.vector.reciprocal(out=out, in_=in_), reads=list(ibs), writes=[ob])

    def load(self, ob, out, ib, in_, q=None, **kw):
        self.fw.dma(q or self.fw.sp, out, in_, reads=[ib], writes=[ob], **kw)

    def loadc(self, ob, out, ib, in_, **kw):
        self.fw.dma(self.fw.pool, out, in_, reads=[ib], writes=[ob], **kw)


def rr(ap, p=128):
    return ap.rearrange("(k p) c -> p k c", p=p)


def build(dbg=None, stop=None):
    nc = bass.Bass("TRN2", target_bir_lowering=False)
    es = ExitStack()
    kb = KB(nc, es, dbg)
    fw = kb.fw
    I = kb.inp
    x_d = I("x", [L, D]); ctx_d = I("ctx", [CL, D]); cc_d = I("ccT", [128, 8, 2])
    adaw_d = I("ada_w", [2, D, 3 * D]); adab_d = I("ada_b", [2, 3 * D]); normw_d = I("norm_w", [2, D]); fnw_d = I("fnw", [1, D])
    win0_d = I("w_in0", [D, 4352]); wrot0_d = I("w_rot0", [D, 1024]); wout0_d = I("w_out0", [D, D])
    ident_d = I("ident", [128, 128])
    csq_d = I("cs_fm", [64, L]); snq_d = I("sn_fm", [64, L]); cstm_d = I("cs_tm", [L, 64]); sntm_d = I("sn_tm", [L, 64])
    retdec_d = I("ret_decay", [16]); colAB_d = I("colAB", [128, 2]); iota12_d = I("iota12", [64, 256])
    dmask_d = I("dmask", [128, 512])
    muT_d = I("muT", [64, 28]); mu128_d = I("mu128", [128, 2]); p512_d = I("P512", [64, 9, 8]); lw_d = I("LW", [128, 2, 512])
    scanmask_d = I("scanmask", [64, NTC]); masks4_d = I("masks4", [128, 512])
    win1_d = I("w_in1", [D, 4096]); wout1_d = I("w_out1", [D, D]); rpb_d = I("na_rpb", [8, 15, 31]); wmask_d = I("wmask", [64, 64])
    Gt_d = I("Gt", [17, 128, 2 * 17 * 128], BF16); zT_d = I("hy_zT", [33, L]); trow_d = I("hy_trow", [1, L]); ndel_d = I("hy_ndelta", [1, 512])
    wtc_d = I("hy_wt", [128, 17]); hw1_d = I("hy_w1", [33, 64]); hw2_d = I("hy_w2", [64, 64]); hw3_d = I("hy_w3", [64, 2048]); hvec_d = I("hy_vec", [64, 4])
    cw_d = I("hy_conv", [4, 1536]); hbias_d = I("hy_bias", [2, 512])
    y_d = kb.out("y", [L, D])
    h1_d = kb.scratch("h1", [L + CL, D])

    ident_bf = fw.sbuf([128, 128], BF16, "ident_bf"); kb.loadc(ident_bf, ident_bf[:], ident_d, ident_d.ap)
    ident_f = fw.sbuf([128, 128], F32, "ident_f"); kb.load(ident_f, ident_f[:], ident_d, ident_d.ap)
    ones_f = fw.sbuf([128, 128], F32, "ones_f"); kb.memset(ones_f, ones_f[:], 1.0)
    ccT = fw.sbuf([128, 8, 2], F32, "ccT"); kb.load(ccT, ccT[:], cc_d, cc_d.ap)
    scT = fw.sbuf([128, 8, 2], F32, "scT"); kb.act(scT, scT[:], [ccT], ccT[:], AF.Silu)
    nT = fw.sbuf([128, 8, NTC], BF16, "nT")
    kb.memset(nT, nT[:], 0.0)
    oT = kb.scratch("oT_d", [128, 8, L + CL], BF16)

    def layer_mod(li):
        modcol = fw.sbuf([128, 4, 8], F32, f"modcol{li}")
        gateb = fw.sbuf([128, 2, D], F32, f"gateb{li}")
        with ExitStack() as les:
            rows = [fw.sbuf([1, 3 * D], F32, f"modrow{li}_{r}", es=les) for r in range(2)]
            adab = fw.sbuf([1, 3 * D], F32, f"adab{li}", es=les); kb.load(adab, adab[:], adab_d, adab_d.ap[li:li + 1, :])
            nw = fw.sbuf([1, D], F32, f"nw{li}", es=les); kb.load(nw, nw[:], normw_d, normw_d.ap[li:li + 1, :])
            wbufs = [fw.sbuf([128, 8, 512], F32, f"adaw{li}_{i}", es=les) for i in range(2)]
            for cg in range(6):
                wb = wbufs[cg % 2]
                kb.load(wb, wb[:], adaw_d, rr(adaw_d.ap[li, :, cg * 512:(cg + 1) * 512]))
                for r in range(2):
                    ps = kb.psum()
                    for kc in range(8):
                        kb.mm(ps, ps[0:1, :], scT, scT[:, kc, r:r + 1], wb, wb[:, kc, :], start=(kc == 0), stop=(kc == 7))
                    kb.tt(rows[r], rows[r][0:1, cg * 512:(cg + 1) * 512], [ps, adab], ps[0:1, :], adab[0:1, cg * 512:(cg + 1) * 512], ALU.add)
            for r in range(2):
                grow = fw.sbuf([1, D], F32, f"grow{li}_{r}", es=les)
                kb.stt(grow, grow[:], [rows[r], nw], rows[r][0:1, D:2 * D], 1.0, nw[:], ALU.add, ALU.mult)
                ps = kb.psum()
                for kc in range(8):
                    kb.mm(ps, ps[:, kc:kc + 1], grow, grow[0:1, kc * 128:(kc + 1) * 128], ones_f, ones_f[0:1, 0:1])
                    kb.mm(ps, ps[:, 8 + kc:9 + kc], rows[r], rows[r][0:1, kc * 128:(kc + 1) * 128], ones_f, ones_f[0:1, 0:1])
                kb.copy(modcol, modcol[:, 2 * r:2 * r + 2, :], [ps], ps[:, 0:16].rearrange("p (a k) -> p a k", a=2))
                for cg in range(2):
                    ps = kb.psum()
                    kb.mm(ps, ps[:], ones_f, ones_f[0:1, :], rows[r], rows[r][0:1, 2 * D + cg * 512:2 * D + (cg + 1) * 512])
                    kb.copy(gateb, gateb[:, r, cg * 512:(cg + 1) * 512], [ps], ps[:], eng=fw.act)
            fw.barrier()
        return modcol, gateb

    def layer_norm(li, src_tile, modcol):
        with ExitStack() as les:
            hb = [fw.sbuf([128, D], F32, f"nh{li}_{i}", es=les) for i in range(2)]
            xs = [fw.sbuf([128, D], BF16, f"nxs{li}_{i}", es=les) for i in range(2)]
            junk = fw.sbuf([128, D], F32, f"njunk{li}", es=les)
            st = [fw.sbuf([128, 4], F32, f"nst{li}_{i}", es=les) for i in range(2)]
            for ti in range(NTT):
                h = hb[ti % 2]; xsb = xs[ti % 2]; s = st[ti % 2]
                sb, sap = src_tile(ti)
                kb.load(h, h[:], sb, sap)
                kb.act(junk, junk[:], [h], h[:], AF.Square, accum=s[:, 0:1], extra_w=[s])
                kb.act(s, s[:, 1:2], [s], s[:, 0:1], AF.Sqrt, scale=1.0 / D, bias=EPS)
                kb.recip(s, s[:, 2:3], [s], s[:, 1:2])
                kb.act(xsb, xsb[:], [h, s], h[:], AF.Identity, scale=s[:, 2:3])
                ps = kb.psum()
                psb = ps.ap[:].bitcast(BF16)
                for kc in range(8):
                    kb.tr(ps, psb[:, kc * 128:(kc + 1) * 128], xsb, xsb[:, kc * 128:(kc + 1) * 128], ident_bf, ident_bf[:])
                a = 0 if ti < NTL else 2
                c0 = tokcol(ti)
                for kc in range(8):
                    if kc % 2 == 0:
                        kb.ts(nT, nT[:, kc, c0:c0 + 128], [ps, modcol], psb[:, kc * 128:(kc + 1) * 128],
                              modcol[:, a, kc:kc + 1], modcol[:, a + 1, kc:kc + 1], ALU.mult, ALU.add)
                    else:
                        kb.act(nT, nT[:, kc, c0:c0 + 128], [ps, modcol], psb[:, kc * 128:(kc + 1) * 128], AF.Identity,
                               scale=modcol[:, a, kc:kc + 1], bias=modcol[:, a + 1, kc:kc + 1])
            fw.barrier()

    def src0(ti):
        if ti < NTL:
            return x_d, x_d.ap[ti * 128:(ti + 1) * 128, :]
        return ctx_d, ctx_d.ap[(ti - NTL) * 128:(ti - NTL + 1) * 128, :]


    def hyena():
        PI = math.pi
        CW = 256
        with ExitStack() as les:
            sb = lambda shape, dt, name: fw.sbuf(shape, dt, "y_" + name, es=les)
            zT = sb([33, L], F32, "zT"); kb.load(zT, zT[:], zT_d, zT_d.ap)
            w1 = sb([33, 64], F32, "w1"); kb.load(w1, w1[:], hw1_d, hw1_d.ap)
            w2 = sb([64, 64], F32, "w2"); kb.load(w2, w2[:], hw2_d, hw2_d.ap)
            w3 = sb([64, 2048], F32, "w3"); kb.load(w3, w3[:], hw3_d, hw3_d.ap)
            hv = sb([64, 4], F32, "hv"); kb.load(hv, hv[:], hvec_d, hvec_d.ap)
            fb = sb([64, 2], F32, "fb")
            kb.tt(fb, fb[:, 0:1], [hv], hv[:, 0:1], hv[:, 1:2], ALU.mult); kb.tt(fb, fb[:, 1:2], [hv], hv[:, 2:3], hv[:, 3:4], ALU.mult)
            hid1 = sb([64, L], F32, "hid1"); hid2 = sb([64, L], F32, "hid2")
            ya = sb([64, 512], F32, "ya"); yb_ = sb([64, 512], F32, "yb")

            def sin_layer(dst, wmat, K_, srcb, fcol, fbcol):
                for g in range(4):
                    ps = kb.psum()
                    kb.mm(ps, ps[0:64, :], wmat, wmat[0:K_, :], srcb, srcb[0:K_, 512 * g:512 * (g + 1)])
                    kb.ts(ya, ya[:], [ps, hv, fb], ps[0:64, :], fcol, fbcol, ALU.mult, ALU.add)
                    for _ in range(2):
                        kb.ts(yb_, yb_[:], [ya], ya[:], -PI, 2 * PI, ALU.is_lt, ALU.mult)
                        kb.tt(yb_, yb_[:], [ya, yb_], ya[:], yb_[:], ALU.add)
                        kb.ts(ya, ya[:], [yb_], yb_[:], PI, -2 * PI, ALU.is_gt, ALU.mult)
                        kb.tt(ya, ya[:], [ya, yb_], ya[:], yb_[:], ALU.add)
                    kb.act(dst, dst[:, 512 * g:512 * (g + 1)], [ya], ya[:], AF.Sin)

            sin_layer(hid1, w1, 33, zT, hv[:, 1:2], fb[:, 0:1])
            sin_layer(hid2, w2, 64, hid1, hv[:, 3:4], fb[:, 1:2])
            kb.tap("y_hid2", hid2, hid2[:], [64, L])
            trow = sb([1, L], F32, "trow"); kb.load(trow, trow[:], trow_d, trow_d.ap)
            ndel = sb([1, 512], F32, "ndel"); kb.load(ndel, ndel[:], ndel_d, ndel_d.ap)
            wtc = sb([128, 17], F32, "wtc"); kb.load(wtc, wtc[:], wtc_d, wtc_d.ap)
            wtn = sb([128, 17], F32, "wtn"); kb.ts(wtn, wtn[:], [wtc], wtc[:], -1.0)
            hsum = sb([128, 16, CW], BF16, "hsum"); hdif = sb([128, 16, CW], BF16, "hdif")
            zz = [sb([128, 16, CW], BF16, f"zz{i}") for i in range(2)]
            Zre = sb([128, 17, CW], BF16, "Zre"); Zim = sb([128, 17, CW], BF16, "Zim")
            Gcs = [sb([128, 2, 17, 128], BF16, f"Gcs{i}") for i in range(2)]
            cwb = sb([128, 4, CW], F32, "cwb"); hbb = sb([128, 2, CW], F32, "hbb")
            wcv = [sb([128, 8, CW], BF16, f"wcv{i}") for i in range(4)]
            dec = sb([128, CW], F32, "dec"); hf = sb([128, CW], F32, "hf"); hb_ = sb([128, CW], F32, "hb")
            t_a = sb([128, CW], F32, "t_a"); t_b = sb([128, CW], F32, "t_b"); t_c = sb([128, CW], F32, "t_c")
            Kc_s = sb([128, CW], F32, "Kc_s"); Ks_s = sb([128, CW], F32, "Ks_s")
            ob = [sb([128, CW], BF16, f"ob{i}") for i in range(2)]
            ostg = [sb([128, 2, 128], BF16, f"ostg{i}") for i in range(2)]
            gn = 0

            def load_G(col_tile):
                nonlocal gn
                g_ = Gcs[gn % 2]
                gn += 1
                kb.load(g_, g_[:].rearrange("p t r c -> p (t r c)"), Gt_d, Gt_d.ap[col_tile])
                return View(g_, g_.ap[:, 0]), View(g_, g_.ap[:, 1])

            def conv_proj(wt, nt, widx, dst):
                c0 = tokcol(nt)
                for s_ in range(3):
                    ps = kb.psum()
                    for kc in range(8):
                        kb.mm(ps, ps[:, 0:CW], nT, nT[:, kc, c0 - 1 + s_:c0 - 1 + s_ + 128], wt, wt[:, kc, :], start=(kc == 0), stop=(kc == 7))
                    if s_ == 0:
                        kb.tt(dst, dst[:], [ps, cwb], ps[:, 0:CW], cwb[:, 0, :], ALU.mult)
                    else:
                        kb.tt(t_c, t_c[:], [ps, cwb], ps[:, 0:CW], cwb[:, s_, :], ALU.mult)
                        kb.tt(dst, dst[:], [dst, t_c], dst[:], t_c[:], ALU.add, eng=fw.pool)
                kb.tt(dst, dst[:], [dst, cwb], dst[:], cwb[:, 3, :], ALU.add, eng=fw.pool)

            for cg in range(512 // CW):
                ch0 = cg * CW
                for i_, base in enumerate((0, 512, 1024, 1536)):
                    kb.loadc(wcv[i_], wcv[i_][:], win1_d, rr(win1_d.ap[:, base + ch0:base + ch0 + CW]))
                for k_ in range(4):
                    pass
                kb.load(cwb, cwb[:], cw_d, bass.AP(cw_d.ap.tensor, ch0, [[0, 128], [1536, 4], [1, CW]]))
                for nt in range(NTL):
                    conv_proj(wcv[0], nt, 0, t_a)
                    kb.copy(zz[0], zz[0][:, nt, :], [t_a], t_a[:], eng=fw.act)
                for o in range(2):
                    zin, zout = zz[o % 2], zz[(o + 1) % 2]
                    kb.load(cwb, cwb[:], cw_d, bass.AP(cw_d.ap.tensor, 512 * (o + 1) + ch0, [[0, 128], [1536, 4], [1, CW]]))
                    kb.load(hbb, hbb[:], hbias_d, bass.AP(hbias_d.ap.tensor, ch0, [[0, 128], [512, 2], [1, CW]]))
                    for nt in range(NTL):
                        psd = kb.psum()
                        kb.mm(psd, psd[:, 0:CW], trow, trow[0:1, 128 * nt:128 * (nt + 1)], ndel, ndel[0:1, ch0:ch0 + CW])
                        kb.act(dec, dec[:], [psd], psd[:, 0:CW], AF.Exp)
                        psf = kb.psum(); psb_ = kb.psum()
                        kb.mm(psf, psf[:, 0:CW], hid2, hid2[:, 128 * nt:128 * (nt + 1)], w3, w3[:, (2 * o) * 512 + ch0:(2 * o) * 512 + ch0 + CW])
                        kb.mm(psb_, psb_[:, 0:CW], hid2, hid2[:, 128 * nt:128 * (nt + 1)], w3, w3[:, (2 * o + 1) * 512 + ch0:(2 * o + 1) * 512 + ch0 + CW])
                        kb.copy(hf, hf[:], [psf], psf[:, 0:CW], eng=fw.act)
                        kb.copy(hb_, hb_[:], [psb_], psb_[:, 0:CW])
                        if nt == 0:
                            kb.memset(hb_, hb_[0:1, :], 0.0)
                        kb.tt(t_a, t_a[:], [hf, hb_], hf[:], hb_[:], ALU.add, eng=fw.pool)
                        kb.tt(hsum, hsum[:, nt, :], [t_a, dec], t_a[:], dec[:], ALU.mult)
                        kb.tt(t_b, t_b[:], [hf, hb_], hb_[:], hf[:], ALU.subtract, eng=fw.pool)
                        kb.tt(hdif, hdif[:, nt, :], [t_b, dec], t_b[:], dec[:], ALU.mult)
                    for ft in range(17):
                        gc, gs = load_G(ft)
                        pUc = kb.psum(hold=True); pUs = kb.psum(hold=True); pKc = kb.psum(hold=True); pKs = kb.psum(hold=True)
                        for kc in range(16):
                            st, sp_ = (kc == 0), (kc == 15)
                            kb.mm(pUc, pUc[:, 0:CW], gc, gc[:, kc, :], zin, zin[:, kc, :], start=st, stop=sp_)
                            kb.mm(pKc, pKc[:, 0:CW], gc, gc[:, kc, :], hsum, hsum[:, kc, :], start=st, stop=sp_)
                            kb.mm(pUs, pUs[:, 0:CW], gs, gs[:, kc, :], zin, zin[:, kc, :], start=st, stop=sp_)
                            kb.mm(pKs, pKs[:, 0:CW], gs, gs[:, kc, :], hdif, hdif[:, kc, :], start=st, stop=sp_)
                        for p_ in (pUc, pUs, pKc, pKs):
                            kb.release(p_)
                        kb.copy(Kc_s, Kc_s[:], [pKc], pKc[:, 0:CW], eng=fw.act)
                        kb.copy(Ks_s, Ks_s[:], [pKs], pKs[:, 0:CW], eng=fw.act)
                        kb.tt(t_a, t_a[:], [pUc, Kc_s], pUc[:, 0:CW], Kc_s[:], ALU.mult)
                        kb.tt(t_b, t_b[:], [pUs, Ks_s], pUs[:, 0:CW], Ks_s[:], ALU.mult)
                        kb.tt(t_a, t_a[:], [t_a, t_b], t_a[:], t_b[:], ALU.add, eng=fw.pool)
                        kb.ts(Zre, Zre[:, ft, :], [t_a, wtc], t_a[:], wtc[:, ft:ft + 1], eng=fw.pool)
                        kb.tt(t_b, t_b[:], [pUs, Kc_s], pUs[:, 0:CW], Kc_s[:], ALU.mult)
                        kb.tt(t_c, t_c[:], [pUc, Ks_s], pUc[:, 0:CW], Ks_s[:], ALU.mult)
                        kb.tt(t_b, t_b[:], [t_b, t_c], t_b[:], t_c[:], ALU.subtract, eng=fw.pool)
                        kb.ts(Zim, Zim[:, ft, :], [t_b, wtc], t_b[:], wtc[:, ft:ft + 1], eng=fw.pool)
                    for nt in range(NTL):
                        gc, gs = load_G(nt)
                        py = kb.psum(hold=True)
                        for fc in range(17):
                            kb.mm(py, py[:, 0:CW], gc, gc[:, fc, :], Zre, Zre[:, fc, :], start=(fc == 0), stop=False)
                            kb.mm(py, py[:, 0:CW], gs, gs[:, fc, :], Zim, Zim[:, fc, :], start=False, stop=(fc == 16))
                        kb.release(py)
                        kb.tt(t_a, t_a[:], [zin, hbb], zin[:, nt, :], hbb[:, o, :], ALU.mult)
                        kb.tt(t_a, t_a[:], [t_a, py], t_a[:], py[:, 0:CW], ALU.add)
                        conv_proj(wcv[1 + o], nt, 1 + o, t_b)
                        if o == 0:
                            kb.tt(zout, zout[:, nt, :], [t_a, t_b], t_a[:], t_b[:], ALU.mult)
                        else:
                            kb.tt(t_a, t_a[:], [t_a, t_b], t_a[:], t_b[:], ALU.mult)
                            c0 = tokcol(nt)
                            ps = kb.psum()
                            for kc in range(8):
                                kb.mm(ps, ps[:, 0:CW], nT, nT[:, kc, c0:c0 + 128], wcv[3], wcv[3][:, kc, :], start=(kc == 0), stop=(kc == 7))
                            kb.act(t_b, t_b[:], [ps], ps[:, 0:CW], AF.Silu)
                            o_ = ob[nt % 2]
                            kb.tt(o_, o_[:], [t_a, t_b], t_a[:], t_b[:], ALU.mult)
                            pst = kb.psum(); pstb = pst.ap[:].bitcast(BF16)
                            for j in range(CW // 128):
                                kb.tr(pst, pstb[:, 128 * j:128 * (j + 1)], o_, o_[:, 128 * j:128 * (j + 1)], ident_bf, ident_bf[:])
                            stg = ostg[nt % 2]
                            kb.copy(stg, stg[:].rearrange("p a b -> p (a b)"), [pst], pstb[:, 0:CW], eng=fw.act)
                            kb.load(oT, oT.ap[:, cg * (CW // 128):(cg + 1) * (CW // 128), 128 * nt:128 * (nt + 1)], stg, stg[:])
            fw.barrier()

    SKIP_L0 = os.environ.get('SKIP_L0') == '1'
    if SKIP_L0:
        h1in_d = I('h1in', [L + CL, D])
    if not SKIP_L0:
        modcol0, gateb0 = layer_mod(0)
    if not SKIP_L0:
        kb.tap("modcol0", modcol0, modcol0[:], [128, 4, 8])
        kb.tap("gateb0", gateb0, gateb0[:], [128, 2, D])
    if not SKIP_L0:
        layer_norm(0, src0, modcol0)
    kb.tap("nT0", nT, nT[:], [128, 8, NTC], BF16)
    if stop == "norm0":
        return finish(kb, es, y_d)

    def proj_fm(ps, M, w, wcols, c0, n):
        for kc in range(8):
            kb.mm(ps, ps[0:M, 0:n], w, w[:, kc, wcols], nT, nT[:, kc, c0:c0 + n], start=(kc == 0), stop=(kc == 7))

    def proj_tm(ps, c0, w, wcols, N):
        for kc in range(8):
            kb.mm(ps, ps[:, 0:N], nT, nT[:, kc, c0:c0 + 128], w, w[:, kc, wcols], start=(kc == 0), stop=(kc == 7))

    def qcol(tile):
        return tile * 128

    def retention():
        with ExitStack() as les:
            sb = lambda shape, dt, name: fw.sbuf(shape, dt, "r_" + name, es=les)
            cs_fm = sb([64, L], F32, "cs_fm"); kb.load(cs_fm, cs_fm[:], csq_d, csq_d.ap)
            sn_fm = sb([64, L], F32, "sn_fm"); kb.load(sn_fm, sn_fm[:], snq_d, snq_d.ap)
            cs_tm = sb([128, 16, 64], F32, "cs_tm"); kb.load(cs_tm, cs_tm[:], cstm_d, rr(cstm_d.ap))
            sn_tm = sb([128, 16, 64], F32, "sn_tm"); kb.load(sn_tm, sn_tm[:], sntm_d, rr(sntm_d.ap))
            colAB = sb([128, 2], F32, "colAB"); kb.load(colAB, colAB[:], colAB_d, colAB_d.ap)
            iota12 = sb([64, 256], F32, "iota12"); kb.load(iota12, iota12[:], iota12_d, iota12_d.ap)
            dmask = sb([128, 512], F32, "dmask"); kb.load(dmask, dmask[:], dmask_d, dmask_d.ap)
            lg0 = sb([128, 16], F32, "lg0"); kb.load(lg0, lg0[:], retdec_d, retdec_d.ap.partition_broadcast(128))
            LGc = sb([128, 16], F32, "LGc")
            kb.act(lg0, lg0[:], [lg0], lg0[:], AF.Exp)
            kb.ts(LGc, LGc[:], [lg0], lg0[:], -1.0)
            wnames = ["wq", "wqr", "wk", "wkr", "wv", "wz"]
            W = [{n: sb([128, 8, 64], BF16, f"{n}{i}") for n in wnames} for i in range(2)]
            qT = sb([64, L + CL], BF16, "qT"); kT = sb([64, L + CL], BF16, "kT")
            qdf = sb([64, L + CL], BF16, "qdf"); qdb = sb([64, L + CL], BF16, "qdb")
            Kdf = sb([128, NTT, 64], BF16, "Kdf"); Kdb = sb([128, NTT, 64], BF16, "Kdb"); Vtm = sb([128, NTT, 64], BF16, "Vtm")
            Saf = sb([64, NTT, 64], BF16, "Saf"); Sab = sb([64, NTT, 64], BF16, "Sab")
            Sf = sb([64, 64], F32, "Sf"); Sb_ = sb([64, 64], F32, "Sb")
            dec = sb([128, 4], F32, "dec"); qdec = sb([64, 256], F32, "qdec"); WtT = sb([128, 128], F32, "WtT"); wtmp = sb([128, 256], F32, "wtmp")
            t1 = [sb([64, 512], F32, f"t1_{i}") for i in range(2)]; t2 = [sb([64, 512], F32, f"t2_{i}") for i in range(2)]
            ktmp = [sb([128, 64], F32, f"ktmp{i}") for i in range(2)]; ktmp2 = [sb([128, 64], F32, f"ktmpb{i}") for i in range(2)]
            PT = [sb([128, 128], BF16, f"PT{i}") for i in range(2)]
            ostg = [sb([64, 512], BF16, f"ostg{i}") for i in range(2)]
            sq = sb([64, 512], F32, "sq"); sd = sb([64, 512], F32, "sd"); sz = sb([64, 512], F32, "sz"); yv = sb([64, 512], F32, "yv")
            for h in range(8):
                w = W[h % 2]
                for n, (srcd, c0) in zip(wnames, [(win0_d, E_RQ + 64 * h), (wrot0_d, 64 * h), (win0_d, E_RK + 64 * h),
                                                  (wrot0_d, 512 + 64 * h), (win0_d, E_RV + 64 * h), (win0_d, E_RZ + 64 * h)]):
                    kb.loadc(w[n], w[n][:], srcd, rr(srcd.ap[:, c0:c0 + 64]))
                lf = LGc[:, h:h + 1]; lb = LGc[:, 8 + h:9 + h]
                kb.act(dec, dec[:, 0:1], [colAB, LGc], colAB[:, 0:1], AF.Exp, scale=lf)
                kb.act(dec, dec[:, 1:2], [colAB, LGc], colAB[:, 1:2], AF.Exp, scale=lb)
                kb.act(dec, dec[0:64, 2:3], [LGc], LGc[0:64, h:h + 1], AF.Exp, scale=128.0)
                kb.act(dec, dec[0:64, 3:4], [LGc], LGc[0:64, 8 + h:9 + h], AF.Exp, scale=128.0)
                kb.act(qdec, qdec[:, 0:128], [iota12, LGc], iota12[:, 0:128], AF.Exp, scale=LGc[0:64, h:h + 1])
                kb.act(qdec, qdec[:, 128:256], [iota12, LGc], iota12[:, 128:256], AF.Exp, scale=LGc[0:64, 8 + h:9 + h])
                kb.act(wtmp, wtmp[:, 0:128], [dmask, LGc], dmask[:, 0:128], AF.Exp, scale=lf)
                kb.act(wtmp, wtmp[:, 128:256], [dmask, LGc], dmask[:, 128:256], AF.Exp, scale=lb)
                kb.tt(wtmp, wtmp[:], [wtmp, dmask], wtmp[:], dmask[:, 256:512], ALU.mult)
                kb.tt(WtT, WtT[:], [wtmp], wtmp[:, 0:128], wtmp[:, 128:256], ALU.add)
                for which, dst, wa, wb_, sc in (("q", qT, "wq", "wqr", 1.0), ("k", kT, "wk", "wkr", 0.125)):
                    for g in range(4):
                        psA = kb.psum(); proj_fm(psA, 64, w[wa], slice(0, 64), LOFF + 512 * g, 512)
                        psB = kb.psum(); proj_fm(psB, 64, w[wb_], slice(0, 64), LOFF + 512 * g, 512)
                        a = t1[g % 2]; b_ = t2[g % 2]
                        kb.stt(a, a[:], [psA, cs_fm], psA[0:64, :], sc, cs_fm[:, 512 * g:512 * (g + 1)], ALU.mult, ALU.mult)
                        kb.stt(b_, b_[:], [psB, sn_fm], psB[0:64, :], sc, sn_fm[:, 512 * g:512 * (g + 1)], ALU.mult, ALU.mult)
                        kb.tt(dst, dst[:, 512 * g:512 * (g + 1)], [a, b_], a[:], b_[:], ALU.add, eng=fw.pool)
                    psA = kb.psum(); proj_fm(psA, 64, w[wa], slice(0, 64), COFF, 256)
                    kb.act(dst, dst[:, L:L + CL], [psA], psA[0:64, 0:256], AF.Identity, scale=sc)
                kb.tt(qdf, qdf[:].rearrange("p (t c) -> p t c", c=128), [qT, qdec], qT[:].rearrange("p (t c) -> p t c", c=128),
                      qdec[:, 0:128].unsqueeze(1).broadcast_to([64, NTT, 128]), ALU.mult)
                kb.tt(qdb, qdb[:].rearrange("p (t c) -> p t c", c=128), [qT, qdec], qT[:].rearrange("p (t c) -> p t c", c=128),
                      qdec[:, 128:256].unsqueeze(1).broadcast_to([64, NTT, 128]), ALU.mult, eng=fw.pool)
                for ti in range(NTT):
                    c0 = tokcol(ti)
                    psA = kb.psum(); proj_tm(psA, c0, w["wk"], slice(0, 64), 64)
                    ka = ktmp[ti % 2]
                    if ti < NTL:
                        psB = kb.psum(); proj_tm(psB, c0, w["wkr"], slice(0, 64), 64)
                        kb_ = ktmp2[ti % 2]
                        kb.stt(ka, ka[:], [psA, cs_tm], psA[:, 0:64], 0.125, cs_tm[:, ti, :], ALU.mult, ALU.mult)
                        kb.stt(kb_, kb_[:], [psB, sn_tm], psB[:, 0:64], 0.125, sn_tm[:, ti, :], ALU.mult, ALU.mult)
                        kb.tt(ka, ka[:], [ka, kb_], ka[:], kb_[:], ALU.add, eng=fw.pool)
                    else:
                        kb.act(ka, ka[:], [psA], psA[:, 0:64], AF.Identity, scale=0.125)
                    kb.ts(Kdf, Kdf[:, ti, :], [ka, dec], ka[:], dec[:, 0:1], eng=fw.pool)
                    kb.ts(Kdb, Kdb[:, ti, :], [ka, dec], ka[:], dec[:, 1:2], eng=fw.pool)
                    psV = kb.psum(); proj_tm(psV, c0, w["wv"], slice(0, 64), 64)
                    kb.copy(Vtm, Vtm[:, ti, :], [psV], psV[:, 0:64], eng=fw.act)
                for S, Sall, Kd, gcol, order in ((Sf, Saf, Kdf, 2, [16, 17] + list(range(16))), (Sb_, Sab, Kdb, 3, [17, 16] + list(range(15, -1, -1)))):
                    kb.memset(S, S[:], 0.0)
                    for i, ti in enumerate(order):
                        kb.copy(Sall, Sall[:, ti, :], [S], S[:], eng=fw.pool)
                        if i == len(order) - 1:
                            break
                        ps = kb.psum()
                        kb.mm(ps, ps[0:64, 0:64], Kd, Kd[:, ti, :], Vtm, Vtm[:, ti, :])
                        kb.stt(S, S[:], [S, dec, ps], S[:], dec[0:64, gcol:gcol + 1], ps[0:64, 0:64], ALU.mult, ALU.add)
                groups = [list(range(4 * g, 4 * g + 4)) for g in range(4)] + [[16, 17]]
                for gi, tiles in enumerate(groups):
                    psY = kb.psum()
                    for j, ti in enumerate(tiles):
                        tc = qcol(ti)
                        psS = kb.psum()
                        kb.mm(psS, psS[:, 0:128], kT, kT[:, tc:tc + 128], qT, qT[:, tc:tc + 128])
                        pt = PT[j % 2]
                        kb.tt(pt, pt[:], [psS, WtT], psS[:, 0:128], WtT[:], ALU.mult)
                        yo = psY[0:64, 128 * j:128 * (j + 1)]
                        kb.mm(psY, yo, Vtm, Vtm[:, ti, :], pt, pt[:], start=True, stop=False)
                        kb.mm(psY, yo, Saf, Saf[:, ti, :], qdf, qdf[:, tc:tc + 128], start=False, stop=False)
                        kb.mm(psY, yo, Sab, Sab[:, ti, :], qdb, qdb[:, tc:tc + 128], start=False, stop=True)
                    n = 128 * len(tiles)
                    tc0 = qcol(tiles[0])
                    nc0 = tokcol(tiles[0])
                    kb.act(sq, sq[:, 0:n], [psY], psY[0:64, 0:n], AF.Square)
                    psN = kb.psum()
                    kb.mm(psN, psN[0:64, 0:n], ones_f, ones_f[0:64, 0:64], sq, sq[:, 0:n])
                    kb.act(sd, sd[:, 0:n], [psN], psN[0:64, 0:n], AF.Sqrt, scale=1.0 / 64, bias=EPS)
                    kb.recip(sd, sd[:, 0:n], [sd], sd[:, 0:n])
                    psZ = kb.psum(); proj_fm(psZ, 64, w["wz"], slice(0, 64), nc0, n)
                    kb.act(sz, sz[:, 0:n], [psZ], psZ[0:64, 0:n], AF.Silu)
                    kb.tt(yv, yv[:, 0:n], [psY, sd], psY[0:64, 0:n], sd[:, 0:n], ALU.mult)
                    po = (h % 2) * 64
                    stg = ostg[gi % 2]
                    kb.tt(stg, stg[:, 0:n], [yv, sz], yv[:, 0:n], sz[:, 0:n], ALU.mult, eng=fw.pool)
                    kb.load(oT, oT.ap[po:po + 64, h // 2, tc0:tc0 + n], stg, stg[:, 0:n])
                if h == 0:
                    kb.tap("r_wq", w["wq"], w["wq"][:], [128, 8, 64], BF16); kb.tap("r_nT", nT, nT[:], [128, 8, NTC], BF16)
                    kb.tap("r_LGc", LGc, LGc[:], [128, 16]); kb.tap("r_dec", dec, dec[:], [128, 4]); kb.tap("r_qdec", qdec, qdec[:], [64, 256])
                    kb.tap("r_WtT", WtT, WtT[:], [128, 128]); kb.tap("r_qT", qT, qT[:], [64, L + CL], BF16); kb.tap("r_kT", kT, kT[:], [64, L + CL], BF16)
                    kb.tap("r_Kdf", Kdf, Kdf[:], [128, NTT, 64], BF16); kb.tap("r_Vtm", Vtm, Vtm[:], [128, NTT, 64], BF16)
                    kb.tap("r_Saf", Saf, Saf[:], [64, NTT, 64], BF16); kb.tap("r_Sab", Sab, Sab[:], [64, NTT, 64], BF16)
                    kb.tap("r_sd", sd, sd[:], [64, 512]); kb.tap("r_sz", sz, sz[:], [64, 512]); kb.tap("r_yv", yv, yv[:], [64, 512])
            fw.barrier()


    def rwkv():
        with ExitStack() as les:
            sb = lambda shape, dt, name: fw.sbuf(shape, dt, "w_" + name, es=les)
            NC_ = NTC
            muT = sb([64, 28], F32, "muT"); kb.load(muT, muT[:], muT_d, muT_d.ap)
            mu128 = sb([128, 2], F32, "mu128"); kb.load(mu128, mu128[:], mu128_d, mu128_d.ap)
            om = sb([64, 28], F32, "om"); hm = sb([64, 28], F32, "hm")
            kb.ts(om, om[:], [muT], muT[:], -1.0, 1.0, ALU.mult, ALU.add); kb.ts(hm, hm[:], [muT], muT[:], 0.5)
            om128 = sb([128, 2], F32, "om128"); hm128 = sb([128, 2], F32, "hm128")
            kb.ts(om128, om128[:], [mu128], mu128[:], -1.0, 1.0, ALU.mult, ALU.add); kb.ts(hm128, hm128[:], [mu128], mu128[:], 0.5)
            P5 = sb([64, 9, 8], F32, "P5"); kb.load(P5, P5[:], p512_d, p512_d.ap)
            omka = sb([64, 8], F32, "omka"); kb.ts(omka, omka[:], [P5], P5[:, 1, :], -1.0, 1.0, ALU.mult, ALU.add)
            LW = sb([128, 2, 512], BF16, "LW"); kb.loadc(LW, LW[:], lw_d, lw_d.ap)
            smask = sb([64, NC_], BF16, "smask"); kb.loadc(smask, smask[:], scanmask_d, scanmask_d.ap)
            m4 = sb([128, 512], F32, "m4"); kb.load(m4, m4[:], masks4_d, masks4_d.ap)
            m4n = sb([128, 512], F32, "m4n"); kb.ts(m4n, m4n[:], [m4], m4[:], -1.0)
            ones64 = sb([64, 64], F32, "ones64"); kb.memset(ones64, ones64[:], 1.0 / 64)
            Upad = sb([128, NC_], F32, "Upad"); kb.memset(Upad, Upad[:], 0.0)
            tmpw = sb([128, NC_], F32, "tmpw"); kb.memset(tmpw, tmpw[:], 0.0)
            wdT = sb([128, NC_], BF16, "wdT"); adT = sb([128, NC_], BF16, "adT")
            kb.memset(wdT, wdT[:], 0.0); kb.memset(adT, adT[:], 0.0)
            wsh = [sb([128, 8, 128], BF16, f"wsh{i}") for i in range(1)] * 2
            groups = [(LOFF + 512 * g, 512) for g in range(4)] + [(COFF, 256)]

            def shift_proj(w, wcols, M, omc, hmc, out, func=None):
                for a_, b_ in ((0, 1), (2049, 2050), (2306, 2308)):
                    kb.memset(Upad, Upad[0:M, a_:b_], 0.0)
                for c0, n in groups:
                    ps = kb.psum(); proj_fm(ps, M, w, wcols, c0, n)
                    kb.copy(Upad, Upad[0:M, c0:c0 + n], [ps], ps[0:M, 0:n], eng=fw.act)
                kb.tt(tmpw, tmpw[0:M, 0:NC_ - 2], [Upad], Upad[0:M, 0:NC_ - 2], Upad[0:M, 2:NC_], ALU.add, eng=fw.pool)
                kb.ts(Upad, Upad[0:M, 1:NC_ - 1], [Upad, om, om128], Upad[0:M, 1:NC_ - 1], omc)
                if func is None:
                    kb.stt(out, out[0:M, 1:NC_ - 1], [tmpw, Upad, hm, hm128], tmpw[0:M, 0:NC_ - 2], hmc, Upad[0:M, 1:NC_ - 1], ALU.mult, ALU.add)
                else:
                    kb.stt(tmpw, tmpw[0:M, 0:NC_ - 2], [tmpw, Upad, hm, hm128], tmpw[0:M, 0:NC_ - 2], hmc, Upad[0:M, 1:NC_ - 1], ALU.mult, ALU.add)
                    kb.act(out, out[0:M, 1:NC_ - 1], [tmpw], tmpw[0:M, 0:NC_ - 2], func)

            for j, (dst, func) in enumerate(((wdT, AF.Tanh), (adT, None))):
                w = wsh[j]
                c0 = E_SHIFT + 1536 + 128 * j
                kb.loadc(w, w[:], win0_d, rr(win0_d.ap[:, c0:c0 + 128]))
                shift_proj(w, slice(0, 128), 128, om128[:, j:j + 1], hm128[:, j:j + 1], dst, func)
            kb.tap("w_wdT", wdT, wdT[:], [128, NC_], BF16); kb.tap("w_adT", adT, adT[:], [128, NC_], BF16)
            if stop == "rwkv_a":
                fw.barrier(); return

            wh = [{n: sb([128, 8, 64], BF16, f"{n}{i}") for n in ("r", "k", "v", "z")} for i in range(1)] * 2
            bfa = lambda name: sb([64, NC_], BF16, name)
            f32a = lambda name: sb([64, NC_], F32, name)
            rS, kS, vS, kkn, aD, kd, bD, rks = [bfa(n) for n in ("rS", "kS", "vS", "kkn", "aD", "kd", "bD", "rks")]
            Bi, Ki, KKd, Rd = [bfa(n) for n in ("Bi", "Ki", "KKd", "Rd")]
            BeT, KeT = Bi, Ki
            lw_, YT = [f32a(n) for n in ("lw", "YT")]
            ex = View(Upad, Upad.ap[0:64, :])
            cum = View(tmpw, tmpw.ap[0:64, :])
            for t_ in (rS, kS, vS, kkn, aD, kd, bD, rks, Bi, Ki, KKd, Rd, lw_, YT):
                kb.memset(t_, t_[:], 0.0)
            tot = sb([64, NTT], F32, "tot"); WC = sb([64, NTT], F32, "WC")
            Ak = sb([128, NTT, 128], BF16, "Ak"); Gb = sb([128, NTT, 128], BF16, "Gb"); Gk = sb([128, NTT, 128], BF16, "Gk")
            Tall = sb([128, NTT, 128], BF16, "Tall")
            Vt = sb([128, NTT, 64], BF16, "Vt"); Be = sb([128, NTT, 64], BF16, "Be"); Ke = sb([128, NTT, 64], BF16, "Ke")
            NDT = F32
            NP = [sb([128, 512], NDT, f"NP{i}") for i in range(2)]; NPT = [sb([128, 512], NDT, f"NPT{i}") for i in range(2)]
            Tn = sb([128, 512], NDT, "Tn")
            S0 = sb([64, 64], F32, "S0"); S0b = sb([64, 64], BF16, "S0b"); Stmp = sb([64, 64], F32, "Stmp")
            PTs = [sb([128, 64], BF16, f"PTs{i}") for i in range(2)]; UT = [sb([128, 64], BF16, f"UT{i}") for i in range(2)]
            ostg = [sb([64, 512], BF16, f"ostg{i}") for i in range(2)]
            g1 = sb([64, 512], F32, "g1"); g2 = sb([64, 512], F32, "g2"); g3 = sb([64, 512], F32, "g3")
            lat3 = lambda a: a[:, LOFF:LOFF + L].rearrange("p (t c) -> p t c", c=128)
            ctx3 = lambda a: a[:, COFF:COFF + CL].rearrange("p (t c) -> p t c", c=128)
            for h in range(8):
                w = wh[h % 2]
                for n, c0 in (("r", E_SHIFT + 64 * h), ("k", E_SHIFT + 512 + 64 * h), ("v", E_SHIFT + 1024 + 64 * h), ("z", E_WZ + 64 * h)):
                    kb.loadc(w[n], w[n][:], win0_d, rr(win0_d.ap[:, c0:c0 + 64]))
                pc = lambda v: P5[:, v, h:h + 1]
                shift_proj(w["r"], slice(0, 64), 64, om[:, h:h + 1], hm[:, h:h + 1], rS)
                shift_proj(w["k"], slice(0, 64), 64, om[:, 8 + h:9 + h], hm[:, 8 + h:9 + h], kS)
                shift_proj(w["v"], slice(0, 64), 64, om[:, 16 + h:17 + h], hm[:, 16 + h:17 + h], vS)
                kb.ts(ex, ex[:], [kS, P5], kS[:], pc(0))
                for c0, n in groups:
                    kb.act(g1, g1[:, 0:n], [ex], ex[:, c0:c0 + n], AF.Square)
                    ps = kb.psum(); kb.mm(ps, ps[0:64, 0:n], ones64, ones64[:], g1, g1[:, 0:n])
                    kb.act(g2, g2[:, 0:n], [ps], ps[0:64, 0:n], AF.Sqrt, scale=64.0)
                    kb.ts(g2, g2[:, 0:n], [g2], g2[:, 0:n], 1e-12, None, ALU.max)
                    kb.recip(g2, g2[:, 0:n], [g2], g2[:, 0:n])
                    kb.tt(kkn, kkn[:, c0:c0 + n], [ex, g2], ex[:, c0:c0 + n], g2[:, 0:n], ALU.mult)
                kb.memset(rks, rks[:], 0.0)
                kb.memset(YT, YT[:], 0.0)
                for d in range(2):
                    po = 64 * d
                    for c0, n in groups:
                        ps = kb.psum(); kb.mm(ps, ps[0:64, 0:n], LW, LW[po:po + 64, 1, 64 * h:64 * h + 64], adT, adT[po:po + 64, c0:c0 + n])
                        kb.act(aD, aD[:, c0:c0 + n], [ps, P5], ps[0:64, 0:n], AF.Sigmoid, bias=pc(7 + d))
                        ps = kb.psum(); kb.mm(ps, ps[0:64, 0:n], LW, LW[po:po + 64, 0, 64 * h:64 * h + 64], wdT, wdT[po:po + 64, c0:c0 + n])
                        kb.act(lw_, lw_[:, c0:c0 + n], [ps, P5], ps[0:64, 0:n], AF.Sigmoid, bias=pc(5 + d))
                    kb.ts(lw_, lw_[:], [lw_], lw_[:], -math.exp(-0.5))
                    kb.ts(ex, ex[:], [aD, P5, omka], aD[:], pc(1), omka[:, h:h + 1], ALU.mult, ALU.add)
                    kb.tt(kd, kd[:], [ex, kS], ex[:], kS[:], ALU.mult)
                    kb.tt(bD, bD[:], [kkn, aD], kkn[:], aD[:], ALU.mult, eng=fw.pool)
                    kb.stt(ex, ex[:], [rS, P5, kd], rS[:], pc(2), kd[:], ALU.mult, ALU.mult)
                    kb.tt(rks, rks[:], [rks, ex], rks[:], ex[:], ALU.add, eng=fw.pool)
                    fw.op(fw.dve, lambda: nc.vector.tensor_tensor_scan(out=cum[:], data0=smask[:], data1=lw_[:], initial=0.0, op0=ALU.mult, op1=ALU.add),
                          reads=[smask, lw_], writes=[cum])
                    kb.copy(tot, tot[:, 0:NTL].unsqueeze(2), [cum], lat3(cum)[:, :, 127:128])
                    kb.copy(tot, tot[:, NTL:NTT].unsqueeze(2), [cum], ctx3(cum)[:, :, 127:128])
                    kb.act(WC, WC[:], [tot], tot[:], AF.Exp)
                    if d == 0:
                        kb.act(ex, ex[:], [cum], cum[:], AF.Exp)
                        kb.tt(Rd, Rd[:], [rS, ex], rS[:], ex[:], ALU.mult)
                        kb.tt(ex, ex[:], [cum, lw_], cum[:], lw_[:], ALU.subtract)
                        kb.act(ex, ex[:], [ex], ex[:], AF.Exp)
                        kb.tt(KKd, KKd[:], [kkn, ex], kkn[:], ex[:], ALU.mult)
                    else:
                        for v3, ts_ in ((lat3, slice(0, NTL)), (ctx3, slice(NTL, NTT))):
                            nt_ = ts_.stop - ts_.start
                            kb.tt(cum, v3(cum), [cum, tot], tot[:, ts_].unsqueeze(2).broadcast_to([64, nt_, 128]), v3(cum), ALU.subtract)
                        kb.act(ex, ex[:], [cum], cum[:], AF.Exp)
                        kb.tt(KKd, KKd[:], [kkn, ex], kkn[:], ex[:], ALU.mult)
                        kb.tt(cum, cum[:], [cum, lw_], cum[:], lw_[:], ALU.add)
                        kb.act(ex, ex[:], [cum], cum[:], AF.Exp)
                        kb.tt(Rd, Rd[:], [rS, ex], rS[:], ex[:], ALU.mult)
                    kb.act(ex, ex[:], [cum], cum[:], AF.Exp, scale=-1.0)
                    kb.tt(Bi, Bi[:], [bD, ex], bD[:], ex[:], ALU.mult)
                    kb.tt(Ki, Ki[:], [kd, ex], kd[:], ex[:], ALU.mult, eng=fw.pool)
                    if h == 0 and d == 0:
                        for nm, t_ in (("rS", rS), ("kkn", kkn), ("aD", aD), ("kd", kd), ("KKd", KKd), ("Rd", Rd), ("Bi", Bi), ("Ki", Ki)):
                            kb.tap("w_" + nm, t_, t_[:], [64, NC_], BF16)
                        kb.tap("w_lw", lw_, lw_[:], [64, NC_]); kb.tap("w_cum", cum, cum[:], [64, NC_]); kb.tap("w_WC", WC, WC[:], [64, NTT])
                        if stop == "rwkv_b":
                            fw.barrier(); return
                    ms, mi, msT = (m4[:, 0:128], m4[:, 128:256], m4[:, 256:384]) if d == 0 else (m4[:, 256:384], m4[:, 384:512], m4[:, 0:128])
                    nms, nmsT = (m4n[:, 0:128], m4n[:, 256:384]) if d == 0 else (m4n[:, 256:384], m4n[:, 0:128])
                    X = aD
                    srcs = ([(vS, Vt)] if d == 0 else []) + [(BeT, Be), (KeT, Ke)]
                    ncp = 0
                    for sbuf_, dst_ in srcs:
                        for c_lo, n_ in ((LOFF, L), (COFF, CL)):
                            fw.op(fw.dve, (lambda s_=sbuf_, c_lo=c_lo, n_=n_: nc.vector.transpose(out=X[:, c_lo:c_lo + n_], in_=s_[:, c_lo:c_lo + n_])),
                                  reads=[sbuf_], writes=[X])
                        for c_lo, n_, t0 in ((LOFF, L, 0), (COFF, CL, NTL)):
                            nt_ = n_ // 128
                            for pi in range(2):
                                xv = X[32 * pi:32 * pi + 32, c_lo:c_lo + n_].rearrange("p (t f c) -> p t f c", f=4, c=32)
                                for fj in range(4):
                                    eng_ = fw.pool if ncp % 2 == 0 else fw.act
                                    ncp += 1
                                    kb.copy(dst_, dst_[32 * fj:32 * fj + 32, t0:t0 + nt_, 32 * pi:32 * pi + 32], [X], xv[:, :, fj, :], eng=eng_)
                    BS_ = int(os.environ.get('RW_BS', '4'))
                    for b0 in ([] if os.environ.get('RW_SKIP_INTRA') == '1' else range(0, NTT, BS_)):
                        tiles = list(range(b0, min(b0 + BS_, NTT)))
                        nb = len(tiles); W_ = 128 * nb
                        pA = kb.psum(); pAT = kb.psum(); pK = kb.psum(); pGb = kb.psum(); pGk = kb.psum()
                        for j, ti in enumerate(tiles):
                            c0 = tokcol(ti); cs_ = slice(128 * j, 128 * j + 128)
                            kb.mm(pA, pA[:, cs_], Bi, Bi[:, c0:c0 + 128], KKd, KKd[:, c0:c0 + 128])
                            kb.mm(pAT, pAT[:, cs_], KKd, KKd[:, c0:c0 + 128], Bi, Bi[:, c0:c0 + 128])
                            kb.mm(pK, pK[:, cs_], Ki, Ki[:, c0:c0 + 128], KKd, KKd[:, c0:c0 + 128])
                            kb.mm(pGb, pGb[:, cs_], Bi, Bi[:, c0:c0 + 128], Rd, Rd[:, c0:c0 + 128])
                            kb.mm(pGk, pGk[:, cs_], Ki, Ki[:, c0:c0 + 128], Rd, Rd[:, c0:c0 + 128])
                        v3 = lambda a, n_=nb: a.rearrange("p (t c) -> p t c", c=128)
                        bc = lambda m_, n_=nb: m_.unsqueeze(1).broadcast_to([128, n_, 128])
                        P_, PT_ = NP[0], NPT[0]
                        kb.tt(P_, v3(P_[:, 0:W_]), [pA, m4n], v3(pA[:, 0:W_]), bc(nms), ALU.mult)
                        kb.tt(PT_, v3(PT_[:, 0:W_]), [pAT, m4n], v3(pAT[:, 0:W_]), bc(nmsT), ALU.mult)
                        kb.tt(Ak, Ak[:, b0:b0 + nb, :], [pK, m4], v3(pK[:, 0:W_]), bc(ms), ALU.mult)
                        kb.tt(Gb, Gb[:, b0:b0 + nb, :], [pGb, m4], v3(pGb[:, 0:W_]), bc(mi), ALU.mult)
                        kb.tt(Gk, Gk[:, b0:b0 + nb, :], [pGk, m4], v3(pGk[:, 0:W_]), bc(mi), ALU.mult)
                        if os.environ.get('RW_NOT') != '1':
                            kb.tt(Tn, v3(Tn[:, 0:W_]), [P_, ident_f], v3(P_[:, 0:W_]), bc(ident_f[:]), ALU.add, eng=(fw.dve if os.environ.get('RW_TN_DVE') == '1' else fw.pool))
                        cur = 0
                        for lvl in range(1, 1 + (int(os.environ.get('RW_LVLS', '6')) if (b0 // BS_ < int(os.environ.get('RW_LV_BATCHES', '9')) and b0 // BS_ >= int(os.environ.get('RW_LV_FROM', '0'))) else 0)):
                            P_, PT_ = NP[cur], NPT[cur]; Pn, PTn = NP[1 - cur], NPT[1 - cur]
                            if os.environ.get("RW_P3FIRST") == "1":
                                p3 = kb.psum(); p1 = kb.psum(); p2 = kb.psum()
                            else:
                                p1 = kb.psum(); p2 = kb.psum(); p3 = kb.psum()
                            for j in range(nb):
                                cs_ = slice(128 * j, 128 * j + 128)
                                kb.mm(p2, p2[:, cs_], P_, P_[:, cs_], PT_, PT_[:, cs_])
                                if lvl < 6:
                                    kb.mm(p1, p1[:, cs_], PT_, PT_[:, cs_], P_, P_[:, cs_])
                            kb.copy(PTn, PTn[:, 0:W_], [p2], p2[:, 0:W_], eng=fw.act)
                            if lvl < 6:
                                kb.copy(Pn, Pn[:, 0:W_], [p1], p1[:, 0:W_])
                            if os.environ.get("RW_LV_MODE") == "sq":
                                cur = 1 - cur
                                continue
                            for j in range(nb):
                                cs_ = slice(128 * j, 128 * j + 128)
                                if os.environ.get("RW_P3SRC") == "old":
                                    kb.mm(p3, p3[:, cs_], PT_, PT_[:, cs_], Tn, Tn[:, cs_])
                                elif os.environ.get("RW_P3SRC") == "rhsP":
                                    kb.mm(p3, p3[:, cs_], PTn, PTn[:, cs_], P_, P_[:, cs_])
                                else:
                                    kb.mm(p3, p3[:, cs_], PTn, PTn[:, cs_], Tn, Tn[:, cs_])
                            if os.environ.get("RW_LV_MODE") != "nott":
                                kb.tt(Tn, Tn[:, 0:W_], [Tn, p3], Tn[:, 0:W_], p3[:, 0:W_], ALU.add)
                            cur = 1 - cur
                        kb.copy(Tall, Tall[:, b0:b0 + nb, :], [Tn], v3(Tn[:, 0:W_]), eng=fw.pool)
                    if os.environ.get('RW_BAR') == '1':
                        fw.barrier()
                    if stop == "rwkv_c":
                        kb.tap("w_Tall", Tall, Tall[:], [128, NTT, 128], BF16); kb.tap("w_Ak", Ak, Ak[:], [128, NTT, 128], BF16)
                        fw.barrier(); return
                    order = ([16, 17] + list(range(16))) if d == 0 else ([17, 16] + list(range(15, -1, -1)))
                    kb.memset(S0, S0[:], 0.0); kb.memset(S0b, S0b[:], 0.0)
                    for i, ti in enumerate(order):
                        c0 = tokcol(ti)
                        pts = PTs[i % 2]; ut = UT[i % 2]
                        ps = kb.psum()
                        kb.mm(ps, ps[:, 0:64], KKd, KKd[:, c0:c0 + 128], S0b, S0b[:], start=True, stop=False)
                        kb.mm(ps, ps[:, 0:64], Ak, Ak[:, ti, :], Vt, Vt[:, ti, :], start=False, stop=True)
                        kb.copy(pts, pts[:], [ps], ps[:, 0:64], eng=fw.act)
                        ps2 = kb.psum()
                        kb.mm(ps2, ps2[:, 0:64], Tall, Tall[:, ti, :], pts, pts[:])
                        kb.ts(ut, ut[:], [ps2], ps2[:, 0:64], -1.0)
                        py = kb.psum()
                        kb.mm(py, py[0:64, 0:128], S0b, S0b[:], Rd, Rd[:, c0:c0 + 128], start=True, stop=False)
                        kb.mm(py, py[0:64, 0:128], ut, ut[:], Gb, Gb[:, ti, :], start=False, stop=False)
                        kb.mm(py, py[0:64, 0:128], Vt, Vt[:, ti, :], Gk, Gk[:, ti, :], start=False, stop=True)
                        kb.tt(YT, YT[:, c0:c0 + 128], [YT, py], YT[:, c0:c0 + 128], py[0:64, 0:128], ALU.add)
                        if i < len(order) - 1:
                            pS = kb.psum()
                            kb.mm(pS, pS[0:64, 0:64], Be, Be[:, ti, :], ut, ut[:], start=True, stop=False)
                            kb.mm(pS, pS[0:64, 0:64], Ke, Ke[:, ti, :], Vt, Vt[:, ti, :], start=False, stop=True)
                            kb.ts(Stmp, Stmp[:], [pS, WC], pS[0:64, 0:64], WC[:, ti:ti + 1])
                            kb.stt(S0, S0[:], [S0, WC, Stmp], S0[:], WC[:, ti:ti + 1], Stmp[:], ALU.mult, ALU.add)
                            kb.copy(S0b, S0b[:], [S0], S0[:], eng=fw.act)
                    if h == 0 and d == 0:
                        kb.tap("w_YTf", YT, YT[:], [64, NC_])
                        if stop == "rwkv_d":
                            fw.barrier(); return
                pp = (h % 2) * 64
                for gci, (c0, n) in enumerate(groups):
                    oc = (c0 - LOFF) if c0 < COFF else (L + c0 - COFF)
                    ps = kb.psum(); kb.mm(ps, ps[0:64, 0:n], ones64, ones64[:], YT, YT[:, c0:c0 + n])
                    kb.tt(g1, g1[:, 0:n], [YT, ps], YT[:, c0:c0 + n], ps[0:64, 0:n], ALU.subtract)
                    kb.act(g2, g2[:, 0:n], [g1], g1[:, 0:n], AF.Square)
                    ps = kb.psum(); kb.mm(ps, ps[0:64, 0:n], ones64, ones64[:], g2, g2[:, 0:n])
                    kb.act(g2, g2[:, 0:n], [ps], ps[0:64, 0:n], AF.Sqrt, bias=64e-5)
                    kb.recip(g2, g2[:, 0:n], [g2], g2[:, 0:n])
                    kb.tt(g1, g1[:, 0:n], [g1, g2], g1[:, 0:n], g2[:, 0:n], ALU.mult)
                    kb.ts(g1, g1[:, 0:n], [g1, P5], g1[:, 0:n], pc(3), pc(4), ALU.mult, ALU.add)
                    kb.copy(g3, g3[:, 0:n], [rks], rks[:, c0:c0 + n], eng=fw.pool)
                    ps = kb.psum(); kb.mm(ps, ps[0:64, 0:n], ones64, ones64[:], g3, g3[:, 0:n])
                    kb.stt(g2, g2[:, 0:n], [ps, vS], ps[0:64, 0:n], 64.0, vS[:, c0:c0 + n], ALU.mult, ALU.mult)
                    kb.tt(g1, g1[:, 0:n], [g1, g2], g1[:, 0:n], g2[:, 0:n], ALU.add, eng=fw.pool)
                    ps = kb.psum(); proj_fm(ps, 64, w["z"], slice(0, 64), c0, n)
                    kb.act(g2, g2[:, 0:n], [ps], ps[0:64, 0:n], AF.Silu)
                    stg = ostg[gci % 2]
                    kb.tt(stg, stg[:, 0:n], [g1, g2], g1[:, 0:n], g2[:, 0:n], ALU.mult, eng=fw.pool)
                    kb.load(oT, oT.ap[pp:pp + 64, 4 + h // 2, oc:oc + n], stg, stg[:, 0:n])
            fw.barrier()

    if not SKIP_L0:
        retention()
    kb.tap("oT0", oT, oT.ap, [128, 8, L + CL], BF16)
    if stop == "ret":
        return finish(kb, es, y_d)
    if os.environ.get("RUN_RWKV", "1") == "1" and not SKIP_L0:
        rwkv()
    kb.tap("oT0b", oT, oT.ap, [128, 8, L + CL], BF16)
    if stop == "rwkv":
        return finish(kb, es, y_d)

    def out_proj(li, wout_d, gateb, src_tile, dst_tile, ntiles, final=False):
        with ExitStack() as les:
            sb = lambda shape, dt, name: fw.sbuf(shape, dt, f"o{li}_" + name, es=les)
            wout = sb([128, 8, D], BF16, "wout"); kb.loadc(wout, wout[:], wout_d, rr(wout_d.ap))
            oTt = [sb([128, 8, 128], BF16, f"oTt{i}") for i in range(2)]
            hb = [sb([128, D], F32, f"h{i}") for i in range(2)]
            hn = [sb([128, D], F32, f"hn{i}") for i in range(2)]
            tmp = sb([128, 512], F32, "tmp")
            if final:
                fnwb = sb([128, D], F32, "fnwb"); kb.load(fnwb, fnwb[:], fnw_d, fnw_d.ap[0].partition_broadcast(128))
                junk = sb([128, D], F32, "junk"); st = [sb([128, 4], F32, f"st{i}") for i in range(2)]
            for ti in range(ntiles):
                ot = oTt[ti % 2]; h = hb[ti % 2]; o = hn[ti % 2]
                nkc = 8 if os.environ.get("RUN_RWKV", "1") == "1" else 4
                kb.load(ot, ot[:, 0:nkc, :], oT, oT.ap[:, 0:nkc, ti * 128:(ti + 1) * 128])
                sbuf_, sap = src_tile(ti)
                kb.load(h, h[:], sbuf_, sap)
                r = 0 if ti < NTL else 1
                for cg in range(2):
                    ps = kb.psum()
                    for kc in range(nkc):
                        kb.mm(ps, ps[:], ot, ot[:, kc, :], wout, wout[:, kc, cg * 512:(cg + 1) * 512], start=(kc == 0), stop=(kc == nkc - 1))
                    kb.tt(tmp, tmp[:], [ps, gateb], ps[:], gateb[:, r, cg * 512:(cg + 1) * 512], ALU.mult)
                    kb.tt(o, o[:, cg * 512:(cg + 1) * 512], [h, tmp], h[:, cg * 512:(cg + 1) * 512], tmp[:], ALU.add, eng=fw.pool)
                if final:
                    s = st[ti % 2]
                    kb.act(junk, junk[:], [o], o[:], AF.Square, accum=s[:, 0:1], extra_w=[s])
                    kb.act(s, s[:, 1:2], [s], s[:, 0:1], AF.Sqrt, scale=1.0 / D, bias=EPS)
                    kb.recip(s, s[:, 2:3], [s], s[:, 1:2])
                    kb.stt(o, o[:], [o, s, fnwb], o[:], s[:, 2:3], fnwb[:], ALU.mult, ALU.mult)
                db, dap = dst_tile(ti)
                kb.load(db, dap, o, o[:])
            fw.barrier()

    def h1_tile(ti):
        if SKIP_L0:
            return h1in_d, h1in_d.ap[ti * 128:(ti + 1) * 128, :]
        return h1_d, h1_d.ap[ti * 128:(ti + 1) * 128, :]

    def y_tile(ti):
        return y_d, y_d.ap[ti * 128:(ti + 1) * 128, :]


    def na_attention():
        with ExitStack() as les:
            sb = lambda shape, dt, name: fw.sbuf(shape, dt, "a_" + name, es=les)
            R_d = kb.scratch("na_R", [120, 64 * 96])
            with ExitStack() as zes:
                z = fw.sbuf([120, 64 * 96], F32, "a_zero", es=zes); kb.memset(z, z[:], 0.0)
                kb.load(R_d, R_d.ap, z, z[:])
                fw.barrier()
            Rt = R_d.ap.tensor
            for h in range(8):
                dst = bass.AP(Rt, h * 15 * 6144, [[6144, 15], [97, 64], [1, 31]])
                srcap = bass.AP(rpb_d.ap.tensor, h * 15 * 31, [[31, 15], [0, 64], [1, 31]])
                kb.load(R_d, dst, rpb_d, srcap)
            BiasT = sb([64, 120, 64], F32, "BiasT")
            for h in range(8):
                srcap = bass.AP(Rt, h * 15 * 6144 + 15, [[96, 64], [6144, 15], [1, 64]])
                kb.load(BiasT, BiasT[:, 15 * h:15 * h + 15, :], R_d, srcap)
            wm = sb([64, 64], F32, "wm"); kb.load(wm, wm[:], wmask_d, wmask_d.ap)
            kb.tt(BiasT, BiasT[:], [BiasT, wm], BiasT[:], wm[:].unsqueeze(1).broadcast_to([64, 120, 64]), ALU.add, eng=fw.pool)
            kb.tap("a_BiasT", BiasT, BiasT[:], [64, 120, 64])
            ones_bf = sb([128, 64], BF16, "ones_bf"); kb.memset(ones_bf, ones_bf[:], 1.0)
            wts = {n: sb([128, 8, 64], BF16, n) for n in ("wq", "wk", "wv", "wz")}
            qT = sb([64, L], BF16, "qT"); kT = sb([64, L + CL], BF16, "kT")
            Ve = sb([128, NTT, 64], BF16, "Ve"); Vo = sb([128, 15, 64], BF16, "Vo")
            szT = sb([64, L], F32, "szT")
            Sb = [sb([64, 768], F32, f"Sb{i}") for i in range(2)]
            Pb = [sb([128, 768], BF16, f"Pb{i}") for i in range(2)]
            PT = [sb([128, 6, 128], BF16, f"PT{i}") for i in range(2)]
            mx = [sb([64, 2], F32, f"mx{i}") for i in range(2)]
            rd = sb([64, 512], F32, "rd"); on = sb([64, 512], F32, "on")
            ostg = [sb([64, 512], BF16, f"ostg{i}") for i in range(2)]
            for h in range(8):
                for n, c0 in (("wq", 2048 + 64 * h), ("wk", 2560 + 64 * h), ("wv", 3072 + 64 * h), ("wz", 3584 + 64 * h)):
                    kb.loadc(wts[n], wts[n][:], win1_d, rr(win1_d.ap[:, c0:c0 + 64]))
                for g in range(4):
                    ps = kb.psum(); proj_fm(ps, 64, wts["wq"], slice(0, 64), LOFF + 512 * g, 512)
                    kb.act(qT, qT[:, 512 * g:512 * (g + 1)], [ps], ps[0:64, :], AF.Identity, scale=0.125)
                    ps = kb.psum(); proj_fm(ps, 64, wts["wk"], slice(0, 64), LOFF + 512 * g, 512)
                    kb.copy(kT, kT[:, 512 * g:512 * (g + 1)], [ps], ps[0:64, :])
                    ps = kb.psum(); proj_fm(ps, 64, wts["wz"], slice(0, 64), LOFF + 512 * g, 512)
                    kb.act(szT, szT[:, 512 * g:512 * (g + 1)], [ps], ps[0:64, :], AF.Silu)
                ps = kb.psum(); proj_fm(ps, 64, wts["wk"], slice(0, 64), COFF, 256)
                kb.copy(kT, kT[:, L:L + CL], [ps], ps[0:64, 0:256])
                for ti in range(NTT):
                    ps = kb.psum(); proj_tm(ps, tokcol(ti), wts["wv"], slice(0, 64), 64)
                    kb.copy(Ve, Ve[:, ti, :], [ps], ps[:, 0:64], eng=fw.act)
                for c in range(15):
                    ps = kb.psum(); proj_tm(ps, LOFF + 64 + 128 * c, wts["wv"], slice(0, 64), 64)
                    kb.copy(Vo, Vo[:, c, :], [ps], ps[:, 0:64], eng=fw.act)
                for g8 in range(4):
                    pnum = kb.psum(hold=True); pden = kb.psum(hold=True)
                    srs_all = {}

                    def stage_a1(rp):
                        pb = Pb[rp % 2]
                        srs = []
                        for half in range(2):
                            r = 8 * g8 + 2 * rp + half
                            sr = min(max(r - 4, 0), 24)
                            srs.append(sr)
                            sbf = Sb[half]; m_ = mx[half]
                            psA = kb.psum(); psB = kb.psum()
                            kb.mm(psA, psA[0:64, :], qT, qT[:, 64 * r:64 * r + 64], kT, kT[:, 64 * sr:64 * sr + 512])
                            kb.mm(psB, psB[0:64, 0:256], qT, qT[:, 64 * r:64 * r + 64], kT, kT[:, L:L + CL])
                            d0 = sr - r + 7
                            kb.tt(sbf, sbf[:, 0:512], [psA, BiasT], psA[0:64, :], BiasT[:, 15 * h + d0:15 * h + d0 + 8, :].rearrange("p a b -> p (a b)"), ALU.add)
                            kb.copy(sbf, sbf[:, 512:768], [psB], psB[0:64, 0:256], eng=fw.act)
                            fw.op(fw.dve, lambda sbf=sbf, m_=m_: nc.vector.reduce_max(out=m_[:, 0:1], in_=sbf[:], axis=AX.X), reads=[sbf], writes=[m_])
                            kb.ts(m_, m_[:, 1:2], [m_], m_[:, 0:1], -1.0)
                            kb.act(pb, pb[64 * half:64 * half + 64, :], [sbf, m_], sbf[:], AF.Exp, bias=m_[:, 1:2])
                        srs_all[rp] = srs

                    def stage_a2(rp):
                        pb = Pb[rp % 2]; pt = PT[rp % 2]
                        pst = kb.psum(); pstb = pst.ap[:].bitcast(BF16)
                        for j in range(6):
                            kb.tr(pst, pstb[:, 128 * j:128 * (j + 1)], pb, pb[:, 128 * j:128 * (j + 1)], ident_bf, ident_bf[:])
                        kb.copy(pt, pt[:].rearrange("p a b -> p (a b)"), [pst], pstb[:, 0:768])

                    def stage_b(rp):
                        pt = PT[rp % 2]
                        for half in range(2):
                            sr = srs_all[rp][half]
                            cs_ = slice(64 * (2 * rp + half), 64 * (2 * rp + half) + 64)
                            hs_ = slice(64 * half, 64 * half + 64)
                            for j in range(6):
                                if j < 4:
                                    vb, vi = (Ve, sr // 2 + j) if sr % 2 == 0 else (Vo, (sr - 1) // 2 + j)
                                else:
                                    vb, vi = Ve, NTL + (j - 4)
                                kb.mm(pnum, pnum[0:64, cs_], vb, vb[:, vi, :], pt, pt[:, j, hs_], start=(j == 0), stop=(j == 5))
                            for j in range(6):
                                kb.mm(pden, pden[0:64, cs_], ones_bf, ones_bf[:], pt, pt[:, j, hs_], start=(j == 0), stop=(j == 5))

                    stage_a1(0); stage_a2(0)
                    for rp in range(4):
                        if rp < 3:
                            stage_a1(rp + 1)
                        stage_b(rp)
                        if rp < 3:
                            stage_a2(rp + 1)
                    kb.release(pnum); kb.release(pden)
                    kb.recip(rd, rd[:], [pden], pden[0:64, :])
                    kb.tt(on, on[:], [pnum, rd], pnum[0:64, :], rd[:], ALU.mult)
                    stg = ostg[g8 % 2]
                    kb.tt(stg, stg[:], [on, szT], on[:], szT[:, 512 * g8:512 * (g8 + 1)], ALU.mult, eng=fw.pool)
                    po = (h % 2) * 64
                    kb.load(oT, oT.ap[po:po + 64, 4 + h // 2, 512 * g8:512 * (g8 + 1)], stg, stg[:])
            fw.barrier()

    only0 = os.environ.get('FULL_L0_TEST') != '1'
    only0 = os.environ.get("ONLY_L0") == "1"
    if not SKIP_L0:
        out_proj(0, wout0_d, gateb0, src0, y_tile if only0 else h1_tile, NTL if only0 else NTT, final=only0)
    if only0:
        return finish(kb, es, y_d)
    if not SKIP_L0:
        kb.tap("h1", h1_d, h1_d.ap, [L + CL, D])
    if stop == "l0":
        return finish(kb, es, y_d)
    modcol1, gateb1 = layer_mod(1)
    layer_norm(1, h1_tile, modcol1)
    kb.tap("nT1", nT, nT[:], [128, 8, NTC], BF16)
    if os.environ.get("SKIP_NA") != "1":
        na_attention()
    kb.tap("oT1", oT, oT.ap, [128, 8, L + CL], BF16)
    if stop == "na":
        return finish(kb, es, y_d)
    hyena()
    kb.tap("oT1b", oT, oT.ap, [128, 8, L + CL], BF16)
    if stop == "hy":
        return finish(kb, es, y_d)
    out_proj(1, wout1_d, gateb1, h1_tile, y_tile, NTL, final=True)
    return finish(kb, es, y_d)


def finish(kb, es, y_d):
    outs = [y_d] + list(kb.taps.values())
    kb.fw.finish([o for o in outs if o.w])
    print("ninstr", kb.fw.ninstr, "nwaits", kb.fw.nwaits, {e.name: e.count for e in (kb.fw.pe, kb.fw.act, kb.fw.dve, kb.fw.pool)}, "dma", max(kb.fw.sp.dcnt), max(kb.fw.pool.dcnt), "waits", {e.name: getattr(e, "nw", 0) for e in (kb.fw.pe, kb.fw.act, kb.fw.dve, kb.fw.pool, kb.fw.sp)})
    es.close()
    return kb.nc, kb


def host_consts():
    c = {}
    c["ident"] = np.eye(128, dtype=np.float32)
    t = np.arange(L)
    row = (t // 64).astype(np.float32); col = (t % 64).astype(np.float32)
    inv = (10000.0 ** (-np.arange(16, dtype=np.float32) / 16)).astype(np.float32)
    ang = np.concatenate([row[:, None] * inv, col[:, None] * inv], -1).astype(np.float32)
    cos = np.cos(ang).astype(np.float32); sin = np.sin(ang).astype(np.float32)
    c["cs_fm"] = np.ascontiguousarray(np.concatenate([cos, cos], -1).T)
    c["sn_fm"] = np.ascontiguousarray(np.concatenate([-sin, sin], -1).T)
    c["cs_tm"] = np.ascontiguousarray(np.concatenate([cos, cos], -1))
    c["sn_tm"] = np.ascontiguousarray(np.concatenate([-sin, sin], -1))
    p = np.arange(128, dtype=np.float32)
    c["colAB"] = np.stack([127 - p, p], -1).astype(np.float32)
    tt = np.arange(128, dtype=np.float32)
    c["iota12"] = np.broadcast_to(np.concatenate([tt + 1, 128 - tt])[None], (64, 256)).astype(np.float32).copy()
    s = p[:, None]; t2 = tt[None, :]
    c["dmask"] = np.concatenate([np.maximum(t2 - s, 0), np.maximum(s - t2, 0), (t2 >= s) * 1.0, (s >= t2) * 1.0], -1).astype(np.float32)
    ab = np.arange(2176, dtype=np.int64)
    prod = (ab[:, None] * ab[None, :]) % 4096
    valid = ((ab[:, None] <= 2048) & (ab[None, :] <= 2048))
    ang_ = prod.astype(np.float64) * (2.0 * np.pi / 4096.0)
    gcs = np.stack([np.where(valid, np.cos(ang_), 0.0), np.where(valid, np.sin(ang_), 0.0)], 0).astype(np.float32)
    gt = gcs.reshape(2, 17, 128, 17, 128).transpose(3, 2, 0, 1, 4)
    c["Gt"] = np.ascontiguousarray(gt.reshape(17, 128, 2 * 17 * 128)).astype(ml_dtypes.bfloat16)
    tl = np.linspace(0.0, 1.0, L, dtype=np.float32)[:, None]
    bands = np.linspace(1e-4, 15, 16, dtype=np.float32)
    angz = (np.float32(2.0 * math.pi / L) * np.arange(L, dtype=np.float32)[:, None] * bands[None]).astype(np.float32)
    c["hy_zT"] = np.ascontiguousarray(np.concatenate([tl, np.cos(angz), -np.sin(angz)], -1).T.astype(np.float32))
    c["hy_trow"] = np.ascontiguousarray(tl.T)
    deltas = np.abs(np.linspace(math.log(1e-2) / 1.5, math.log(1e-2) / 0.3, 512, dtype=np.float32))
    c["hy_ndelta"] = (-deltas[None, :]).astype(np.float32)
    fidx = np.arange(17 * 128).reshape(17, 128).T
    c["hy_wt"] = np.where((fidx == 0) | (fidx == 2048), 1.0 / 4096, np.where(fidx < 2048, 2.0 / 4096, 0.0)).astype(np.float32)
    qc = np.arange(64)[:, None]; kc = np.arange(64)[None, :]
    st = np.clip(qc - 8, 0, 48)
    c["wmask"] = np.where((kc >= st) & (kc < st + 16), 0.0, NEG).astype(np.float32)
    sm = np.ones((64, NTC), np.float32)
    for ti in range(NTT):
        sm[:, tokcol(ti)] = 0.0
    c["scanmask"] = sm
    c["masks4"] = np.concatenate([(s < t2) * 1.0, (s <= t2) * 1.0, (s > t2) * 1.0, (s >= t2) * 1.0], -1).astype(np.float32)
    return c


def rot_perm():
    idx = []
    for blk in range(16):
        b = blk * 64
        idx += list(range(b + 32, b + 64)) + list(range(b, b + 32))
    return np.array(idx)


def prep_inputs(inputs, b, consts):
    f = lambda a: np.ascontiguousarray(np.asarray(a, dtype=np.float32))
    m = dict(consts)
    m["x"] = f(inputs["x"][b]); m["ctx"] = f(inputs["ctx"][b])
    cc = np.stack([np.asarray(inputs["c"][b]), np.asarray(inputs["c_ctx"])], -1)
    m["ccT"] = f(cc.reshape(8, 128, 2).transpose(1, 0, 2))
    m["ada_w"] = f(inputs["ada_w"]); m["ada_b"] = f(inputs["ada_b"]); m["norm_w"] = f(inputs["norm_w"])
    m["fnw"] = f(np.asarray(inputs["final_norm_w"])[None])
    w0 = np.asarray(inputs["even_w_in"][0])
    m["w_in0"] = f(w0); m["w_rot0"] = f(w0[:, rot_perm()]); m["w_out0"] = f(inputs["even_w_out"][0])
    mu = np.asarray(inputs["rw_mu"][0])
    m["muT"] = f(mu.reshape(28, 64).T); m["mu128"] = f(np.stack([mu[1536:1664], mu[1664:1792]], -1))
    vecs = [inputs["rw_kk"][0], inputs["rw_ka"][0], np.asarray(inputs["rw_rk"][0]).reshape(-1), inputs["rw_ln_w"][0], inputs["rw_ln_b"][0],
            inputs["rw_w0"][0][0], inputs["rw_w0"][0][1], inputs["rw_a0"][0][0], inputs["rw_a0"][0][1]]
    m["P512"] = f(np.stack([np.asarray(v).reshape(8, 64).T for v in vecs], 1))
    m["LW"] = f(np.stack([np.asarray(inputs["rw_w2"][0]).reshape(128, 512), np.asarray(inputs["rw_a2"][0]).reshape(128, 512)], 1))
    m["w_in1"] = f(inputs["odd_w_in"][0]); m["w_out1"] = f(inputs["odd_w_out"][0]); m["na_rpb"] = f(inputs["na_rpb"][0])
    m["hy_w1"] = f(inputs["hy_w1"][0]); m["hy_w2"] = f(inputs["hy_w2"][0]); m["hy_w3"] = f(inputs["hy_w3"][0])
    m["hy_vec"] = f(np.stack([inputs["hy_b1"][0], inputs["hy_f1"][0], inputs["hy_b2"][0], inputs["hy_f2"][0]], -1))
    m["hy_conv"] = f(np.concatenate([inputs["hy_conv_w"][0], inputs["hy_conv_b"][0][None]], 0)); m["hy_bias"] = f(inputs["hy_bias"][0])
    m["ret_decay"] = f(np.asarray(inputs["ret_decay"][0]).reshape(16))
    return m


_CACHE = {}


def kernel(**inputs):
    if "nc" not in _CACHE:
        _CACHE["nc"] = build()
    nc, kb = _CACHE["nc"]
    consts = host_consts()
    in_maps = []
    for b in range(8):
        m = prep_inputs(inputs, b, consts)
        in_maps.append({k: m[k] for k in kb.din})
    res = run_bass_kernel_spmd(nc, in_maps, core_ids=list(range(8)))
    return np.stack([np.asarray(r["y"]) for r in res.results], 0).astype(np.float32)
```
